# Optimizing a Trainium2 kernel written in Bass

```python
import math
import jax, jax.numpy as jnp
from jax import lax
import numpy as np

D_MODEL = 2048
BATCH = 2
SEQ = 8192
DEPTH = 2

CHUNK = 64
Q_BLOCK = 128
HA = 8
A_NOPE = 128
A_ROPE = 64
A_V = 128
Q_RANK = 384
KV_RANK = 128
ROPE_THETA = 10000.0
HB = 4
B_DH = 64
HC = 4
C_DH = 128
LEFT_CHUNKS = 8
REL_CLIP = 256
REL_SIZE = REL_CLIP + CHUNK
MIX_WIDTH = HA * A_V + HB * 2 * B_DH + HC * C_DH
IN_SIZES = (Q_RANK, KV_RANK, A_ROPE,
            HB * 2 * B_DH, HB * 2 * B_DH, HB * 2 * B_DH,
            HC * C_DH, HC * C_DH, HC * C_DH)
IN_WIDTH = sum(IN_SIZES)
D_FF = 4 * D_MODEL
EPS = 1e-6
NEG = -1e30

kernel_name = "hybrid_mla_diff_chunkband_stream_encoder"


def rms_norm(x, g):
    xf = x.astype(jnp.float32)
    y = xf * lax.rsqrt(jnp.mean(xf * xf, axis=-1, keepdims=True) + EPS)
    return (y * g.astype(jnp.float32)).astype(x.dtype)


def apply_rope(x, pos):
    half = x.shape[-1] // 2
    inv_freq = ROPE_THETA ** (-jnp.arange(half, dtype=jnp.float32) / half)
    ang = pos.astype(jnp.float32)[:, None] * inv_freq[None, :]
    cos = jnp.cos(ang)[:, None, :]
    sin = jnp.sin(ang)[:, None, :]
    xf = x.astype(jnp.float32)
    x1, x2 = xf[..., :half], xf[..., half:]
    return jnp.concatenate([x1 * cos - x2 * sin, x2 * cos + x1 * sin], axis=-1).astype(x.dtype)


def chunk_causal_mask(q_pos, seq):
    k_chunk = jnp.arange(seq) // CHUNK
    return k_chunk[None, :] <= (q_pos // CHUNK)[:, None]


def sweep_query_blocks(block_fn, q):
    b, h, s = q.shape[:3]
    nblk = s // Q_BLOCK
    qb = jnp.moveaxis(q.reshape((b, h, nblk, Q_BLOCK) + q.shape[3:]), 2, 0)
    out = lax.map(lambda a: block_fn(a[0], a[1]), (qb, jnp.arange(nblk)))
    return out.transpose(1, 0, 3, 2, 4).reshape(b, s, h * out.shape[-1])


def mla_mixer(c_q, c_kv, k_rope, q_norm, kv_norm, w_uq, w_ukv):
    b, s, _ = c_q.shape
    pos = jnp.arange(s)
    q = (rms_norm(c_q, q_norm) @ w_uq).reshape(b, s, HA, A_NOPE + A_ROPE)
    q = jnp.concatenate([q[..., :A_NOPE], apply_rope(q[..., A_NOPE:], pos)], axis=-1)
    kv = (rms_norm(c_kv, kv_norm) @ w_ukv).reshape(b, s, HA, A_NOPE + A_V)
    kr = apply_rope(k_rope[:, :, None, :], pos)
    k = jnp.concatenate([kv[..., :A_NOPE], jnp.broadcast_to(kr, (b, s, HA, A_ROPE))], axis=-1)
    q = q.transpose(0, 2, 1, 3)
    k = k.transpose(0, 2, 1, 3)
    v = kv[..., A_NOPE:].transpose(0, 2, 1, 3)
    scale = (A_NOPE + A_ROPE) ** -0.5

    def block(qb, blk):
        q_pos = blk * Q_BLOCK + jnp.arange(Q_BLOCK)
        sc = jnp.einsum('bhqd,bhkd->bhqk', qb, k).astype(jnp.float32) * scale
        sc = jnp.where(chunk_causal_mask(q_pos, s), sc, NEG)
        p = jax.nn.softmax(sc, axis=-1).astype(v.dtype)
        return jnp.einsum('bhqk,bhkd->bhqd', p, v)

    return sweep_query_blocks(block, q)


def diff_mixer(q, k, v, lq1, lk1, lq2, lk2, subln, lam_init):
    b, s, _ = q.shape
    q = q.reshape(b, s, HB, 2, B_DH).transpose(0, 2, 1, 3, 4)
    k = k.reshape(b, s, HB, 2, B_DH).transpose(0, 2, 1, 3, 4)
    v = v.reshape(b, s, HB, 2 * B_DH).transpose(0, 2, 1, 3)
    f32 = jnp.float32
    lam = (jnp.exp(jnp.sum(lq1.astype(f32) * lk1.astype(f32)))
           - jnp.exp(jnp.sum(lq2.astype(f32) * lk2.astype(f32))) + lam_init)
    slopes = 2.0 ** (-8.0 * (jnp.arange(HB, dtype=f32) + 1.0) / HB)
    k_pos = jnp.arange(s)
    scale = B_DH ** -0.5

    def block(qb, blk):
        q_pos = blk * Q_BLOCK + jnp.arange(Q_BLOCK)
        sc = jnp.einsum('bhqnd,bhknd->nbhqk', qb, k).astype(f32) * scale
        dist = jnp.abs(q_pos[:, None] - k_pos[None, :]).astype(f32)
        sc = sc - slopes[:, None, None] * dist
        sc = jnp.where(chunk_causal_mask(q_pos, s), sc, NEG)
        p = jax.nn.softmax(sc, axis=-1)
        w = (p[0] - lam * p[1]).astype(v.dtype)
        o = jnp.einsum('bhqk,bhkd->bhqd', w, v)
        return rms_norm(o, subln) * (1.0 - lam_init)

    return sweep_query_blocks(block, q)


def chunk_band_mixer(q, k, v, rel_bias):
    b, s, _ = q.shape
    nc = s // CHUNK
    win = LEFT_CHUNKS + 1
    idx = jnp.arange(nc)[:, None] + jnp.arange(win)[None, :]

    def band(t):
        t = t.reshape(b, nc, CHUNK, HC, C_DH)
        t = jnp.pad(t, ((0, 0), (LEFT_CHUNKS, 0), (0, 0), (0, 0), (0, 0)))
        return t[:, idx].reshape(b, nc, win * CHUNK, HC, C_DH)

    qc = q.reshape(b, nc, CHUNK, HC, C_DH)
    kb, vb = band(k), band(v)
    valid = (jnp.arange(nc)[:, None] - LEFT_CHUNKS + jnp.arange(win)[None, :]) >= 0
    valid = jnp.repeat(valid, CHUNK, axis=1)
    rel = LEFT_CHUNKS * CHUNK + jnp.arange(CHUNK)[:, None] - jnp.arange(win * CHUNK)[None, :]
    rel_idx = jnp.clip(rel, -(CHUNK - 1), REL_CLIP) + (CHUNK - 1)
    bias = rel_bias[:, rel_idx].astype(jnp.float32)
    sc = jnp.einsum('bcqhd,bckhd->bchqk', qc, kb).astype(jnp.float32) * (C_DH ** -0.5)
    sc = sc + bias[None, None]
    sc = jnp.where(valid[None, :, None, None, :], sc, NEG)
    p = jax.nn.softmax(sc, axis=-1).astype(vb.dtype)
    o = jnp.einsum('bchqk,bckhd->bcqhd', p, vb)
    return o.reshape(b, s, HC * C_DH)


def setup_inputs(seed: int = 0) -> dict:
    key = jax.random.key(seed)
    ks = jax.random.split(key, 20)
    f32 = jnp.float32
    nrm = lambda k, shape, s: jax.random.normal(k, shape, f32) * s
    gain = lambda k, shape: 1.0 + 0.02 * jax.random.normal(k, shape, f32)
    return {
        "x": nrm(ks[0], (BATCH, SEQ, D_MODEL), 1.0),
        "attn_norm": gain(ks[1], (DEPTH, D_MODEL)),
        "w_in": nrm(ks[2], (DEPTH, D_MODEL, IN_WIDTH), D_MODEL ** -0.5),
        "q_a_norm": gain(ks[3], (DEPTH, Q_RANK)),
        "kv_a_norm": gain(ks[4], (DEPTH, KV_RANK)),
        "w_uq": nrm(ks[5], (DEPTH, Q_RANK, HA * (A_NOPE + A_ROPE)), Q_RANK ** -0.5),
        "w_ukv": nrm(ks[6], (DEPTH, KV_RANK, HA * (A_NOPE + A_V)), KV_RANK ** -0.5),
        "lambda_q1": nrm(ks[7], (DEPTH, B_DH), 0.1),
        "lambda_k1": nrm(ks[8], (DEPTH, B_DH), 0.1),
        "lambda_q2": nrm(ks[9], (DEPTH, B_DH), 0.1),
        "lambda_k2": nrm(ks[10], (DEPTH, B_DH), 0.1),
        "diff_subln": gain(ks[11], (DEPTH, 2 * B_DH)),
        "rel_bias": nrm(ks[12], (DEPTH, HC, REL_SIZE), 0.1),
        "w_o": nrm(ks[13], (DEPTH, MIX_WIDTH, D_MODEL), MIX_WIDTH ** -0.5),
        "mlp_norm": gain(ks[14], (DEPTH, D_MODEL)),
        "w_up": nrm(ks[15], (DEPTH, D_MODEL, D_FF), D_MODEL ** -0.5),
        "w_down": nrm(ks[16], (DEPTH, D_FF, D_MODEL), D_FF ** -0.5),
        "final_norm": gain(ks[17], (D_MODEL,)),
    }


def reference(x, attn_norm, w_in, q_a_norm, kv_a_norm, w_uq, w_ukv,
              lambda_q1, lambda_k1, lambda_q2, lambda_k2, diff_subln, rel_bias,
              w_o, mlp_norm, w_up, w_down, final_norm):
    split_points = np.cumsum(np.array(IN_SIZES))[:-1]
    for l in range(DEPTH):
        h = rms_norm(x, attn_norm[l])
        proj = h @ w_in[l]
        (c_q, c_kv, k_rope, qb, kb, vb, qc, kc, vc) = jnp.split(proj, split_points, axis=-1)
        out_a = mla_mixer(c_q, c_kv, k_rope, q_a_norm[l], kv_a_norm[l], w_uq[l], w_ukv[l])
        lam_init = 0.8 - 0.6 * math.exp(-0.3 * l)
        out_b = diff_mixer(qb, kb, vb, lambda_q1[l], lambda_k1[l], lambda_q2[l],
                           lambda_k2[l], diff_subln[l], lam_init)
        out_c = chunk_band_mixer(qc, kc, vc, rel_bias[l])
        mixed = jnp.concatenate([out_a, out_b, out_c], axis=-1)
        x = x + mixed @ w_o[l]
        h = rms_norm(x, mlp_norm[l])
        x = x + jnp.square(jax.nn.relu(h @ w_up[l])) @ w_down[l]
    return rms_norm(x, final_norm)
```

```python
import numpy as np
from contextlib import ExitStack
import ml_dtypes
import concourse.bass as bass
import concourse.mybir as mybir
from concourse.bass_utils import run_bass_kernel_spmd

F32 = mybir.dt.float32
BF16 = mybir.dt.bfloat16
AF = mybir.ActivationFunctionType
ALU = mybir.AluOpType
AX = mybir.AxisListType

RING = 8
FOLD_WAITS = True
_UIDC = [0]


def _uid(nc):
    _UIDC[0] += 1
    return "t%d_" % _UIDC[0]

ENGS = ("pe", "act", "dve", "pool", "sp")


class Ins:
    __slots__ = ("eng", "idx", "fn", "deps", "dma", "dma_n", "needs_inc", "semval", "cc", "lhs_deps")


class Sched:
    def __init__(self, nc):
        self.nc = nc
        self.progs = {e: [] for e in ENGS}
        self.res = {}
        self.ndma = {e: 0 for e in ENGS}
        self.emitted = {e: 0 for e in ENGS}
        self.waited = {e: {} for e in ENGS}
        self.cnt = {e: 0 for e in ENGS}
        self.sems = {}
        self.rings = {}
        self.cc_sems = []

    def alloc_sems(self):
        nc = self.nc
        for e in ENGS:
            self.sems[e] = nc.alloc_semaphore("c_" + e)
        for e in ("act", "pool", "sp"):
            self.rings[e] = [nc.alloc_semaphore("r_%s_%d" % (e, i)) for i in range(RING)]

    def add(self, eng, fn, reads=(), writes=(), dma=False, cc=False, rhs=()):
        if cc:
            dma = True
        progs = self.progs[eng]
        idx = len(progs)
        deps = {}
        lhs_deps = set()
        for k in reads:
            st = self.res.get(k)
            if st is not None and st[0] is not None:
                deps[st[0]] = True
                if k not in rhs:
                    lhs_deps.add(st[0])
        for k in writes:
            st = self.res.get(k)
            if st is not None:
                if st[0] is not None:
                    deps.setdefault(st[0], False)
                for re_, ri in st[1].items():
                    deps.setdefault((re_, ri), False)
                for r in st[2]:
                    deps.setdefault(r, False)
        ins = Ins()
        ins.eng = eng
        ins.idx = idx
        ins.fn = fn
        ins.dma = dma
        ins.needs_inc = False
        ins.semval = None
        ins.dma_n = -1
        ins.cc = cc
        if cc:
            ins.semval = (self.nc.alloc_semaphore("cc_%d" % len(self.cc_sems)), 1)
            self.cc_sems.append(ins.semval[0])
        elif dma:
            ins.dma_n = self.ndma[eng]
            self.ndma[eng] += 1
        fd = []
        for (pe_, pi), raw in deps.items():
            p = self.progs[pe_][pi]
            if pe_ == eng and not p.dma and not dma:
                if eng == "pe" or not raw:
                    continue
            fd.append((pe_, pi))
            if not p.dma:
                p.needs_inc = True
        ins.deps = fd
        ins.lhs_deps = lhs_deps
        progs.append(ins)
        for k in reads:
            st = self.res.get(k)
            if st is None:
                st = [None, {}, []]
                self.res[k] = st
            if dma:
                st[2].append((eng, idx))
            else:
                st[1][eng] = idx
        for k in writes:
            self.res[k] = [(eng, idx), {}, []]
        return ins

    def _semval(self, ins):
        return ins.semval

    def emit_engine(self, eng, e):
        progs = self.progs[eng]
        waited = self.waited[eng]
        start = self.emitted[eng]
        c = self.cnt[eng]
        for ins in progs[start:]:
            if ins.cc:
                pass
            elif ins.dma:
                ins.semval = (self.rings[eng][ins.dma_n % RING], 16 * (ins.dma_n // RING + 1))
            else:
                if ins.needs_inc:
                    c += 1
                ins.semval = (self.sems[eng], c)
        self.cnt[eng] = c

        for ins in progs[start:]:
            pend = {}
            lhsv = {}
            if ins.dma and not ins.cc and ins.dma_n >= RING:
                sem = self.rings[eng][ins.dma_n % RING]
                pend[id(sem)] = (sem, 16 * (ins.dma_n // RING))
            for (pe_, pi) in ins.deps:
                p = self.progs[pe_][pi]
                assert p.semval is not None, (eng, ins.idx, pe_, pi)
                k = id(p.semval[0])
                if k not in pend or pend[k][1] < p.semval[1]:
                    pend[k] = p.semval
                if eng == "pe" and (pe_, pi) in ins.lhs_deps:
                    lhsv[k] = max(lhsv.get(k, 0), p.semval[1])
            todo = []
            foldable = []
            for k, (sem, val) in pend.items():
                w0 = waited.get(k, 0)
                if w0 < val:
                    if eng == "pe" and lhsv.get(k, 0) > w0:
                        todo.append((sem, val))
                    else:
                        foldable.append((sem, val))
                    waited[k] = val
            fold = None
            if foldable and FOLD_WAITS and not ins.dma:
                fold = foldable.pop()
            todo += foldable
            for sem, val in todo:
                e.wait_ge(sem, val)
            bi = ins.fn(e)
            if fold is not None:
                bi._wait_ge(fold[0], fold[1])
            if ins.cc:
                bi.then_inc(ins.semval[0], 1)
            elif ins.dma:
                bi.then_inc(ins.semval[0], 16)
            elif ins.needs_inc:
                bi.then_inc(ins.semval[0], 1)
        self.emitted[eng] = len(progs)

    def drain_dma(self, e, eng_name="sp", final=False):
        waited = self.waited[eng_name]
        for q, ring in self.rings.items():
            n = self.ndma[q]
            for slot in range(RING):
                cnt = (n - slot + RING - 1) // RING if n > slot else 0
                if cnt > 0:
                    k = id(ring[slot])
                    if waited.get(k, 0) < 16 * cnt:
                        e.wait_ge(ring[slot], 16 * cnt)
                        waited[k] = 16 * cnt
        if final:
            for sem in self.cc_sems:
                k = id(sem)
                if waited.get(k, 0) < 1:
                    e.wait_ge(sem, 1)
                    waited[k] = 1

    def run_block(self, final=False):
        nc = self.nc
        for eng in ENGS:
            progs = self.progs[eng]
            c = self.cnt[eng]
            for ins in progs[self.emitted[eng]:]:
                if ins.cc:
                    pass
                elif ins.dma:
                    ins.semval = (self.rings[eng][ins.dma_n % RING], 16 * (ins.dma_n // RING + 1))
                else:
                    if ins.needs_inc:
                        c += 1
                    ins.semval = (self.sems[eng], c)
        with nc.Block() as block:
            @block.tensor
            def _(e):
                self.emit_engine("pe", e)

            @block.scalar
            def _(e):
                self.emit_engine("act", e)

            @block.vector
            def _(e):
                self.emit_engine("dve", e)

            @block.gpsimd
            def _(e):
                self.emit_engine("pool", e)

            @block.sync
            def _(e):
                self.emit_engine("sp", e)
                self.drain_dma(e, "sp", final)
        keep = {}
        for k, st in self.res.items():
            w = st[0]
            if w is not None and self.progs[w[0]][w[1]].cc:
                keep[k] = [w, {}, []]
        self.res.clear()
        self.res.update(keep)
        nc.all_engine_barrier()


D = 2048
TOK = 2048
NTT = TOK // 128
KC = D // 128
DFF = 8192
EPS = 1e-6


def make_ident(S, ident_bf, ident_f):
    S.add("pool", lambda e: e.memset(ident_f[:], 0.0), writes=["ident_f"])
    S.add("pool", lambda e: e.affine_select(out=ident_f[:], in_=ident_f[:], pattern=[[-1, 128]],
                                            compare_op=ALU.not_equal, fill=1.0, base=0,
                                            channel_multiplier=1),
          reads=["ident_f"], writes=["ident_f"])
    S.add("dve", lambda e: e.tensor_copy(out=ident_bf[:], in_=ident_f[:]), reads=["ident_f"], writes=["ident"])


def rms_rstd(S, eng_sq, src_ap, junk_ap, ss_ap, rstd_ap, n, rkeys, tag):
    S.add("act", lambda e: e.activation(out=junk_ap, in_=src_ap, func=AF.Square, accum_out=ss_ap),
          reads=rkeys, writes=[tag + "_junk", tag + "_ss"])
    S.add("dve", lambda e: e.tensor_scalar(out=rstd_ap, in0=ss_ap, scalar1=1.0 / n, scalar2=EPS,
                                           op0=ALU.mult, op1=ALU.add),
          reads=[tag + "_ss"], writes=[tag + "_rstd"])
    S.add("act", lambda e: e.activation(out=rstd_ap, in_=rstd_ap, func=AF.Sqrt),
          reads=[tag + "_rstd"], writes=[tag + "_rstd"])
    S.add("dve", lambda e: e.reciprocal(out=rstd_ap, in_=rstd_ap),
          reads=[tag + "_rstd"], writes=[tag + "_rstd"])


def phase_O1(nc, S, x_in, mixT, w_o, g_mlp, x1_out, h2T_out):
    with ExitStack() as es:
        mixT_sb = es.enter_context(nc.sbuf_tensor(_uid(nc) + "o1_mixT", [128, KC, TOK], BF16))
        wo_sb = es.enter_context(nc.sbuf_tensor(_uid(nc) + "o1_wo", [128, KC, D], BF16))
        xt = es.enter_context(nc.sbuf_tensor(_uid(nc) + "o1_xt", [128, 3, D], F32))
        xn = es.enter_context(nc.sbuf_tensor(_uid(nc) + "o1_xn", [128, 2, D], BF16))
        junk = es.enter_context(nc.sbuf_tensor(_uid(nc) + "o1_junk", [128, D], BF16))
        hT = es.enter_context(nc.sbuf_tensor(_uid(nc) + "o1_hT", [128, 2, KC, 128], BF16))
        gT = es.enter_context(nc.sbuf_tensor(_uid(nc) + "o1_gT", [128, KC], F32))
        st = es.enter_context(nc.sbuf_tensor(_uid(nc) + "o1_st", [128, 8], F32))
        ident_f = es.enter_context(nc.sbuf_tensor(_uid(nc) + "o1_identf", [128, 128], F32))
        ident = es.enter_context(nc.sbuf_tensor(_uid(nc) + "o1_ident", [128, 128], BF16))
        pm = es.enter_context(nc.psum_tensor(_uid(nc) + "o1_pm", [128, 4, 512], F32))
        pt = es.enter_context(nc.psum_tensor(_uid(nc) + "o1_pt", [128, 2, 8, 128], BF16))
        make_ident(S, ident, ident_f)
        S.add("sp", lambda e: e.dma_start(out=gT[:], in_=g_mlp),
              writes=["gT"], dma=True)
        dyn = {}

        def ld_mix(e, kc):
            if "off" not in dyn:
                dyn["off"] = e.snap((nc.partition_id([mybir.EngineType.SP]) % 4) * TOK, min_val=0, max_val=3 * TOK)
            return e.dma_start(out=mixT_sb[:, kc, :], in_=mixT[kc // 2][(kc % 2) * 128:(kc % 2 + 1) * 128, bass.ds(dyn["off"], TOK)])

        for q in range(4):
            for k4 in range(4):
                S.add("sp", lambda e, kc=q * 4 + k4: ld_mix(e, kc), reads=[("mixg", id(mixT[(q * 4 + k4) // 2]))], writes=[("mixT", q * 4 + k4)], dma=True)
            S.add("pool", lambda e, q=q: e.dma_start(out=wo_sb[:, q * 4:(q + 1) * 4, :],
                                                     in_=w_o[q * 512:(q + 1) * 512, :].rearrange("(kc p) n -> p kc n", p=128)),
                  writes=[("wo", q)], dma=True)
        def part1(tt):
            b = tt % 3
            S.add("sp", lambda e, tt=tt, b=b: e.dma_start(out=xt[:, b, :], in_=x_in[tt * 128:(tt + 1) * 128, :]),
                  writes=[("xt", b)], dma=True)
            for cg in range(4):
                for kc in range(KC):
                    S.add("pe", lambda e, tt=tt, cg=cg, kc=kc: e.matmul(
                        pm[:, cg, :], lhsT=mixT_sb[:, kc, tt * 128:(tt + 1) * 128],
                        rhs=wo_sb[:, kc, cg * 512:(cg + 1) * 512], start=(kc == 0), stop=(kc == KC - 1)),
                        reads=[("mixT", kc), ("wo", kc // 4)], writes=[("pm", cg)], rhs=[("wo", kc // 4)])
                S.add("dve", lambda e, b=b, cg=cg: e.tensor_tensor(
                    out=xt[:, b, cg * 512:(cg + 1) * 512], in0=pm[:, cg, :], in1=xt[:, b, cg * 512:(cg + 1) * 512], op=ALU.add),
                    reads=[("pm", cg), ("xt", b)], writes=[("xt", b)])
            S.add("sp", lambda e, tt=tt, b=b: e.dma_start(out=x1_out[tt * 128:(tt + 1) * 128, :], in_=xt[:, b, :]),
                  reads=[("xt", b)], dma=True)

        def part2(tt):
            b = tt % 3
            h = tt % 2
            rms_rstd(S, "act", xt[:, b, :], junk[:], st[:, h:h + 1], st[:, 2 + h:3 + h], D, [("xt", b)], "o1n%d" % h)
            S.add("act", lambda e, b=b, h=h: e.activation(out=xn[:, h, :], in_=xt[:, b, :], func=AF.Copy, scale=st[:, 2 + h:3 + h]),
                  reads=[("xt", b), "o1n%d_rstd" % h], writes=[("xn", h)])
            for hg in range(2):
                for j in range(8):
                    kc = hg * 8 + j
                    S.add("pe", lambda e, h=h, hg=hg, j=j, kc=kc: e.transpose(
                        out=pt[:, hg, j, :], in_=xn[:, h, kc * 128:(kc + 1) * 128], identity=ident[:]),
                        reads=[("xn", h), "ident"], writes=[("pt", hg)])
                for j in range(8):
                    kc = hg * 8 + j
                    if j % 2 == 0:
                        S.add("act", lambda e, h=h, hg=hg, j=j, kc=kc: e.activation(
                            out=hT[:, h, kc, :], in_=pt[:, hg, j, :], func=AF.Copy, scale=gT[:, kc:kc + 1]),
                            reads=[("pt", hg), "gT"], writes=[("hT", h)])
                    else:
                        S.add("dve", lambda e, h=h, hg=hg, j=j, kc=kc: e.tensor_scalar(
                            out=hT[:, h, kc, :], in0=pt[:, hg, j, :], scalar1=gT[:, kc:kc + 1], scalar2=None, op0=ALU.mult),
                            reads=[("pt", hg), "gT"], writes=[("hT", h)])
            S.add("sp", lambda e, tt=tt, h=h: e.dma_start(
                out=h2T_out[:, tt * 128:(tt + 1) * 128].rearrange("(kc p) t -> p kc t", p=128), in_=hT[:, h, :, :]),
                reads=[("hT", h)], dma=True)

        part1(0)
        for tt in range(1, NTT):
            part1(tt)
            part2(tt - 1)
        part2(NTT - 1)
        S.run_block()


def phase_O2(nc, S, x1_in, h2T, w_up, w_down, x_out, g_final=None):
    HT = TOK // 2
    NFB = DFF // 512
    with ExitStack() as es:
        x1 = es.enter_context(nc.sbuf_tensor(_uid(nc) + "o2_x1", [128, HT // 128, D], F32))
        hT = es.enter_context(nc.sbuf_tensor(_uid(nc) + "o2_hT", [128, KC, HT], BF16))
        wu = es.enter_context(nc.sbuf_tensor(_uid(nc) + "o2_wu", [128, 2, KC, 512], BF16))
        wd = es.enter_context(nc.sbuf_tensor(_uid(nc) + "o2_wd", [128, 2, 4, D], BF16))
        aT = es.enter_context(nc.sbuf_tensor(_uid(nc) + "o2_aT", [128, 2, 4, HT], BF16))
        gB = es.enter_context(nc.sbuf_tensor(_uid(nc) + "o2_gB", [128, D], F32))
        junk = es.enter_context(nc.sbuf_tensor(_uid(nc) + "o2_junk", [128, D], BF16))
        st = es.enter_context(nc.sbuf_tensor(_uid(nc) + "o2_st", [128, 4], F32))
        pu = es.enter_context(nc.psum_tensor(_uid(nc) + "o2_pu", [128, 2, 512], F32))
        pd = es.enter_context(nc.psum_tensor(_uid(nc) + "o2_pd", [128, 4, 512], F32))
        if g_final is not None:
            S.add("sp", lambda e: e.dma_start(out=gB[:], in_=g_final.partition_broadcast(128)), writes=["gB"], dma=True)
        for hf in range(2):
            t0 = hf * HT
            for q in range(4):
                S.add("sp", lambda e, q=q, t0=t0: e.dma_start(
                    out=x1[:, 2 * q:2 * q + 2, :],
                    in_=x1_in[t0 + q * 256:t0 + (q + 1) * 256, :].rearrange("(t p) d -> p t d", p=128)),
                    writes=[("x1", 2 * q), ("x1", 2 * q + 1)], dma=True)
                S.add("sp", lambda e, q=q, t0=t0: e.dma_start(
                    out=hT[:, q * 4:(q + 1) * 4, :],
                    in_=h2T[q * 512:(q + 1) * 512, t0:t0 + HT].rearrange("(kc p) t -> p kc t", p=128)),
                    writes=[("hT", q)], dma=True)

            def load_w(fb):
                b = fb % 2
                for q in range(2):
                    S.add("pool", lambda e, fb=fb, b=b, q=q: e.dma_start(
                        out=wu[:, b, q * 8:(q + 1) * 8, :],
                        in_=w_up[q * 1024:(q + 1) * 1024, fb * 512:(fb + 1) * 512].rearrange("(kc p) n -> p kc n", p=128)),
                        writes=[("wu", b, q)], dma=True)
                S.add("pool", lambda e, fb=fb, b=b: e.dma_start(
                    out=wd[:, b, :, :],
                    in_=w_down[fb * 512:(fb + 1) * 512, :].rearrange("(fc p) n -> p fc n", p=128)),
                    writes=[("wd", b)], dma=True)

            def up(fb):
                b = fb % 2
                n = 0
                for fc in range(4):
                    for tg in range(HT // 512):
                        slot = n % 2
                        n += 1
                        for kc in range(KC):
                            S.add("pe", lambda e, b=b, fc=fc, tg=tg, kc=kc, slot=slot: e.matmul(
                                pu[:, slot, :], lhsT=wu[:, b, kc, fc * 128:(fc + 1) * 128],
                                rhs=hT[:, kc, tg * 512:(tg + 1) * 512], start=(kc == 0), stop=(kc == KC - 1)),
                                reads=[("wu", b, kc // 8), ("hT", kc // 4)], writes=[("pu", slot)], rhs=[("hT", kc // 4)])
                        S.add("act", lambda e, b=b, fc=fc, tg=tg, slot=slot: e.activation(
                            out=aT[:, b, fc, tg * 512:(tg + 1) * 512], in_=pu[:, slot, :], func=AF.Relu),
                            reads=[("pu", slot)], writes=[("aT", b, fc)])
                        S.add("pool", lambda e, b=b, fc=fc, tg=tg: e.tensor_tensor(
                            out=aT[:, b, fc, tg * 512:(tg + 1) * 512], in0=aT[:, b, fc, tg * 512:(tg + 1) * 512],
                            in1=aT[:, b, fc, tg * 512:(tg + 1) * 512], op=ALU.mult),
                            reads=[("aT", b, fc)], writes=[("aT", b, fc)])

            def down(fb):
                b = fb % 2
                n = 0
                for tt in range(HT // 128):
                    for cg in range(4):
                        slot = n % 4
                        n += 1
                        for fc in range(4):
                            S.add("pe", lambda e, b=b, tt=tt, cg=cg, fc=fc, slot=slot: e.matmul(
                                pd[:, slot, :], lhsT=aT[:, b, fc, tt * 128:(tt + 1) * 128],
                                rhs=wd[:, b, fc, cg * 512:(cg + 1) * 512], start=(fc == 0), stop=(fc == 3)),
                                reads=[("aT", b, fc), ("wd", b)], writes=[("pd", slot)], rhs=[("wd", b)])
                        S.add("dve", lambda e, tt=tt, cg=cg, slot=slot: e.tensor_tensor(
                            out=x1[:, tt, cg * 512:(cg + 1) * 512], in0=pd[:, slot, :],
                            in1=x1[:, tt, cg * 512:(cg + 1) * 512], op=ALU.add),
                            reads=[("pd", slot), ("x1", tt)], writes=[("x1", tt)])

            load_w(0)
            up(0)
            for fb in range(NFB):
                if fb + 1 < NFB:
                    load_w(fb + 1)
                    up(fb + 1)
                down(fb)
            for tt in range(HT // 128):
                if g_final is not None:
                    b = tt % 2
                    rms_rstd(S, "act", x1[:, tt, :], junk[:], st[:, b:b + 1], st[:, 2 + b:3 + b], D, [("x1", tt)], "o2n%d" % b)
                    S.add("dve", lambda e, tt=tt, b=b: e.scalar_tensor_tensor(
                        out=x1[:, tt, :], in0=x1[:, tt, :], scalar=st[:, 2 + b:3 + b], in1=gB[:], op0=ALU.mult, op1=ALU.mult),
                        reads=[("x1", tt), "o2n%d_rstd" % b, "gB"], writes=[("x1", tt)])
                S.add("sp", lambda e, tt=tt, t0=t0: e.dma_start(out=x_out[t0 + tt * 128:t0 + (tt + 1) * 128, :], in_=x1[:, tt, :]),
                      reads=[("x1", tt)], dma=True)
        S.run_block()


SEQ = 8192
WIN_C = 1408
QRH = 640
KRH = 576
QA = (0, 192)
QB_ROW = 384
QC_ROW = 512
KROPE_ROW = 256
KB_ROW = 320
KC_ROW = 448


def add_ag(S, src, dst, rkeys, wkeys):
    S.add("pool", lambda e: e.collective_compute(
        "AllGather", ALU.bypass, replica_groups=[[0, 1, 2, 3], [4, 5, 6, 7]], ins=[src], outs=[dst]),
        reads=rkeys, writes=wkeys, cc=True)


def phase_P1(nc, S, x_in, gT_attn, hT_out, hT_gath):
    with ExitStack() as es:
        xt = es.enter_context(nc.sbuf_tensor(_uid(nc) + "p1_xt", [128, NTT, D], F32))
        xn = es.enter_context(nc.sbuf_tensor(_uid(nc) + "p1_xn", [128, 2, D], BF16))
        junk = es.enter_context(nc.sbuf_tensor(_uid(nc) + "p1_junk", [128, D], BF16))
        hT = es.enter_context(nc.sbuf_tensor(_uid(nc) + "p1_hT", [128, 2, KC, 256], BF16))
        gT = es.enter_context(nc.sbuf_tensor(_uid(nc) + "p1_gT", [128, KC], F32))
        ss = es.enter_context(nc.sbuf_tensor(_uid(nc) + "p1_ss", [128, NTT], F32))
        rs = es.enter_context(nc.sbuf_tensor(_uid(nc) + "p1_rs", [128, NTT], F32))
        ident_f = es.enter_context(nc.sbuf_tensor(_uid(nc) + "p1_identf", [128, 128], F32))
        ident = es.enter_context(nc.sbuf_tensor(_uid(nc) + "p1_ident", [128, 128], BF16))
        pt = es.enter_context(nc.psum_tensor(_uid(nc) + "p1_pt", [128, 2, 8, 128], BF16))
        make_ident(S, ident, ident_f)
        S.add("sp", lambda e: e.dma_start(out=gT[:], in_=gT_attn), writes=["gT"], dma=True)
        for tt in range(NTT):
            S.add("sp", lambda e, tt=tt: e.dma_start(out=xt[:, tt, :], in_=x_in[tt * 128:(tt + 1) * 128, :]),
                  writes=[("xt", tt)], dma=True)
        for tt in range(NTT):
            S.add("act", lambda e, tt=tt: e.activation(out=junk[:], in_=xt[:, tt, :], func=AF.Square, accum_out=ss[:, tt:tt + 1]),
                  reads=[("xt", tt)], writes=["junk", ("ss", tt)])
        S.add("dve", lambda e: e.tensor_scalar(out=rs[:], in0=ss[:], scalar1=1.0 / D, scalar2=EPS, op0=ALU.mult, op1=ALU.add),
              reads=[("ss", tt) for tt in range(NTT)], writes=["rs"])
        S.add("act", lambda e: e.activation(out=rs[:], in_=rs[:], func=AF.Sqrt), reads=["rs"], writes=["rs"])
        S.add("dve", lambda e: e.reciprocal(out=rs[:], in_=rs[:]), reads=["rs"], writes=["rs"])
        for tt in range(NTT):
            b = tt % 2
            cb = (tt // 2) % 2
            co = (tt % 2) * 128
            S.add("act", lambda e, tt=tt, b=b: e.activation(out=xn[:, b, :], in_=xt[:, tt, :], func=AF.Copy, scale=rs[:, tt:tt + 1]),
                  reads=[("xt", tt), "rs"], writes=[("xn", b)])
            for hg in range(2):
                for j in range(8):
                    kc = hg * 8 + j
                    S.add("pe", lambda e, b=b, hg=hg, j=j, kc=kc: e.transpose(
                        out=pt[:, hg, j, :], in_=xn[:, b, kc * 128:(kc + 1) * 128], identity=ident[:]),
                        reads=[("xn", b), "ident"], writes=[("pt", hg)])
                for j in range(8):
                    kc = hg * 8 + j
                    if j % 2 == 0:
                        S.add("act", lambda e, cb=cb, co=co, hg=hg, j=j, kc=kc: e.activation(
                            out=hT[:, cb, kc, co:co + 128], in_=pt[:, hg, j, :], func=AF.Copy, scale=gT[:, kc:kc + 1]),
                            reads=[("pt", hg), "gT"], writes=[("hT", cb, tt % 2)])
                    else:
                        S.add("dve", lambda e, cb=cb, co=co, hg=hg, j=j, kc=kc: e.tensor_scalar(
                            out=hT[:, cb, kc, co:co + 128], in0=pt[:, hg, j, :], scalar1=gT[:, kc:kc + 1], scalar2=None, op0=ALU.mult),
                            reads=[("pt", hg), "gT"], writes=[("hT", cb, tt % 2)])
            if tt % 2 == 1:
                ch = tt // 2
                S.add("sp", lambda e, ch=ch, cb=cb: e.dma_start(
                    out=hT_out[ch].rearrange("(kc p) t -> p kc t", p=128), in_=hT[:, cb, :, :]),
                    reads=[("hT", cb, 0), ("hT", cb, 1)], writes=[("hTc", ch)], dma=True)
                add_ag(S, hT_out[ch], hT_gath[ch], [("hTc", ch)], [("hTg", id(hT_gath[ch]))])
        S.run_block()


def phase_P2h(nc, S, hT_all, w_in_c, qagT, kvagT, w_uq_c, w_ukv_c, ropeC, ropeS, qT_out, kT_out, v_out):
    GT = 1024
    NG = SEQ // GT
    with ExitStack() as es:
        hT = es.enter_context(nc.sbuf_tensor(_uid(nc) + "p2_hT", [128, 2, KC, GT], BF16))
        wb = es.enter_context(nc.sbuf_tensor(_uid(nc) + "p2_w", [128, KC, WIN_C], BF16))
        wuq = es.enter_context(nc.sbuf_tensor(_uid(nc) + "p2_wuq", [128, 3, 512], BF16))
        wukv = es.enter_context(nc.sbuf_tensor(_uid(nc) + "p2_wukv", [128, 512], BF16))
        cnT = es.enter_context(nc.sbuf_tensor(_uid(nc) + "p2_cnT", [128, 2, 4, GT], BF16))
        junk = es.enter_context(nc.sbuf_tensor(_uid(nc) + "p2_junk", [128, 512], BF16))
        cq = es.enter_context(nc.sbuf_tensor(_uid(nc) + "p2_cq", [128, 2, 512], F32))
        cn = es.enter_context(nc.sbuf_tensor(_uid(nc) + "p2_cn", [128, 2, 512], BF16))
        stage = es.enter_context(nc.sbuf_tensor(_uid(nc) + "p2_stage", [128, 4, 512], BF16))
        rt = es.enter_context(nc.sbuf_tensor(_uid(nc) + "p2_rt", [64, 2, 2, 512], F32))
        rC = es.enter_context(nc.sbuf_tensor(_uid(nc) + "p2_rC", [64, 2, GT], F32))
        rS = es.enter_context(nc.sbuf_tensor(_uid(nc) + "p2_rS", [64, 2, GT], F32))
        qag = es.enter_context(nc.sbuf_tensor(_uid(nc) + "p2_qag", [128, 4], F32))
        st = es.enter_context(nc.sbuf_tensor(_uid(nc) + "p2_st", [128, 16], F32))
        ident_f = es.enter_context(nc.sbuf_tensor(_uid(nc) + "p2_identf", [128, 128], F32))
        ident = es.enter_context(nc.sbuf_tensor(_uid(nc) + "p2_ident", [128, 128], BF16))
        pm = es.enter_context(nc.psum_tensor(_uid(nc) + "p2_pm", [128, 4, 512], F32))
        pt = es.enter_context(nc.psum_tensor(_uid(nc) + "p2_pt", [128, 2, 4, 128], BF16))
        make_ident(S, ident, ident_f)
        S.add("sp", lambda e: e.dma_start(out=qag[:, 0:3], in_=qagT), writes=["qag"], dma=True)
        S.add("sp", lambda e: e.dma_start(out=qag[:, 3:4], in_=kvagT), reads=["qag"], writes=["qag"], dma=True)
        for q in range(4):
            S.add("pool", lambda e, q=q: e.dma_start(
                out=wb[:, q * 4:(q + 1) * 4, :], in_=w_in_c[q * 512:(q + 1) * 512, :].rearrange("(kc p) n -> p kc n", p=128)),
                writes=[("wb", q)], dma=True)
        S.add("pool", lambda e: e.dma_start(out=wuq[:], in_=w_uq_c.rearrange("(kc p) n -> p kc n", p=128)), writes=["wuq"], dma=True)
        S.add("pool", lambda e: e.dma_start(out=wukv[:], in_=w_ukv_c), writes=["wukv"], dma=True)
        wb_all = [("wb", q) for q in range(4)]

        cnt = {"pm": 0, "st": 0, "ev": 0}

        def pm_slot():
            s = cnt["pm"] % 4
            cnt["pm"] += 1
            return s

        def evac_store(ps_ap, dram_ap, nrows, ncols, slot_pm):
            ss = cnt["st"] % 4
            cnt["st"] += 1
            eng = "act" if cnt["ev"] % 2 == 0 else "dve"
            cnt["ev"] += 1
            if eng == "act":
                S.add("act", lambda e, ss=ss: e.activation(out=stage[0:nrows, ss, 0:ncols], in_=ps_ap, func=AF.Copy),
                      reads=[("pm", slot_pm)], writes=[("stage", ss)])
            else:
                S.add("dve", lambda e, ss=ss: e.tensor_copy(out=stage[0:nrows, ss, 0:ncols], in_=ps_ap),
                      reads=[("pm", slot_pm)], writes=[("stage", ss)])
            S.add("sp", lambda e, ss=ss: e.dma_start(out=dram_ap, in_=stage[0:nrows, ss, 0:ncols]),
                  reads=[("stage", ss)], dma=True)

        def rope_store(psA, psB, slotA, slotB, hb, tg, dram_ap):
            r = cnt["st"] % 2
            ss = cnt["st"] % 4
            cnt["st"] += 1
            S.add("dve", lambda e, r=r: e.tensor_tensor(out=rt[:, r, 0, :], in0=psA, in1=rC[:, hb, tg * 512:(tg + 1) * 512], op=ALU.mult),
                  reads=[("pm", slotA), ("rC", hb)], writes=[("rt", r, 0)])
            S.add("dve", lambda e, r=r: e.tensor_tensor(out=rt[:, r, 1, :], in0=psB, in1=rS[:, hb, tg * 512:(tg + 1) * 512], op=ALU.mult),
                  reads=[("pm", slotB), ("rS", hb)], writes=[("rt", r, 1)])
            S.add("pool", lambda e, r=r, ss=ss: e.tensor_tensor(out=stage[0:64, ss, :], in0=rt[:, r, 0, :], in1=rt[:, r, 1, :], op=ALU.add),
                  reads=[("rt", r, 0), ("rt", r, 1)], writes=[("stage", ss)])
            S.add("sp", lambda e, ss=ss: e.dma_start(out=dram_ap, in_=stage[0:64, ss, :]),
                  reads=[("stage", ss)], dma=True)

        def load_group(g, hb):
            r, half = divmod(g, 2)
            for c4 in range(4):
                ch = 4 * half + c4
                for q in range(4):
                    S.add("sp", lambda e, hb=hb, r=r, ch=ch, c4=c4, q=q: e.dma_start(
                        out=hT[:, hb, q * 4:(q + 1) * 4, c4 * 256:(c4 + 1) * 256],
                        in_=hT_all[ch][r * D + q * 512:r * D + (q + 1) * 512, :].rearrange("(kc p) t -> p kc t", p=128)),
                        reads=[("hTg", id(hT_all[ch]))], writes=[("hT", hb, q, c4)], dma=True)
            S.add("sp", lambda e, hb=hb, g=g: e.dma_start(out=rC[:, hb, :], in_=ropeC[:, g * GT:(g + 1) * GT]), writes=[("rC", hb)], dma=True)
            S.add("sp", lambda e, hb=hb, g=g: e.dma_start(out=rS[:, hb, :], in_=ropeS[:, g * GT:(g + 1) * GT]), writes=[("rS", hb)], dma=True)

        gorder = [0, 2, 4, 6, 1, 3, 5, 7]
        load_group(gorder[0], 0)
        for gi, g in enumerate(gorder):
            hb = gi % 2
            t0 = g * GT
            if gi + 1 < NG:
                load_group(gorder[gi + 1], (gi + 1) % 2)
            hk = [("hT", hb, q) for q in range(4)]
            for tt in range(GT // 128):
                b = tt % 2
                sl = pm_slot()
                for kc in range(KC):
                    S.add("pe", lambda e, hb=hb, tt=tt, kc=kc, sl=sl: e.matmul(
                        pm[:, sl, :], lhsT=hT[:, hb, kc, tt * 128:(tt + 1) * 128], rhs=wb[:, kc, 0:512],
                        start=(kc == 0), stop=(kc == KC - 1)),
                        reads=[("hT", hb, kc // 4, x) for x in range(4)] + [("wb", kc // 4)], writes=[("pm", sl)])
                S.add("dve", lambda e, b=b, sl=sl: e.tensor_copy(out=cq[:, b, :], in_=pm[:, sl, :]),
                      reads=[("pm", sl)], writes=[("cq", b)])
                rms_rstd(S, "act", cq[:, b, 0:384], junk[:, 0:384], st[:, 4 + b:5 + b], st[:, 6 + b:7 + b], 384, [("cq", b)], "pq%d" % b)
                rms_rstd(S, "act", cq[:, b, 384:512], junk[:, 384:512], st[:, 8 + b:9 + b], st[:, 10 + b:11 + b], 128, [("cq", b)], "pk%d" % b)
                S.add("act", lambda e, b=b: e.activation(out=cn[:, b, 0:384], in_=cq[:, b, 0:384], func=AF.Copy, scale=st[:, 6 + b:7 + b]),
                      reads=[("cq", b), "pq%d_rstd" % b], writes=[("cn", b, 0)])
                S.add("act", lambda e, b=b: e.activation(out=cn[:, b, 384:512], in_=cq[:, b, 384:512], func=AF.Copy, scale=st[:, 10 + b:11 + b]),
                      reads=[("cq", b), "pk%d_rstd" % b], writes=[("cn", b, 1)])
                hg = tt % 2
                for j in range(4):
                    S.add("pe", lambda e, b=b, hg=hg, j=j: e.transpose(
                        out=pt[:, hg, j, :], in_=cn[:, b, j * 128:(j + 1) * 128], identity=ident[:]),
                        reads=[("cn", b, 0), ("cn", b, 1), "ident"], writes=[("pt", hg)])
                for j in range(4):
                    S.add("act", lambda e, hb=hb, tt=tt, hg=hg, j=j: e.activation(
                        out=cnT[:, hb, j, tt * 128:(tt + 1) * 128], in_=pt[:, hg, j, :], func=AF.Copy, scale=qag[:, j:j + 1]),
                        reads=[("pt", hg), "qag"], writes=[("cnT", hb, tt)])
            ck = [("cnT", hb, tt) for tt in range(GT // 128)]
            for tg in range(GT // 512):
                c0t = t0 + tg * 512
                slA = pm_slot()
                slB = pm_slot()
                for v_, sl in ((0, slA), (1, slB)):
                    for kc in range(KC):
                        S.add("pe", lambda e, hb=hb, tg=tg, kc=kc, sl=sl, v_=v_: e.matmul(
                            pm[0:64, sl, :], lhsT=wb[:, kc, 512 + 64 * v_:576 + 64 * v_], rhs=hT[:, hb, kc, tg * 512:(tg + 1) * 512],
                            start=(kc == 0), stop=(kc == KC - 1)),
                            reads=[("hT", hb, kc // 4, x) for x in range(4)] + [("wb", kc // 4)], writes=[("pm", sl)])
                rope_store(pm[0:64, slA, :], pm[0:64, slB, :], slA, slB, hb, tg, kT_out[KROPE_ROW:KROPE_ROW + 64, c0t:c0t + 512])
                for fi, (dst, r0) in enumerate(((qT_out, QB_ROW), (kT_out, KB_ROW), (qT_out, QC_ROW), (kT_out, KC_ROW))):
                    sl = pm_slot()
                    for kc in range(KC):
                        S.add("pe", lambda e, hb=hb, tg=tg, kc=kc, sl=sl, fi=fi: e.matmul(
                            pm[:, sl, :], lhsT=wb[:, kc, 640 + fi * 128:640 + (fi + 1) * 128], rhs=hT[:, hb, kc, tg * 512:(tg + 1) * 512],
                            start=(kc == 0), stop=(kc == KC - 1)),
                            reads=[("hT", hb, kc // 4, x) for x in range(4)] + [("wb", kc // 4)], writes=[("pm", sl)])
                    evac_store(pm[:, sl, :], dst[r0:r0 + 128, c0t:c0t + 512], 128, 512, sl)
                for hh in range(2):
                    sl = pm_slot()
                    for kc in range(3):
                        S.add("pe", lambda e, hb=hb, tg=tg, kc=kc, sl=sl, hh=hh: e.matmul(
                            pm[:, sl, :], lhsT=wuq[:, kc, hh * 256:hh * 256 + 128], rhs=cnT[:, hb, kc, tg * 512:(tg + 1) * 512],
                            start=(kc == 0), stop=(kc == 2)),
                            reads=ck[tg * 4:(tg + 1) * 4] + ["wuq"], writes=[("pm", sl)])
                    evac_store(pm[:, sl, :], qT_out[hh * 192:hh * 192 + 128, c0t:c0t + 512], 128, 512, sl)
                    slA = pm_slot()
                    slB = pm_slot()
                    for v_, sl in ((0, slA), (1, slB)):
                        for kc in range(3):
                            S.add("pe", lambda e, hb=hb, tg=tg, kc=kc, sl=sl, hh=hh, v_=v_: e.matmul(
                                pm[0:64, sl, :], lhsT=wuq[:, kc, hh * 256 + 128 + 64 * v_:hh * 256 + 192 + 64 * v_],
                                rhs=cnT[:, hb, kc, tg * 512:(tg + 1) * 512], start=(kc == 0), stop=(kc == 2)),
                                reads=ck[tg * 4:(tg + 1) * 4] + ["wuq"], writes=[("pm", sl)])
                    rope_store(pm[0:64, slA, :], pm[0:64, slB, :], slA, slB, hb, tg, qT_out[hh * 192 + 128:hh * 192 + 192, c0t:c0t + 512])
                    sl = pm_slot()
                    S.add("pe", lambda e, hb=hb, tg=tg, sl=sl, hh=hh: e.matmul(
                        pm[:, sl, :], lhsT=wukv[:, hh * 128:(hh + 1) * 128], rhs=cnT[:, hb, 3, tg * 512:(tg + 1) * 512],
                        start=True, stop=True),
                        reads=ck[tg * 4:(tg + 1) * 4] + ["wukv"], writes=[("pm", sl)])
                    evac_store(pm[:, sl, :], kT_out[hh * 128:(hh + 1) * 128, c0t:c0t + 512], 128, 512, sl)
            for tt in range(GT // 128):
                sl = pm_slot()
                S.add("pe", lambda e, hb=hb, tt=tt, sl=sl: e.matmul(
                    pm[:, sl, 0:256], lhsT=cnT[:, hb, 3, tt * 128:(tt + 1) * 128], rhs=wukv[:, 256:512], start=True, stop=True),
                    reads=[("cnT", hb, tt), "wukv"], writes=[("pm", sl)])
                for kc in range(KC):
                    S.add("pe", lambda e, hb=hb, tt=tt, kc=kc, sl=sl: e.matmul(
                        pm[:, sl, 256:512], lhsT=hT[:, hb, kc, tt * 128:(tt + 1) * 128], rhs=wb[:, kc, 1152:1408],
                        start=(kc == 0), stop=(kc == KC - 1)),
                        reads=[("hT", hb, kc // 4, x) for x in range(4)] + [("wb", kc // 4)], writes=[("pm", sl)])
                evac_store(pm[:, sl, :], v_out[t0 + tt * 128:t0 + (tt + 1) * 128, :], 128, 512, sl)
        S.run_block()


def phase_ATT(nc, S, qT, kT, v, kaug, qaug, maskA, corrB, biasC, maskC, lamv, sublnT, lam_init, mixT_out, mix_gath):
    NB = SEQ // 512
    scA = float(192 ** -0.5)
    scB = 0.125
    scC = float(128 ** -0.5)
    with ExitStack() as es:
        T = lambda n, s, dt: es.enter_context(nc.sbuf_tensor(_uid(nc) + n, s, dt))
        vsb = T("a_v", [128, SEQ // 128, 512], BF16)
        kbuf = T("a_k", [128, 2, SEQ], BF16)
        krope = T("a_kr", [64, SEQ], BF16)
        qn = T("a_qn", [128, 3, 512], BF16)
        qr = T("a_qr", [64, 3, 512], BF16)
        pT = T("a_pT", [128, 6, 512], BF16)
        acc = T("a_acc", [128, 2, 512], F32)
        rec = T("a_rec", [128, 2, 512], F32)
        dsb = T("a_dsb", [128, 2, 512], F32)
        acc2 = T("a_acc2", [128, 2, 512], F32)
        ones5 = T("a_ones5", [128, 512], F32)
        ones_b = T("a_onesb", [128, 128], BF16)
        sel_f = T("a_self", [128, 128], F32)
        dsum = T("a_dsum", [128, 2, 512], F32)
        mhalf = T("a_mhalf", [128, 512], F32)
        ost = T("a_ost", [128, 2, 512], BF16)
        tmp0 = T("a_tmp0", [128, 512], F32)
        t1 = T("a_t1", [128, 512], F32)
        ob_ = T("a_o", [128, 512], F32)
        sq = T("a_sq", [128, 512], F32)
        tS = T("a_tS", [128, 2, 128], F32)
        bm = T("a_bm", [128, 5, 128], F32)
        mC = T("a_mC", [128, 2, 128], F32)
        mA = T("a_mA", [128, 128], BF16)
        cB = T("a_cB", [128, 128], BF16)
        ident_f = T("a_identf", [128, 128], F32)
        ident = T("a_ident", [128, 128], BF16)
        ones_f = T("a_ones", [128, 128], F32)
        lv = T("a_lv", [128, 4, 64], F32)
        lt = T("a_lt", [128, 2, 64], F32)
        ls = T("a_ls", [128, 8], F32)
        sg = T("a_sg", [128, 2], F32)
        ps_s = es.enter_context(nc.psum_tensor(_uid(nc) + "a_ps_s", [128, 4, 512], F32))
        ps_o = es.enter_context(nc.psum_tensor(_uid(nc) + "a_ps_o", [128, 2, 512], F32))
        ps_d = es.enter_context(nc.psum_tensor(_uid(nc) + "a_ps_d", [128, 2, 512], F32))

        make_ident(S, ident, ident_f)
        S.add("pool", lambda e: e.memset(ones_f[:], 1.0), writes=["ones"])
        S.add("pool", lambda e: e.memset(ones5[:], -1.0), writes=["ones5"])
        S.add("dve", lambda e: e.tensor_copy(out=ones_b[:], in_=ones_f[:]), reads=["ones"], writes=["onesb"])
        S.add("pool", lambda e: e.memset(sel_f[:], 0.0), writes=["sel"])
        for r_ in (0, 32, 64):
            S.add("pool", lambda e, r_=r_: e.memset(sel_f[r_:r_ + 1, :], 1.0), reads=["sel"], writes=["sel"])
        S.add("pool", lambda e: e.memset(mhalf[:], -0.5), writes=["mhalf"])
        S.add("sp", lambda e: e.dma_start(out=mA[:], in_=maskA), writes=["mA"], dma=True)
        S.add("sp", lambda e: e.dma_start(out=cB[:], in_=corrB), writes=["cB"], dma=True)
        S.add("sp", lambda e: e.dma_start(out=bm[:], in_=biasC.rearrange("d k q -> k d q")), writes=["bm"], dma=True)
        S.add("sp", lambda e: e.dma_start(out=mC[:], in_=maskC.rearrange("d k q -> k d q")), writes=["mC"], dma=True)
        S.add("sp", lambda e: e.dma_start(out=lv[:], in_=lamv.partition_broadcast(128)), writes=["lv"], dma=True)
        S.add("sp", lambda e: e.dma_start(out=sg[:, 0:1], in_=sublnT), writes=["sg0"], dma=True)
        S.add("sp", lambda e: e.dma_start(out=vsb[:], in_=v.rearrange("(kt p) c -> p kt c", p=128)), writes=["vsb"], dma=True)
        S.add("sp", lambda e: e.dma_start(out=krope[:], in_=kT[KROPE_ROW:KROPE_ROW + 64, :]), writes=["krope"], dma=True)
        S.add("dve", lambda e: e.tensor_tensor(out=bm[:, 0, :], in0=bm[:, 0, :], in1=mC[:, 0, :], op=ALU.add), reads=["bm", "mC"], writes=["bm"])
        S.add("dve", lambda e: e.tensor_tensor(out=bm[:, 4, :], in0=bm[:, 4, :], in1=mC[:, 1, :], op=ALU.add), reads=["bm", "mC"], writes=["bm"])
        for i in range(2):
            S.add("dve", lambda e, i=i: e.tensor_tensor(out=lt[:, i, :], in0=lv[:, 2 * i, :], in1=lv[:, 2 * i + 1, :], op=ALU.mult),
                  reads=["lv"], writes=[("lt", i)])
            S.add("dve", lambda e, i=i: e.reduce_sum(out=ls[:, i:i + 1], in_=lt[:, i, :], axis=AX.X), reads=[("lt", i)], writes=[("ls", i)])
            S.add("act", lambda e, i=i: e.activation(out=ls[:, 2 + i:3 + i], in_=ls[:, i:i + 1], func=AF.Exp), reads=[("ls", i)], writes=[("le", i)])
        S.add("dve", lambda e: e.tensor_tensor(out=ls[:, 4:5], in0=ls[:, 3:4], in1=ls[:, 2:3], op=ALU.subtract), reads=[("le", 0), ("le", 1)], writes=["nl0"])
        S.add("dve", lambda e: e.tensor_scalar(out=ls[:, 5:6], in0=ls[:, 4:5], scalar1=-float(lam_init), scalar2=None, op0=ALU.add), reads=["nl0"], writes=["nlam"])
        S.add("dve", lambda e: e.tensor_scalar(out=sg[:, 1:2], in0=sg[:, 0:1], scalar1=float(1.0 - lam_init), scalar2=None, op0=ALU.mult), reads=["sg0"], writes=["sgain"])
        nlam = ls[:, 5:6]
        sgain = sg[:, 1:2]

        def load_k(kind, slot):
            if kind in ("A0", "A1"):
                r0 = 0 if kind == "A0" else 128
                for q in range(2):
                    S.add("sp", lambda e, q=q, r0=r0, slot=slot: e.dma_start(
                        out=kbuf[:, slot, q * 4096:(q + 1) * 4096], in_=kT[r0:r0 + 128, q * 4096:(q + 1) * 4096]),
                        writes=[("kbuf", slot, q)], dma=True)
            elif kind in ("B0", "B1"):
                n = 0 if kind == "B0" else 1
                for q in range(2):
                    S.add("sp", lambda e, q=q, n=n, slot=slot: e.dma_start(
                        out=kbuf[0:64, slot, q * 4096:(q + 1) * 4096], in_=kT[KB_ROW + 64 * n:KB_ROW + 64 * n + 64, q * 4096:(q + 1) * 4096]),
                        writes=[("kbuf", slot, q)], dma=True)
                S.add("sp", lambda e, slot=slot: e.dma_start(out=kbuf[64:68, slot, :], in_=kaug),
                      reads=[("kbuf", slot, 0), ("kbuf", slot, 1)], writes=[("kbuf", slot, 0), ("kbuf", slot, 1)], dma=True)
            else:
                for q in range(2):
                    S.add("sp", lambda e, q=q, slot=slot: e.dma_start(
                        out=kbuf[:, slot, q * 4096:(q + 1) * 4096], in_=kT[KC_ROW:KC_ROW + 128, q * 4096:(q + 1) * 4096]),
                        writes=[("kbuf", slot, q)], dma=True)

        qcnt = [0]

        def load_q(kind, I):
            qs = qcnt[0] % 3
            qcnt[0] += 1
            c = slice(I * 512, (I + 1) * 512)
            if kind in ("A0", "A1"):
                r0 = 0 if kind == "A0" else 192
                S.add("sp", lambda e: e.dma_start(out=qn[:, qs, :], in_=qT[r0:r0 + 128, c]), writes=[("q", qs)], dma=True)
                S.add("sp", lambda e: e.dma_start(out=qr[:, qs, :], in_=qT[r0 + 128:r0 + 192, c]), writes=[("qr", qs)], dma=True)
            elif kind in ("B0", "B1"):
                n = 0 if kind == "B0" else 1
                S.add("sp", lambda e: e.dma_start(out=qn[0:64, qs, :], in_=qT[QB_ROW + 64 * n:QB_ROW + 64 * n + 64, c]), writes=[("q", qs)], dma=True)
                S.add("sp", lambda e: e.dma_start(out=qn[64:68, qs, :], in_=qaug[:, c]), reads=[("q", qs)], writes=[("q", qs)], dma=True)
            else:
                S.add("sp", lambda e: e.dma_start(out=qn[:, qs, :], in_=qT[QC_ROW:QC_ROW + 128, c]), writes=[("q", qs)], dma=True)
            return qs

        blocks = []
        for I in range(NB):
            blocks.append(("A0", I, 0))
        for I in range(NB):
            blocks.append(("A1", I, 1))
        for I in range(NB):
            blocks.append(("B0", I, 0))
            blocks.append(("B1", I, 1))
        for I in range(NB):
            blocks.append(("C", I, 0))
        vcol = {"A0": 0, "A1": 128, "B0": 256, "B1": 256, "C": 384}
        orow = {"A0": 0, "A1": 128, "B1": 256, "C": 384}

        tiles = []
        for bi, (kind, I, ks) in enumerate(blocks):
            tl = []
            if kind == "C":
                for qi in range(4):
                    qt = 4 * I + qi
                    dl = [d for d in (4, 3, 2, 1, 0) if qt - d >= 0]
                    for d in dl:
                        tl.append(dict(kt=qt - d, c0=128 * qi, n=128, mask=None, bias=d, fc=(d == dl[0]), lc=(d == 0)))
            else:
                for kt in range(4 * I + 4):
                    t = kt - 4 * I
                    if t < 0:
                        tl.append(dict(kt=kt, c0=0, n=512, mask=None, bias=None, fc=(kt == 0), lc=False))
                    else:
                        tl.append(dict(kt=kt, c0=128 * t, n=512 - 128 * t, mask=True, bias=None, fc=(kt == 0), lc=False))
                tl[-1]["lc"] = True
            for ti, t in enumerate(tl):
                t.update(kind=kind, I=I, ks=ks, bi=bi, ob=bi % 2, ti=ti, first_blk=(ti == 0), last_blk=(ti == len(tl) - 1))
                tiles.append(t)
        N = len(tiles)
        for g, t in enumerate(tiles):
            t["ss"] = g % 4
            t["ps"] = g % 6
            t["tsr"] = g % 2

        blk_q = {}

        def start_block(bi):
            if bi >= len(blocks) or bi in blk_q:
                return
            kind, I, ks = blocks[bi]
            blk_q[bi] = load_q(kind, I)

        def emit_qk(g):
            t = tiles[g]
            kind, ks, ss, c0, n, kt = t["kind"], t["ks"], t["ss"], t["c0"], t["n"], t["kt"]
            qs = blk_q[t["bi"]]
            kq = t["kt"] // 32
            has_mask = t["mask"] is not None
            if kind in ("A0", "A1"):
                S.add("pe", lambda e: e.matmul(ps_s[:, ss, c0:c0 + n], lhsT=kbuf[:, ks, kt * 128:(kt + 1) * 128], rhs=qn[:, qs, c0:c0 + n],
                                               start=True, stop=False),
                      reads=[("kbuf", ks, kq), ("q", qs)], writes=[("ps_s", ss)], rhs=[("q", qs)])
                S.add("pe", lambda e: e.matmul(ps_s[:, ss, c0:c0 + n], lhsT=krope[:, kt * 128:(kt + 1) * 128], rhs=qr[:, qs, c0:c0 + n],
                                               start=False, stop=not has_mask),
                      reads=["krope", ("qr", qs)], writes=[("ps_s", ss)], rhs=[("qr", qs)])
                if has_mask:
                    S.add("pe", lambda e: e.matmul(ps_s[:, ss, c0:c0 + 128], lhsT=ident[:], rhs=mA[:], start=False, stop=True),
                          reads=["ident", "mA"], writes=[("ps_s", ss)])
            elif kind in ("B0", "B1"):
                S.add("pe", lambda e: e.matmul(ps_s[:, ss, c0:c0 + n], lhsT=kbuf[0:68, ks, kt * 128:(kt + 1) * 128], rhs=qn[0:68, qs, c0:c0 + n],
                                               start=True, stop=not has_mask),
                      reads=[("kbuf", ks, kq), ("q", qs)], writes=[("ps_s", ss)], rhs=[("q", qs)])
                if has_mask:
                    S.add("pe", lambda e: e.matmul(ps_s[:, ss, c0:c0 + 128], lhsT=ident[:], rhs=cB[:], start=False, stop=True),
                          reads=["ident", "cB"], writes=[("ps_s", ss)])
            else:
                S.add("pe", lambda e: e.matmul(ps_s[:, ss, c0:c0 + n], lhsT=kbuf[:, ks, kt * 128:(kt + 1) * 128], rhs=qn[:, qs, c0:c0 + n],
                                               start=True, stop=True),
                      reads=[("kbuf", ks, kq), ("q", qs)], writes=[("ps_s", ss)], rhs=[("q", qs)])

        def emit_exp(g):
            t = tiles[g]
            kind, ss, ps, c0, n = t["kind"], t["ss"], t["ps"], t["c0"], t["n"]
            if kind == "C":
                r = t["tsr"]
                d = t["bias"]
                S.add("dve", lambda e: e.scalar_tensor_tensor(out=tS[:, r, :], in0=ps_s[:, ss, c0:c0 + n], scalar=scC, in1=bm[:, d, :],
                                                              op0=ALU.mult, op1=ALU.add),
                      reads=[("ps_s", ss), "bm"], writes=[("tS", r)])
                S.add("act", lambda e: e.activation(out=pT[:, ps, c0:c0 + n], in_=tS[:, r, :], func=AF.Exp),
                      reads=[("tS", r)], writes=[("pT", ps)])
            else:
                sc = scA if kind in ("A0", "A1") else scB
                S.add("act", lambda e: e.activation(out=pT[:, ps, c0:c0 + n], in_=ps_s[:, ss, c0:c0 + n], func=AF.Exp, scale=sc),
                      reads=[("ps_s", ss)], writes=[("pT", ps)])

        def emit_pv(g):
            t = tiles[g]
            ps, c0, n, kt, ob = t["ps"], t["c0"], t["n"], t["kt"], t["ob"]
            vc = vcol[t["kind"]]
            S.add("pe", lambda e: e.matmul(ps_o[:, ob, c0:c0 + n], lhsT=vsb[:, kt, vc:vc + 128], rhs=pT[:, ps, c0:c0 + n],
                                           start=t["fc"], stop=t["lc"]),
                  reads=["vsb", ("pT", ps)], writes=[("ps_o", ob)], rhs=[("pT", ps)])

        den_pend = []

        def flush_den():
            for cgi, g in enumerate(den_pend):
                t = tiles[g]
                ps, c0, n, ob = t["ps"], t["c0"], t["n"], t["ob"]
                S.add("pe", lambda e, cgi=cgi, ps=ps, c0=c0, n=n, ob=ob: e.matmul(
                    ps_d[32 * cgi:32 * cgi + 32, ob, c0:c0 + n], lhsT=ones_b[:, 0:32], rhs=pT[:, ps, c0:c0 + n],
                    start=False, stop=False, skip_group_check=True, tile_position=(0, 32 * cgi)),
                    reads=["onesb", ("pT", ps)], writes=[("ps_d", ob)], rhs=[("pT", ps)])
            del den_pend[:]

        def emit_acc(g):
            t = tiles[g]
            if t["first_blk"]:
                ob = t["ob"]
                S.add("dve", lambda e: e.memset(ps_d[:, ob, :], 0.0), writes=[("ps_d", ob)])
            den_pend.append(g)
            if len(den_pend) == 3 or t["last_blk"]:
                flush_den()

        def emit_epilogue(bi):
            kind, I, ks = blocks[bi]
            ob = bi % 2
            c = slice(I * 512, (I + 1) * 512)
            S.add("dve", lambda e: e.tensor_copy(out=dsum[:, ob, :], in_=ps_d[:, ob, :]), reads=[("ps_d", ob)], writes=[("dsum", ob)])
            S.add("pe", lambda e: e.matmul(ps_d[:, ob, :], lhsT=sel_f[:], rhs=dsum[:, ob, :], start=True, stop=True),
                  reads=["sel", ("dsum", ob)], writes=[("ps_d", ob)], rhs=[("dsum", ob)])
            S.add("act", lambda e: e.activation(out=dsb[:, ob, :], in_=ps_d[:, ob, :], func=AF.Ln), reads=[("ps_d", ob)], writes=[("dsb", ob)])
            S.add("act", lambda e: e.activation(out=rec[:, ob, :], in_=dsb[:, ob, :], func=AF.Exp, scale=-1.0), reads=[("dsb", ob)], writes=[("rec", ob)])
            if kind in ("A0", "A1", "C"):
                S.add("dve", lambda e: e.tensor_tensor(out=ost[:, ob, :], in0=ps_o[:, ob, :], in1=rec[:, ob, :], op=ALU.mult),
                      reads=[("ps_o", ob), ("rec", ob)], writes=[("ost", ob)])
                r0 = orow[kind]
                S.add("pool", lambda e: e.dma_start(out=mixT_out[r0 // 64][:, c], in_=ost[0:64, ob, :]), reads=[("ost", ob)], writes=[("mixc", r0 // 64)], dma=True)
                S.add("pool", lambda e: e.dma_start(out=mixT_out[r0 // 64 + 1][:, c], in_=ost[64:128, ob, :]), reads=[("ost", ob)], writes=[("mixc", r0 // 64 + 1)], dma=True)
                if I == NB - 1:
                    for ch in (r0 // 64, r0 // 64 + 1):
                        add_ag(S, mixT_out[ch], mix_gath[ch], [("mixc", ch)], [("mixg", id(mix_gath[ch]))])
            elif kind == "B0":
                S.add("dve", lambda e: e.tensor_tensor(out=tmp0[:], in0=ps_o[:, ob, :], in1=rec[:, ob, :], op=ALU.mult),
                      reads=[("ps_o", ob), ("rec", ob)], writes=["tmp0"])
            else:
                S.add("dve", lambda e: e.tensor_tensor(out=t1[:], in0=ps_o[:, ob, :], in1=rec[:, ob, :], op=ALU.mult),
                      reads=[("ps_o", ob), ("rec", ob)], writes=["t1"])
                S.add("dve", lambda e: e.scalar_tensor_tensor(out=ob_[:], in0=t1[:], scalar=nlam, in1=tmp0[:], op0=ALU.mult, op1=ALU.add),
                      reads=["t1", "tmp0", "nlam"], writes=["o"])
                S.add("pool", lambda e: e.tensor_tensor(out=sq[:], in0=ob_[:], in1=ob_[:], op=ALU.mult), reads=["o"], writes=["sq"])
                S.add("pe", lambda e: e.matmul(ps_d[:, ob, :], lhsT=ones_f[:], rhs=sq[:], start=True, stop=True),
                      reads=["ones", "sq"], writes=[("ps_d", ob)])
                S.add("dve", lambda e: e.tensor_scalar(out=sq[:], in0=ps_d[:, ob, :], scalar1=1.0 / 128, scalar2=EPS, op0=ALU.mult, op1=ALU.add),
                      reads=[("ps_d", ob)], writes=["sq"])
                S.add("act", lambda e: e.activation(out=sq[:], in_=sq[:], func=AF.Ln), reads=["sq"], writes=["sq"])
                S.add("act", lambda e: e.activation(out=sq[:], in_=sq[:], func=AF.Exp, scale=-0.5), reads=["sq"], writes=["sq"])
                S.add("dve", lambda e: e.scalar_tensor_tensor(out=ost[:, ob, :], in0=ob_[:], scalar=sgain, in1=sq[:], op0=ALU.mult, op1=ALU.mult),
                      reads=["o", "sq", "sgain"], writes=[("ost", ob)])
                S.add("pool", lambda e: e.dma_start(out=mixT_out[4][:, c], in_=ost[0:64, ob, :]), reads=[("ost", ob)], writes=[("mixc", 4)], dma=True)
                S.add("pool", lambda e: e.dma_start(out=mixT_out[5][:, c], in_=ost[64:128, ob, :]), reads=[("ost", ob)], writes=[("mixc", 5)], dma=True)
                if I == NB - 1:
                    for ch in (4, 5):
                        add_ag(S, mixT_out[ch], mix_gath[ch], [("mixc", ch)], [("mixg", id(mix_gath[ch]))])

        unit_first = {}
        for bi, (kind, I, ks) in enumerate(blocks):
            unit_first.setdefault(kind, bi)
        load_k("A0", 0)
        load_k("A1", 1)
        start_block(0)
        start_block(1)

        def maybe_unit_loads(bi):
            kind, I, ks = blocks[bi]
            if unit_first[kind] != bi:
                return
            if kind == "A1":
                load_k("B0", 0)
            elif kind == "B0":
                load_k("B1", 1)

        c_loaded = [False]
        pend_ep = []
        LA = 3
        for g0 in range(LA):
            start_block(tiles[g0]["bi"])
            emit_qk(g0)
        for g in range(N):
            t = tiles[g]
            if t["first_blk"]:
                start_block(t["bi"] + 1)
                start_block(t["bi"] + 2)
                maybe_unit_loads(t["bi"])
            emit_exp(g)
            if g + LA < N:
                t2 = tiles[g + LA]
                if t2["kind"] == "C" and not c_loaded[0]:
                    load_k("C", 0)
                    c_loaded[0] = True
                start_block(t2["bi"])
                emit_qk(g + LA)
            emit_pv(g)
            emit_acc(g)
            for pe_ in list(pend_ep):
                if g >= pe_[0]:
                    emit_epilogue(pe_[1])
                    pend_ep.remove(pe_)
            if t["last_blk"]:
                pend_ep.append((g + 3, t["bi"]))
        for pe_ in pend_ep:
            emit_epilogue(pe_[1])
        S.run_block()


def colT(vec, nchunk):
    return np.ascontiguousarray(np.asarray(vec, np.float32).reshape(nchunk, 128).T)

def w_in_core(w_in_l, j):
    s = lambda a, n: w_in_l[:, a:a + n]
    qb0, kb0, vb0, qc0, kc0, vc0 = 576, 1088, 1600, 2112, 2624, 3136
    cols = [s(0, 512), s(512, 64), s(544, 32), s(512, 32),
            s(qb0 + 128 * j, 128), s(kb0 + 128 * j, 128), s(qc0 + 128 * j, 128), s(kc0 + 128 * j, 128),
            s(vb0 + 128 * j, 128), s(vc0 + 128 * j, 128)]
    return np.ascontiguousarray(np.concatenate(cols, axis=1))

def w_uq_core(w_uq_l, j):
    cols = []
    for hh in range(2):
        b = (2 * j + hh) * 192
        cols += [w_uq_l[:, b:b + 128], w_uq_l[:, b + 128:b + 192], w_uq_l[:, b + 160:b + 192], w_uq_l[:, b + 128:b + 160]]
    return np.ascontiguousarray(np.concatenate(cols, axis=1))

def w_ukv_core(w_ukv_l, j):
    b0, b1 = (2 * j) * 256, (2 * j + 1) * 256
    return np.ascontiguousarray(np.concatenate([w_ukv_l[:, b0:b0 + 128], w_ukv_l[:, b1:b1 + 128],
                                                w_ukv_l[:, b0 + 128:b0 + 256], w_ukv_l[:, b1 + 128:b1 + 256]], axis=1))

def rope_tables(seq=8192):
    half = 32
    inv = (np.float32(10000.0) ** (-np.arange(half, dtype=np.float32) / np.float32(half))).astype(np.float32)
    ang = (np.arange(seq, dtype=np.float32)[None, :] * inv[:, None]).astype(np.float32)
    c = np.cos(ang.astype(np.float64)).astype(np.float32)
    s = np.sin(ang.astype(np.float64)).astype(np.float32)
    return np.ascontiguousarray(np.concatenate([c, c], 0)), np.ascontiguousarray(np.concatenate([-s, s], 0))

NEGM = -30000.0

def mask_A():
    m = np.zeros((128, 128), np.float32)
    m[64:, :64] = NEGM
    return m.astype(ml_dtypes.bfloat16)

def corr_B(j):
    c = (2.0 ** (-2.0 * (j + 1))) * 8.0
    k = np.arange(128)[:, None]; q = np.arange(128)[None, :]
    m = np.where((k // 64 == q // 64) & (k > q), -2.0 * c * (k - q), 0.0).astype(np.float32)
    m[64:, :64] = NEGM
    return m.astype(ml_dtypes.bfloat16)

def mask_C():
    m = np.zeros((2, 128, 128), np.float32)
    m[0, 64:, :64] = NEGM
    m[1, :64, 64:] = NEGM
    return m

def bias_C(rel_bias_lh):
    k = np.arange(128)[:, None]; q = np.arange(128)[None, :]
    out = np.empty((5, 128, 128), np.float32)
    for d in range(5):
        idx = np.clip(128 * d + q - k, -63, 256) + 63
        out[d] = rel_bias_lh[idx]
    return out

def k_aug(seq=8192):
    p = np.arange(seq)
    return np.stack([p // 128, p % 128, np.ones(seq), np.ones(seq)]).astype(np.float32).astype(ml_dtypes.bfloat16)

def q_aug(j, seq=8192):
    c = (2.0 ** (-2.0 * (j + 1))) * 8.0
    p = np.arange(seq)
    return np.stack([np.full(seq, 128 * c), np.full(seq, c), -128 * c * (p // 128), -c * (p % 128)]).astype(np.float32).astype(ml_dtypes.bfloat16)


NCORES = 8
DEPTH = 2
FUSED = True
_PROG_CACHE = {}


def _lam_init(l):
    import math
    return 0.8 - 0.6 * math.exp(-0.3 * l)


def _di(nc, n, s, dt=F32):
    return nc.dram_tensor(n, list(s), dt, kind="ExternalInput").ap()


def _do(nc, n, s, dt=BF16):
    return nc.dram_tensor(n, list(s), dt, kind="ExternalOutput").ap()


def _dint(nc, n, s, dt=BF16):
    return nc.dram_tensor(n, list(s), dt, kind="Internal").ap()


def _decl_att_inputs(nc, pfx=""):
    d = {}
    d["w_in_c"] = _di(nc, pfx + "w_in_c", [D, WIN_C])
    d["qag"] = _di(nc, pfx + "qag", [128, 3])
    d["kvag"] = _di(nc, pfx + "kvag", [128, 1])
    d["w_uq_c"] = _di(nc, pfx + "w_uq_c", [384, 512])
    d["w_ukv_c"] = _di(nc, pfx + "w_ukv_c", [128, 512])
    d["biasC"] = _di(nc, pfx + "biasC", [5, 128, 128])
    d["lamv"] = _di(nc, pfx + "lamv", [4, 64])
    d["sublnT"] = _di(nc, pfx + "sublnT", [128, 1])
    return d


def _decl_consts(nc):
    d = {}
    d["rC"] = _di(nc, "rC", [64, SEQ])
    d["rS"] = _di(nc, "rS", [64, SEQ])
    d["kaug"] = _di(nc, "kaug", [4, SEQ], BF16)
    d["qaug"] = _di(nc, "qaug", [4, SEQ], BF16)
    d["maskA"] = _di(nc, "maskA", [128, 128], BF16)
    d["corrB"] = _di(nc, "corrB", [128, 128], BF16)
    d["maskC"] = _di(nc, "maskC", [2, 128, 128])
    return d


def _decl_mlp_inputs(nc, pfx=""):
    d = {}
    d["w_o_p"] = _di(nc, pfx + "w_o_p", [D, D])
    d["gT_mlp"] = _di(nc, pfx + "gT_mlp", [128, KC])
    d["w_up"] = _di(nc, pfx + "w_up", [D, DFF])
    d["w_down"] = _di(nc, pfx + "w_down", [DFF, D])
    return d


def build_fused():
    nc = bass.Bass("TRN2", target_bir_lowering=False, num_devices=NCORES)
    x = _di(nc, "x", [TOK, D])
    consts = _decl_consts(nc)
    gfin = _di(nc, "g_final", [D])
    y = _do(nc, "y", [TOK, D], F32)
    S = Sched(nc)
    S.alloc_sems()
    x_cur = x
    for l in range(DEPTH):
        pfx = "l%d_" % l
        gT = _di(nc, pfx + "gT_attn", [128, KC])
        a = _decl_att_inputs(nc, pfx)
        m = _decl_mlp_inputs(nc, pfx)
        hTc = [_dint(nc, pfx + "hTc%d" % c, [D, 256]) for c in range(8)]
        hTg = [_dint(nc, pfx + "hTg%d" % c, [4 * D, 256]) for c in range(8)]
        mixc = [_dint(nc, pfx + "mixc%d" % c, [64, SEQ]) for c in range(8)]
        mixg = [_dint(nc, pfx + "mixg%d" % c, [256, SEQ]) for c in range(8)]
        qT = _dint(nc, pfx + "qT_h", [QRH, SEQ])
        kT = _dint(nc, pfx + "kT_h", [KRH, SEQ])
        v = _dint(nc, pfx + "v_h", [SEQ, 512])
        x1 = _dint(nc, pfx + "x1", [TOK, D], F32)
        h2T = _dint(nc, pfx + "h2T", [D, TOK])
        final = (l == DEPTH - 1)
        x_next = y if final else _dint(nc, pfx + "xo", [TOK, D], F32)
        phase_P1(nc, S, x_cur, gT, hTc, hTg)
        phase_P2h(nc, S, hTg, a["w_in_c"], a["qag"], a["kvag"], a["w_uq_c"], a["w_ukv_c"], consts["rC"], consts["rS"], qT, kT, v)
        phase_ATT(nc, S, qT, kT, v, consts["kaug"], consts["qaug"], consts["maskA"], consts["corrB"], a["biasC"], consts["maskC"],
                  a["lamv"], a["sublnT"], _lam_init(l), mixc, mixg)
        phase_O1(nc, S, x_cur, mixg, m["w_o_p"], m["gT_mlp"], x1, h2T)
        phase_O2(nc, S, x1, h2T, m["w_up"], m["w_down"], x_next, g_final=gfin if final else None)
        x_cur = x_next
    return nc


def _prog(key, fn):
    if key not in _PROG_CACHE:
        _PROG_CACHE[key] = fn()
    return _PROG_CACHE[key]


def w_o_perm(w_o_l):
    def f(r, lr):
        if lr < 256:
            return 256 * r + lr
        if lr < 384:
            return 1024 + 128 * r + (lr - 256)
        return 1536 + 128 * r + (lr - 384)
    idx = [f(r, 64 * c + i) for c in range(8) for r in range(4) for i in range(64)]
    return np.ascontiguousarray(w_o_l[np.asarray(idx)])


def _f32(a):
    return np.ascontiguousarray(np.asarray(a, dtype=np.float32))


def _att_inputs(inp, l, j):
    return {
        "w_in_c": w_in_core(inp["w_in"][l], j),
        "qag": colT(inp["q_a_norm"][l], 3),
        "kvag": colT(inp["kv_a_norm"][l], 1),
        "w_uq_c": w_uq_core(inp["w_uq"][l], j),
        "w_ukv_c": w_ukv_core(inp["w_ukv"][l], j),
        "biasC": bias_C(inp["rel_bias"][l, j]),
        "lamv": np.ascontiguousarray(np.stack([inp["lambda_q1"][l], inp["lambda_k1"][l], inp["lambda_q2"][l], inp["lambda_k2"][l]])),
        "sublnT": np.ascontiguousarray(inp["diff_subln"][l].reshape(128, 1)),
    }


def _const_inputs(j):
    rC, rS = rope_tables()
    return {"rC": rC, "rS": rS, "kaug": k_aug(), "qaug": q_aug(j), "maskA": mask_A(), "corrB": corr_B(j), "maskC": mask_C()}


def _mlp_inputs(inp, l):
    return {"w_o_p": w_o_perm(inp["w_o"][l]), "gT_mlp": colT(inp["mlp_norm"][l], KC),
            "w_up": _f32(inp["w_up"][l]), "w_down": _f32(inp["w_down"][l])}


def kernel_fused(inp):
    cores = list(range(NCORES))
    x = inp["x"]
    nc = _prog("fused", build_fused)
    shared = {}
    for l in range(DEPTH):
        pfx = "l%d_" % l
        shared[pfx + "gT_attn"] = colT(inp["attn_norm"][l], KC)
        for k, v in _mlp_inputs(inp, l).items():
            shared[pfx + k] = v
    shared["g_final"] = _f32(inp["final_norm"])
    ins = []
    for c in cores:
        j = c % 4
        d = {"x": np.ascontiguousarray(x[c // 4, j * TOK:(j + 1) * TOK])}
        d.update(shared)
        d.update(_const_inputs(j))
        for l in range(DEPTH):
            for k, v in _att_inputs(inp, l, j).items():
                d["l%d_" % l + k] = v
        ins.append(d)
    res = run_bass_kernel_spmd(nc, ins, core_ids=cores)
    out = np.empty((2, SEQ, D), np.float32)
    for c in cores:
        out[c // 4, (c % 4) * TOK:(c % 4 + 1) * TOK] = res.results[c]["y"]
    return out


def kernel(**inputs):
    inp = {k: np.asarray(v) for k, v in inputs.items()}
    return kernel_fused(inp)
```

```python
import numpy as np
from contextlib import ExitStack
import ml_dtypes
import concourse.bass as bass
import concourse.mybir as mybir
from concourse.bass_utils import run_bass_kernel_spmd

F32 = mybir.dt.float32
BF16 = mybir.dt.bfloat16
AF = mybir.ActivationFunctionType
ALU = mybir.AluOpType
AX = mybir.AxisListType

RING = 8
FOLD_WAITS = True
_UIDC = [0]


def _uid(nc):
    _UIDC[0] += 1
    return "t%d_" % _UIDC[0]

ENGS = ("pe", "act", "dve", "pool", "sp")


class Ins:
    __slots__ = ("eng", "idx", "fn", "deps", "dma", "dma_n", "needs_inc", "semval", "cc", "lhs_deps")


class Sched:
    def __init__(self, nc):
        self.nc = nc
        self.progs = {e: [] for e in ENGS}
        self.res = {}
        self.ndma = {e: 0 for e in ENGS}
        self.emitted = {e: 0 for e in ENGS}
        self.waited = {e: {} for e in ENGS}
        self.cnt = {e: 0 for e in ENGS}
        self.sems = {}
        self.rings = {}
        self.cc_sems = []

    def alloc_sems(self):
        nc = self.nc
        for e in ENGS:
            self.sems[e] = nc.alloc_semaphore("c_" + e)
        for e in ("act", "pool", "sp"):
            self.rings[e] = [nc.alloc_semaphore("r_%s_%d" % (e, i)) for i in range(RING)]

    def add(self, eng, fn, reads=(), writes=(), dma=False, cc=False, rhs=()):
        if cc:
            dma = True
        progs = self.progs[eng]
        idx = len(progs)
        deps = {}
        lhs_deps = set()
        for k in reads:
            st = self.res.get(k)
            if st is not None and st[0] is not None:
                deps[st[0]] = True
                if k not in rhs:
                    lhs_deps.add(st[0])
        for k in writes:
            st = self.res.get(k)
            if st is not None:
                if st[0] is not None:
                    deps.setdefault(st[0], False)
                for re_, ri in st[1].items():
                    deps.setdefault((re_, ri), False)
                for r in st[2]:
                    deps.setdefault(r, False)
        ins = Ins()
        ins.eng = eng
        ins.idx = idx
        ins.fn = fn
        ins.dma = dma
        ins.needs_inc = False
        ins.semval = None
        ins.dma_n = -1
        ins.cc = cc
        if cc:
            ins.semval = (self.nc.alloc_semaphore("cc_%d" % len(self.cc_sems)), 1)
            self.cc_sems.append(ins.semval[0])
        elif dma:
            ins.dma_n = self.ndma[eng]
            self.ndma[eng] += 1
        fd = []
        for (pe_, pi), raw in deps.items():
            p = self.progs[pe_][pi]
            if pe_ == eng and not p.dma and not dma:
                if eng == "pe" or not raw:
                    continue
            fd.append((pe_, pi))
            if not p.dma:
                p.needs_inc = True
        ins.deps = fd
        ins.lhs_deps = lhs_deps
        progs.append(ins)
        for k in reads:
            st = self.res.get(k)
            if st is None:
                st = [None, {}, []]
                self.res[k] = st
            if dma:
                st[2].append((eng, idx))
            else:
                st[1][eng] = idx
        for k in writes:
            self.res[k] = [(eng, idx), {}, []]
        return ins

    def _semval(self, ins):
        return ins.semval

    def emit_engine(self, eng, e):
        progs = self.progs[eng]
        waited = self.waited[eng]
        start = self.emitted[eng]
        c = self.cnt[eng]
        for ins in progs[start:]:
            if ins.cc:
                pass
            elif ins.dma:
                ins.semval = (self.rings[eng][ins.dma_n % RING], 16 * (ins.dma_n // RING + 1))
            else:
                if ins.needs_inc:
                    c += 1
                ins.semval = (self.sems[eng], c)
        self.cnt[eng] = c

        for ins in progs[start:]:
            pend = {}
            lhsv = {}
            if ins.dma and not ins.cc and ins.dma_n >= RING:
                sem = self.rings[eng][ins.dma_n % RING]
                pend[id(sem)] = (sem, 16 * (ins.dma_n // RING))
            for (pe_, pi) in ins.deps:
                p = self.progs[pe_][pi]
                assert p.semval is not None, (eng, ins.idx, pe_, pi)
                k = id(p.semval[0])
                if k not in pend or pend[k][1] < p.semval[1]:
                    pend[k] = p.semval
                if eng == "pe" and (pe_, pi) in ins.lhs_deps:
                    lhsv[k] = max(lhsv.get(k, 0), p.semval[1])
            todo = []
            foldable = []
            for k, (sem, val) in pend.items():
                w0 = waited.get(k, 0)
                if w0 < val:
                    if eng == "pe" and lhsv.get(k, 0) > w0:
                        todo.append((sem, val))
                    else:
                        foldable.append((sem, val))
                    waited[k] = val
            fold = None
            if foldable and FOLD_WAITS and not ins.dma:
                fold = foldable.pop()
            todo += foldable
            for sem, val in todo:
                e.wait_ge(sem, val)
            bi = ins.fn(e)
            if fold is not None:
                bi._wait_ge(fold[0], fold[1])
            if ins.cc:
                bi.then_inc(ins.semval[0], 1)
            elif ins.dma:
                bi.then_inc(ins.semval[0], 16)
            elif ins.needs_inc:
                bi.then_inc(ins.semval[0], 1)
        self.emitted[eng] = len(progs)

    def drain_dma(self, e, eng_name="sp", final=False):
        waited = self.waited[eng_name]
        for q, ring in self.rings.items():
            n = self.ndma[q]
            for slot in range(RING):
                cnt = (n - slot + RING - 1) // RING if n > slot else 0
                if cnt > 0:
                    k = id(ring[slot])
                    if waited.get(k, 0) < 16 * cnt:
                        e.wait_ge(ring[slot], 16 * cnt)
                        waited[k] = 16 * cnt
        if final:
            for sem in self.cc_sems:
                k = id(sem)
                if waited.get(k, 0) < 1:
                    e.wait_ge(sem, 1)
                    waited[k] = 1

    def run_block(self, final=False):
        nc = self.nc
        for eng in ENGS:
            progs = self.progs[eng]
            c = self.cnt[eng]
            for ins in progs[self.emitted[eng]:]:
                if ins.cc:
                    pass
                elif ins.dma:
                    ins.semval = (self.rings[eng][ins.dma_n % RING], 16 * (ins.dma_n // RING + 1))
                else:
                    if ins.needs_inc:
                        c += 1
                    ins.semval = (self.sems[eng], c)
        with nc.Block() as block:
            @block.tensor
            def _(e):
                self.emit_engine("pe", e)

            @block.scalar
            def _(e):
                self.emit_engine("act", e)

            @block.vector
            def _(e):
                self.emit_engine("dve", e)

            @block.gpsimd
            def _(e):
                self.emit_engine("pool", e)

            @block.sync
            def _(e):
                self.emit_engine("sp", e)
                self.drain_dma(e, "sp", final)
        keep = {}
        for k, st in self.res.items():
            w = st[0]
            if w is not None and self.progs[w[0]][w[1]].cc:
                keep[k] = [w, {}, []]
        self.res.clear()
        self.res.update(keep)
        nc.all_engine_barrier()


D = 2048
TOK = 2048
NTT = TOK // 128
KC = D // 128
DFF = 8192
EPS = 1e-6


def make_ident(S, ident_bf, ident_f):
    S.add("pool", lambda e: e.memset(ident_f[:], 0.0), writes=["ident_f"])
    S.add("pool", lambda e: e.affine_select(out=ident_f[:], in_=ident_f[:], pattern=[[-1, 128]],
                                            compare_op=ALU.not_equal, fill=1.0, base=0,
                                            channel_multiplier=1),
          reads=["ident_f"], writes=["ident_f"])
    S.add("dve", lambda e: e.tensor_copy(out=ident_bf[:], in_=ident_f[:]), reads=["ident_f"], writes=["ident"])


def rms_rstd(S, eng_sq, src_ap, junk_ap, ss_ap, rstd_ap, n, rkeys, tag):
    S.add("act", lambda e: e.activation(out=junk_ap, in_=src_ap, func=AF.Square, accum_out=ss_ap),
          reads=rkeys, writes=[tag + "_junk", tag + "_ss"])
    S.add("dve", lambda e: e.tensor_scalar(out=rstd_ap, in0=ss_ap, scalar1=1.0 / n, scalar2=EPS,
                                           op0=ALU.mult, op1=ALU.add),
          reads=[tag + "_ss"], writes=[tag + "_rstd"])
    S.add("act", lambda e: e.activation(out=rstd_ap, in_=rstd_ap, func=AF.Sqrt),
          reads=[tag + "_rstd"], writes=[tag + "_rstd"])
    S.add("dve", lambda e: e.reciprocal(out=rstd_ap, in_=rstd_ap),
          reads=[tag + "_rstd"], writes=[tag + "_rstd"])


def phase_O1(nc, S, x_in, mixT, w_o, g_mlp, x1_out, h2T_out):
    with ExitStack() as es:
        mixT_sb = es.enter_context(nc.sbuf_tensor(_uid(nc) + "o1_mixT", [128, KC, TOK], BF16))
        wo_sb = es.enter_context(nc.sbuf_tensor(_uid(nc) + "o1_wo", [128, KC, D], BF16))
        xt = es.enter_context(nc.sbuf_tensor(_uid(nc) + "o1_xt", [128, 3, D], F32))
        xn = es.enter_context(nc.sbuf_tensor(_uid(nc) + "o1_xn", [128, 2, D], BF16))
        junk = es.enter_context(nc.sbuf_tensor(_uid(nc) + "o1_junk", [128, D], BF16))
        hT = es.enter_context(nc.sbuf_tensor(_uid(nc) + "o1_hT", [128, 2, KC, 128], BF16))
        gT = es.enter_context(nc.sbuf_tensor(_uid(nc) + "o1_gT", [128, KC], F32))
        st = es.enter_context(nc.sbuf_tensor(_uid(nc) + "o1_st", [128, 8], F32))
        ident_f = es.enter_context(nc.sbuf_tensor(_uid(nc) + "o1_identf", [128, 128], F32))
        ident = es.enter_context(nc.sbuf_tensor(_uid(nc) + "o1_ident", [128, 128], BF16))
        pm = es.enter_context(nc.psum_tensor(_uid(nc) + "o1_pm", [128, 4, 512], F32))
        pt = es.enter_context(nc.psum_tensor(_uid(nc) + "o1_pt", [128, 2, 8, 128], BF16))
        make_ident(S, ident, ident_f)
        S.add("sp", lambda e: e.dma_start(out=gT[:], in_=g_mlp),
              writes=["gT"], dma=True)
        dyn = {}

        def ld_mix(e, kc):
            if "off" not in dyn:
                dyn["off"] = e.snap((nc.partition_id([mybir.EngineType.SP]) % 4) * TOK, min_val=0, max_val=3 * TOK)
            return e.dma_start(out=mixT_sb[:, kc, :], in_=mixT[kc // 2][(kc % 2) * 128:(kc % 2 + 1) * 128, bass.ds(dyn["off"], TOK)])

        for q in range(4):
            for k4 in range(4):
                S.add("sp", lambda e, kc=q * 4 + k4: ld_mix(e, kc), reads=[("mixg", id(mixT[(q * 4 + k4) // 2]))], writes=[("mixT", q * 4 + k4)], dma=True)
            S.add("pool", lambda e, q=q: e.dma_start(out=wo_sb[:, q * 4:(q + 1) * 4, :],
                                                     in_=w_o[q * 512:(q + 1) * 512, :].rearrange("(kc p) n -> p kc n", p=128)),
                  writes=[("wo", q)], dma=True)
        def part1(tt):
            b = tt % 3
            S.add("sp", lambda e, tt=tt, b=b: e.dma_start(out=xt[:, b, :], in_=x_in[tt * 128:(tt + 1) * 128, :]),
                  writes=[("xt", b)], dma=True)
            for cg in range(4):
                for kc in range(KC):
                    S.add("pe", lambda e, tt=tt, cg=cg, kc=kc: e.matmul(
                        pm[:, cg, :], lhsT=mixT_sb[:, kc, tt * 128:(tt + 1) * 128],
                        rhs=wo_sb[:, kc, cg * 512:(cg + 1) * 512], start=(kc == 0), stop=(kc == KC - 1)),
                        reads=[("mixT", kc), ("wo", kc // 4)], writes=[("pm", cg)], rhs=[("wo", kc // 4)])
                S.add("dve", lambda e, b=b, cg=cg: e.tensor_tensor(
                    out=xt[:, b, cg * 512:(cg + 1) * 512], in0=pm[:, cg, :], in1=xt[:, b, cg * 512:(cg + 1) * 512], op=ALU.add),
                    reads=[("pm", cg), ("xt", b)], writes=[("xt", b)])
            S.add("sp", lambda e, tt=tt, b=b: e.dma_start(out=x1_out[tt * 128:(tt + 1) * 128, :], in_=xt[:, b, :]),
                  reads=[("xt", b)], dma=True)

        def part2(tt):
            b = tt % 3
            h = tt % 2
            rms_rstd(S, "act", xt[:, b, :], junk[:], st[:, h:h + 1], st[:, 2 + h:3 + h], D, [("xt", b)], "o1n%d" % h)
            S.add("act", lambda e, b=b, h=h: e.activation(out=xn[:, h, :], in_=xt[:, b, :], func=AF.Copy, scale=st[:, 2 + h:3 + h]),
                  reads=[("xt", b), "o1n%d_rstd" % h], writes=[("xn", h)])
            for hg in range(2):
                for j in range(8):
                    kc = hg * 8 + j
                    S.add("pe", lambda e, h=h, hg=hg, j=j, kc=kc: e.transpose(
                        out=pt[:, hg, j, :], in_=xn[:, h, kc * 128:(kc + 1) * 128], identity=ident[:]),
                        reads=[("xn", h), "ident"], writes=[("pt", hg)])
                for j in range(8):
                    kc = hg * 8 + j
                    if j % 2 == 0:
                        S.add("act", lambda e, h=h, hg=hg, j=j, kc=kc: e.activation(
                            out=hT[:, h, kc, :], in_=pt[:, hg, j, :], func=AF.Copy, scale=gT[:, kc:kc + 1]),
                            reads=[("pt", hg), "gT"], writes=[("hT", h)])
                    else:
                        S.add("dve", lambda e, h=h, hg=hg, j=j, kc=kc: e.tensor_scalar(
                            out=hT[:, h, kc, :], in0=pt[:, hg, j, :], scalar1=gT[:, kc:kc + 1], scalar2=None, op0=ALU.mult),
                            reads=[("pt", hg), "gT"], writes=[("hT", h)])
            S.add("sp", lambda e, tt=tt, h=h: e.dma_start(
                out=h2T_out[:, tt * 128:(tt + 1) * 128].rearrange("(kc p) t -> p kc t", p=128), in_=hT[:, h, :, :]),
                reads=[("hT", h)], dma=True)

        part1(0)
        for tt in range(1, NTT):
            part1(tt)
            part2(tt - 1)
        part2(NTT - 1)
        S.run_block()


def phase_O2(nc, S, x1_in, h2T, w_up, w_down, x_out, g_final=None):
    HT = TOK // 2
    NFB = DFF // 512
    with ExitStack() as es:
        x1 = es.enter_context(nc.sbuf_tensor(_uid(nc) + "o2_x1", [128, HT // 128, D], F32))
        hT = es.enter_context(nc.sbuf_tensor(_uid(nc) + "o2_hT", [128, KC, HT], BF16))
        wu = es.enter_context(nc.sbuf_tensor(_uid(nc) + "o2_wu", [128, 2, KC, 512], BF16))
        wd = es.enter_context(nc.sbuf_tensor(_uid(nc) + "o2_wd", [128, 2, 4, D], BF16))
        aT = es.enter_context(nc.sbuf_tensor(_uid(nc) + "o2_aT", [128, 2, 4, HT], BF16))
        gB = es.enter_context(nc.sbuf_tensor(_uid(nc) + "o2_gB", [128, D], F32))
        junk = es.enter_context(nc.sbuf_tensor(_uid(nc) + "o2_junk", [128, D], BF16))
        st = es.enter_context(nc.sbuf_tensor(_uid(nc) + "o2_st", [128, 4], F32))
        pu = es.enter_context(nc.psum_tensor(_uid(nc) + "o2_pu", [128, 2, 512], F32))
        pd = es.enter_context(nc.psum_tensor(_uid(nc) + "o2_pd", [128, 4, 512], F32))
        if g_final is not None:
            S.add("sp", lambda e: e.dma_start(out=gB[:], in_=g_final.partition_broadcast(128)), writes=["gB"], dma=True)
        for hf in range(2):
            t0 = hf * HT
            for q in range(4):
                S.add("sp", lambda e, q=q, t0=t0: e.dma_start(
                    out=x1[:, 2 * q:2 * q + 2, :],
                    in_=x1_in[t0 + q * 256:t0 + (q + 1) * 256, :].rearrange("(t p) d -> p t d", p=128)),
                    writes=[("x1", 2 * q), ("x1", 2 * q + 1)], dma=True)
                S.add("sp", lambda e, q=q, t0=t0: e.dma_start(
                    out=hT[:, q * 4:(q + 1) * 4, :],
                    in_=h2T[q * 512:(q + 1) * 512, t0:t0 + HT].rearrange("(kc p) t -> p kc t", p=128)),
                    writes=[("hT", q)], dma=True)

            def load_w(fb):
                b = fb % 2
                for q in range(2):
                    S.add("pool", lambda e, fb=fb, b=b, q=q: e.dma_start(
                        out=wu[:, b, q * 8:(q + 1) * 8, :],
                        in_=w_up[q * 1024:(q + 1) * 1024, fb * 512:(fb + 1) * 512].rearrange("(kc p) n -> p kc n", p=128)),
                        writes=[("wu", b, q)], dma=True)
                S.add("pool", lambda e, fb=fb, b=b: e.dma_start(
                    out=wd[:, b, :, :],
                    in_=w_down[fb * 512:(fb + 1) * 512, :].rearrange("(fc p) n -> p fc n", p=128)),
                    writes=[("wd", b)], dma=True)

            def up(fb):
                b = fb % 2
                n = 0
                for fc in range(4):
                    for tg in range(HT // 512):
                        slot = n % 2
                        n += 1
                        for kc in range(KC):
                            S.add("pe", lambda e, b=b, fc=fc, tg=tg, kc=kc, slot=slot: e.matmul(
                                pu[:, slot, :], lhsT=wu[:, b, kc, fc * 128:(fc + 1) * 128],
                                rhs=hT[:, kc, tg * 512:(tg + 1) * 512], start=(kc == 0), stop=(kc == KC - 1)),
                                reads=[("wu", b, kc // 8), ("hT", kc // 4)], writes=[("pu", slot)], rhs=[("hT", kc // 4)])
                        S.add("act", lambda e, b=b, fc=fc, tg=tg, slot=slot: e.activation(
                            out=aT[:, b, fc, tg * 512:(tg + 1) * 512], in_=pu[:, slot, :], func=AF.Relu),
                            reads=[("pu", slot)], writes=[("aT", b, fc)])
                        S.add("pool", lambda e, b=b, fc=fc, tg=tg: e.tensor_tensor(
                            out=aT[:, b, fc, tg * 512:(tg + 1) * 512], in0=aT[:, b, fc, tg * 512:(tg + 1) * 512],
                            in1=aT[:, b, fc, tg * 512:(tg + 1) * 512], op=ALU.mult),
                            reads=[("aT", b, fc)], writes=[("aT", b, fc)])

            def down(fb):
                b = fb % 2
                n = 0
                for tt in range(HT // 128):
                    for cg in range(4):
                        slot = n % 4
                        n += 1
                        for fc in range(4):
                            S.add("pe", lambda e, b=b, tt=tt, cg=cg, fc=fc, slot=slot: e.matmul(
                                pd[:, slot, :], lhsT=aT[:, b, fc, tt * 128:(tt + 1) * 128],
                                rhs=wd[:, b, fc, cg * 512:(cg + 1) * 512], start=(fc == 0), stop=(fc == 3)),
                                reads=[("aT", b, fc), ("wd", b)], writes=[("pd", slot)], rhs=[("wd", b)])
                        S.add("dve", lambda e, tt=tt, cg=cg, slot=slot: e.tensor_tensor(
                            out=x1[:, tt, cg * 512:(cg + 1) * 512], in0=pd[:, slot, :],
                            in1=x1[:, tt, cg * 512:(cg + 1) * 512], op=ALU.add),
                            reads=[("pd", slot), ("x1", tt)], writes=[("x1", tt)])

            load_w(0)
            up(0)
            for fb in range(NFB):
                if fb + 1 < NFB:
                    load_w(fb + 1)
                    up(fb + 1)
                down(fb)
            for tt in range(HT // 128):
                if g_final is not None:
                    b = tt % 2
                    rms_rstd(S, "act", x1[:, tt, :], junk[:], st[:, b:b + 1], st[:, 2 + b:3 + b], D, [("x1", tt)], "o2n%d" % b)
                    S.add("dve", lambda e, tt=tt, b=b: e.scalar_tensor_tensor(
                        out=x1[:, tt, :], in0=x1[:, tt, :], scalar=st[:, 2 + b:3 + b], in1=gB[:], op0=ALU.mult, op1=ALU.mult),
                        reads=[("x1", tt), "o2n%d_rstd" % b, "gB"], writes=[("x1", tt)])
                S.add("sp", lambda e, tt=tt, t0=t0: e.dma_start(out=x_out[t0 + tt * 128:t0 + (tt + 1) * 128, :], in_=x1[:, tt, :]),
                      reads=[("x1", tt)], dma=True)
        S.run_block()


SEQ = 8192
WIN_C = 1408
QRH = 640
KRH = 576
QA = (0, 192)
QB_ROW = 384
QC_ROW = 512
KROPE_ROW = 256
KB_ROW = 320
KC_ROW = 448


def add_ag(S, src, dst, rkeys, wkeys):
    S.add("pool", lambda e: e.collective_compute(
        "AllGather", ALU.bypass, replica_groups=[[0, 1, 2, 3], [4, 5, 6, 7]], ins=[src], outs=[dst]),
        reads=rkeys, writes=wkeys, cc=True)


def phase_P1(nc, S, x_in, gT_attn, hT_out, hT_gath):
    with ExitStack() as es:
        xt = es.enter_context(nc.sbuf_tensor(_uid(nc) + "p1_xt", [128, NTT, D], F32))
        xn = es.enter_context(nc.sbuf_tensor(_uid(nc) + "p1_xn", [128, 2, D], BF16))
        junk = es.enter_context(nc.sbuf_tensor(_uid(nc) + "p1_junk", [128, D], BF16))
        hT = es.enter_context(nc.sbuf_tensor(_uid(nc) + "p1_hT", [128, 2, KC, 256], BF16))
        gT = es.enter_context(nc.sbuf_tensor(_uid(nc) + "p1_gT", [128, KC], F32))
        ss = es.enter_context(nc.sbuf_tensor(_uid(nc) + "p1_ss", [128, NTT], F32))
        rs = es.enter_context(nc.sbuf_tensor(_uid(nc) + "p1_rs", [128, NTT], F32))
        ident_f = es.enter_context(nc.sbuf_tensor(_uid(nc) + "p1_identf", [128, 128], F32))
        ident = es.enter_context(nc.sbuf_tensor(_uid(nc) + "p1_ident", [128, 128], BF16))
        pt = es.enter_context(nc.psum_tensor(_uid(nc) + "p1_pt", [128, 2, 8, 128], BF16))
        make_ident(S, ident, ident_f)
        S.add("sp", lambda e: e.dma_start(out=gT[:], in_=gT_attn), writes=["gT"], dma=True)
        for tt in range(NTT):
            S.add("sp", lambda e, tt=tt: e.dma_start(out=xt[:, tt, :], in_=x_in[tt * 128:(tt + 1) * 128, :]),
                  writes=[("xt", tt)], dma=True)
        for tt in range(NTT):
            S.add("act", lambda e, tt=tt: e.activation(out=junk[:], in_=xt[:, tt, :], func=AF.Square, accum_out=ss[:, tt:tt + 1]),
                  reads=[("xt", tt)], writes=["junk", ("ss", tt)])
        S.add("dve", lambda e: e.tensor_scalar(out=rs[:], in0=ss[:], scalar1=1.0 / D, scalar2=EPS, op0=ALU.mult, op1=ALU.add),
              reads=[("ss", tt) for tt in range(NTT)], writes=["rs"])
        S.add("act", lambda e: e.activation(out=rs[:], in_=rs[:], func=AF.Sqrt), reads=["rs"], writes=["rs"])
        S.add("dve", lambda e: e.reciprocal(out=rs[:], in_=rs[:]), reads=["rs"], writes=["rs"])
        for tt in range(NTT):
            b = tt % 2
            cb = (tt // 2) % 2
            co = (tt % 2) * 128
            S.add("act", lambda e, tt=tt, b=b: e.activation(out=xn[:, b, :], in_=xt[:, tt, :], func=AF.Copy, scale=rs[:, tt:tt + 1]),
                  reads=[("xt", tt), "rs"], writes=[("xn", b)])
            for hg in range(2):
                for j in range(8):
                    kc = hg * 8 + j
                    S.add("pe", lambda e, b=b, hg=hg, j=j, kc=kc: e.transpose(
                        out=pt[:, hg, j, :], in_=xn[:, b, kc * 128:(kc + 1) * 128], identity=ident[:]),
                        reads=[("xn", b), "ident"], writes=[("pt", hg)])
                for j in range(8):
                    kc = hg * 8 + j
                    if j % 2 == 0:
                        S.add("act", lambda e, cb=cb, co=co, hg=hg, j=j, kc=kc: e.activation(
                            out=hT[:, cb, kc, co:co + 128], in_=pt[:, hg, j, :], func=AF.Copy, scale=gT[:, kc:kc + 1]),
                            reads=[("pt", hg), "gT"], writes=[("hT", cb, tt % 2)])
                    else:
                        S.add("dve", lambda e, cb=cb, co=co, hg=hg, j=j, kc=kc: e.tensor_scalar(
                            out=hT[:, cb, kc, co:co + 128], in0=pt[:, hg, j, :], scalar1=gT[:, kc:kc + 1], scalar2=None, op0=ALU.mult),
                            reads=[("pt", hg), "gT"], writes=[("hT", cb, tt % 2)])
            if tt % 2 == 1:
                ch = tt // 2
                S.add("sp", lambda e, ch=ch, cb=cb: e.dma_start(
                    out=hT_out[ch].rearrange("(kc p) t -> p kc t", p=128), in_=hT[:, cb, :, :]),
                    reads=[("hT", cb, 0), ("hT", cb, 1)], writes=[("hTc", ch)], dma=True)
                add_ag(S, hT_out[ch], hT_gath[ch], [("hTc", ch)], [("hTg", id(hT_gath[ch]))])
        S.run_block()


def phase_P2h(nc, S, hT_all, w_in_c, qagT, kvagT, w_uq_c, w_ukv_c, ropeC, ropeS, qT_out, kT_out, v_out):
    GT = 1024
    NG = SEQ // GT
    with ExitStack() as es:
        hT = es.enter_context(nc.sbuf_tensor(_uid(nc) + "p2_hT", [128, 2, KC, GT], BF16))
        wb = es.enter_context(nc.sbuf_tensor(_uid(nc) + "p2_w", [128, KC, WIN_C], BF16))
        wuq = es.enter_context(nc.sbuf_tensor(_uid(nc) + "p2_wuq", [128, 3, 512], BF16))
        wukv = es.enter_context(nc.sbuf_tensor(_uid(nc) + "p2_wukv", [128, 512], BF16))
        cnT = es.enter_context(nc.sbuf_tensor(_uid(nc) + "p2_cnT", [128, 2, 4, GT], BF16))
        junk = es.enter_context(nc.sbuf_tensor(_uid(nc) + "p2_junk", [128, 512], BF16))
        cq = es.enter_context(nc.sbuf_tensor(_uid(nc) + "p2_cq", [128, 2, 512], F32))
        cn = es.enter_context(nc.sbuf_tensor(_uid(nc) + "p2_cn", [128, 2, 512], BF16))
        stage = es.enter_context(nc.sbuf_tensor(_uid(nc) + "p2_stage", [128, 4, 512], BF16))
        rt = es.enter_context(nc.sbuf_tensor(_uid(nc) + "p2_rt", [64, 2, 2, 512], F32))
        rC = es.enter_context(nc.sbuf_tensor(_uid(nc) + "p2_rC", [64, 2, GT], F32))
        rS = es.enter_context(nc.sbuf_tensor(_uid(nc) + "p2_rS", [64, 2, GT], F32))
        qag = es.enter_context(nc.sbuf_tensor(_uid(nc) + "p2_qag", [128, 4], F32))
        st = es.enter_context(nc.sbuf_tensor(_uid(nc) + "p2_st", [128, 16], F32))
        ident_f = es.enter_context(nc.sbuf_tensor(_uid(nc) + "p2_identf", [128, 128], F32))
        ident = es.enter_context(nc.sbuf_tensor(_uid(nc) + "p2_ident", [128, 128], BF16))
        pm = es.enter_context(nc.psum_tensor(_uid(nc) + "p2_pm", [128, 4, 512], F32))
        pt = es.enter_context(nc.psum_tensor(_uid(nc) + "p2_pt", [128, 2, 4, 128], BF16))
        make_ident(S, ident, ident_f)
        S.add("sp", lambda e: e.dma_start(out=qag[:, 0:3], in_=qagT), writes=["qag"], dma=True)
        S.add("sp", lambda e: e.dma_start(out=qag[:, 3:4], in_=kvagT), reads=["qag"], writes=["qag"], dma=True)
        for q in range(4):
            S.add("pool", lambda e, q=q: e.dma_start(
                out=wb[:, q * 4:(q + 1) * 4, :], in_=w_in_c[q * 512:(q + 1) * 512, :].rearrange("(kc p) n -> p kc n", p=128)),
                writes=[("wb", q)], dma=True)
        S.add("pool", lambda e: e.dma_start(out=wuq[:], in_=w_uq_c.rearrange("(kc p) n -> p kc n", p=128)), writes=["wuq"], dma=True)
        S.add("pool", lambda e: e.dma_start(out=wukv[:], in_=w_ukv_c), writes=["wukv"], dma=True)
        wb_all = [("wb", q) for q in range(4)]

        cnt = {"pm": 0, "st": 0, "ev": 0}

        def pm_slot():
            s = cnt["pm"] % 4
            cnt["pm"] += 1
            return s

        def evac_store(ps_ap, dram_ap, nrows, ncols, slot_pm):
            ss = cnt["st"] % 4
            cnt["st"] += 1
            eng = "act" if cnt["ev"] % 2 == 0 else "dve"
            cnt["ev"] += 1
            if eng == "act":
                S.add("act", lambda e, ss=ss: e.activation(out=stage[0:nrows, ss, 0:ncols], in_=ps_ap, func=AF.Copy),
                      reads=[("pm", slot_pm)], writes=[("stage", ss)])
            else:
                S.add("dve", lambda e, ss=ss: e.tensor_copy(out=stage[0:nrows, ss, 0:ncols], in_=ps_ap),
                      reads=[("pm", slot_pm)], writes=[("stage", ss)])
            S.add("sp", lambda e, ss=ss: e.dma_start(out=dram_ap, in_=stage[0:nrows, ss, 0:ncols]),
                  reads=[("stage", ss)], dma=True)

        def rope_store(psA, psB, slotA, slotB, hb, tg, dram_ap):
            r = cnt["st"] % 2
            ss = cnt["st"] % 4
            cnt["st"] += 1
            S.add("dve", lambda e, r=r: e.tensor_tensor(out=rt[:, r, 0, :], in0=psA, in1=rC[:, hb, tg * 512:(tg + 1) * 512], op=ALU.mult),
                  reads=[("pm", slotA), ("rC", hb)], writes=[("rt", r, 0)])
            S.add("dve", lambda e, r=r: e.tensor_tensor(out=rt[:, r, 1, :], in0=psB, in1=rS[:, hb, tg * 512:(tg + 1) * 512], op=ALU.mult),
                  reads=[("pm", slotB), ("rS", hb)], writes=[("rt", r, 1)])
            S.add("pool", lambda e, r=r, ss=ss: e.tensor_tensor(out=stage[0:64, ss, :], in0=rt[:, r, 0, :], in1=rt[:, r, 1, :], op=ALU.add),
                  reads=[("rt", r, 0), ("rt", r, 1)], writes=[("stage", ss)])
            S.add("sp", lambda e, ss=ss: e.dma_start(out=dram_ap, in_=stage[0:64, ss, :]),
                  reads=[("stage", ss)], dma=True)

        def load_group(g, hb):
            r, half = divmod(g, 2)
            for c4 in range(4):
                ch = 4 * half + c4
                for q in range(4):
                    S.add("sp", lambda e, hb=hb, r=r, ch=ch, c4=c4, q=q: e.dma_start(
                        out=hT[:, hb, q * 4:(q + 1) * 4, c4 * 256:(c4 + 1) * 256],
                        in_=hT_all[ch][r * D + q * 512:r * D + (q + 1) * 512, :].rearrange("(kc p) t -> p kc t", p=128)),
                        reads=[("hTg", id(hT_all[ch]))], writes=[("hT", hb, q, c4)], dma=True)
            S.add("sp", lambda e, hb=hb, g=g: e.dma_start(out=rC[:, hb, :], in_=ropeC[:, g * GT:(g + 1) * GT]), writes=[("rC", hb)], dma=True)
            S.add("sp", lambda e, hb=hb, g=g: e.dma_start(out=rS[:, hb, :], in_=ropeS[:, g * GT:(g + 1) * GT]), writes=[("rS", hb)], dma=True)

        gorder = [0, 2, 4, 6, 1, 3, 5, 7]
        load_group(gorder[0], 0)
        for gi, g in enumerate(gorder):
            hb = gi % 2
            t0 = g * GT
            if gi + 1 < NG:
                load_group(gorder[gi + 1], (gi + 1) % 2)
            hk = [("hT", hb, q) for q in range(4)]
            def g0_part1(tt):
                b = tt % 2
                sl = pm_slot()
                for kc in range(KC):
                    S.add("pe", lambda e, hb=hb, tt=tt, kc=kc, sl=sl: e.matmul(
                        pm[:, sl, :], lhsT=hT[:, hb, kc, tt * 128:(tt + 1) * 128], rhs=wb[:, kc, 0:512],
                        start=(kc == 0), stop=(kc == KC - 1)),
                        reads=[("hT", hb, kc // 4, x) for x in range(4)] + [("wb", kc // 4)], writes=[("pm", sl)], rhs=[("wb", kc // 4)])
                S.add("dve", lambda e, b=b, sl=sl: e.tensor_copy(out=cq[:, b, :], in_=pm[:, sl, :]),
                      reads=[("pm", sl)], writes=[("cq", b)])

            def g0_part2(tt):
                b = tt % 2
                rms_rstd(S, "act", cq[:, b, 0:384], junk[:, 0:384], st[:, 4 + b:5 + b], st[:, 6 + b:7 + b], 384, [("cq", b)], "pq%d" % b)
                rms_rstd(S, "act", cq[:, b, 384:512], junk[:, 384:512], st[:, 8 + b:9 + b], st[:, 10 + b:11 + b], 128, [("cq", b)], "pk%d" % b)
                S.add("act", lambda e, b=b: e.activation(out=cn[:, b, 0:384], in_=cq[:, b, 0:384], func=AF.Copy, scale=st[:, 6 + b:7 + b]),
                      reads=[("cq", b), "pq%d_rstd" % b], writes=[("cn", b, 0)])
                S.add("act", lambda e, b=b: e.activation(out=cn[:, b, 384:512], in_=cq[:, b, 384:512], func=AF.Copy, scale=st[:, 10 + b:11 + b]),
                      reads=[("cq", b), "pk%d_rstd" % b], writes=[("cn", b, 1)])
                hg = tt % 2
                for j in range(4):
                    S.add("pe", lambda e, b=b, hg=hg, j=j: e.transpose(
                        out=pt[:, hg, j, :], in_=cn[:, b, j * 128:(j + 1) * 128], identity=ident[:]),
                        reads=[("cn", b, 0), ("cn", b, 1), "ident"], writes=[("pt", hg)])
                for j in range(4):
                    S.add("act", lambda e, hb=hb, tt=tt, hg=hg, j=j: e.activation(
                        out=cnT[:, hb, j, tt * 128:(tt + 1) * 128], in_=pt[:, hg, j, :], func=AF.Copy, scale=qag[:, j:j + 1]),
                        reads=[("pt", hg), "qag"], writes=[("cnT", hb, tt)])

            ntt = GT // 128
            g0_part1(0)
            for tt in range(1, ntt):
                g0_part1(tt)
                g0_part2(tt - 1)
            g0_part2(ntt - 1)
            ck = [("cnT", hb, tt) for tt in range(GT // 128)]
            for tg in range(GT // 512):
                c0t = t0 + tg * 512
                slA = pm_slot()
                slB = pm_slot()
                for v_, sl in ((0, slA), (1, slB)):
                    for kc in range(KC):
                        S.add("pe", lambda e, hb=hb, tg=tg, kc=kc, sl=sl, v_=v_: e.matmul(
                            pm[0:64, sl, :], lhsT=wb[:, kc, 512 + 64 * v_:576 + 64 * v_], rhs=hT[:, hb, kc, tg * 512:(tg + 1) * 512],
                            start=(kc == 0), stop=(kc == KC - 1)),
                            reads=[("hT", hb, kc // 4, x) for x in range(4)] + [("wb", kc // 4)], writes=[("pm", sl)])
                rope_store(pm[0:64, slA, :], pm[0:64, slB, :], slA, slB, hb, tg, kT_out[KROPE_ROW:KROPE_ROW + 64, c0t:c0t + 512])
                for fi, (dst, r0) in enumerate(((qT_out, QB_ROW), (kT_out, KB_ROW), (qT_out, QC_ROW), (kT_out, KC_ROW))):
                    sl = pm_slot()
                    for kc in range(KC):
                        S.add("pe", lambda e, hb=hb, tg=tg, kc=kc, sl=sl, fi=fi: e.matmul(
                            pm[:, sl, :], lhsT=wb[:, kc, 640 + fi * 128:640 + (fi + 1) * 128], rhs=hT[:, hb, kc, tg * 512:(tg + 1) * 512],
                            start=(kc == 0), stop=(kc == KC - 1)),
                            reads=[("hT", hb, kc // 4, x) for x in range(4)] + [("wb", kc // 4)], writes=[("pm", sl)])
                    evac_store(pm[:, sl, :], dst[r0:r0 + 128, c0t:c0t + 512], 128, 512, sl)
                for hh in range(2):
                    sl = pm_slot()
                    for kc in range(3):
                        S.add("pe", lambda e, hb=hb, tg=tg, kc=kc, sl=sl, hh=hh: e.matmul(
                            pm[:, sl, :], lhsT=wuq[:, kc, hh * 256:hh * 256 + 128], rhs=cnT[:, hb, kc, tg * 512:(tg + 1) * 512],
                            start=(kc == 0), stop=(kc == 2)),
                            reads=ck[tg * 4:(tg + 1) * 4] + ["wuq"], writes=[("pm", sl)])
                    evac_store(pm[:, sl, :], qT_out[hh * 192:hh * 192 + 128, c0t:c0t + 512], 128, 512, sl)
                    slA = pm_slot()
                    slB = pm_slot()
                    for v_, sl in ((0, slA), (1, slB)):
                        for kc in range(3):
                            S.add("pe", lambda e, hb=hb, tg=tg, kc=kc, sl=sl, hh=hh, v_=v_: e.matmul(
                                pm[0:64, sl, :], lhsT=wuq[:, kc, hh * 256 + 128 + 64 * v_:hh * 256 + 192 + 64 * v_],
                                rhs=cnT[:, hb, kc, tg * 512:(tg + 1) * 512], start=(kc == 0), stop=(kc == 2)),
                                reads=ck[tg * 4:(tg + 1) * 4] + ["wuq"], writes=[("pm", sl)])
                    rope_store(pm[0:64, slA, :], pm[0:64, slB, :], slA, slB, hb, tg, qT_out[hh * 192 + 128:hh * 192 + 192, c0t:c0t + 512])
                    sl = pm_slot()
                    S.add("pe", lambda e, hb=hb, tg=tg, sl=sl, hh=hh: e.matmul(
                        pm[:, sl, :], lhsT=wukv[:, hh * 128:(hh + 1) * 128], rhs=cnT[:, hb, 3, tg * 512:(tg + 1) * 512],
                        start=True, stop=True),
                        reads=ck[tg * 4:(tg + 1) * 4] + ["wukv"], writes=[("pm", sl)])
                    evac_store(pm[:, sl, :], kT_out[hh * 128:(hh + 1) * 128, c0t:c0t + 512], 128, 512, sl)
            for tt in range(GT // 128):
                sl = pm_slot()
                S.add("pe", lambda e, hb=hb, tt=tt, sl=sl: e.matmul(
                    pm[:, sl, 0:256], lhsT=cnT[:, hb, 3, tt * 128:(tt + 1) * 128], rhs=wukv[:, 256:512], start=True, stop=True),
                    reads=[("cnT", hb, tt), "wukv"], writes=[("pm", sl)])
                for kc in range(KC):
                    S.add("pe", lambda e, hb=hb, tt=tt, kc=kc, sl=sl: e.matmul(
                        pm[:, sl, 256:512], lhsT=hT[:, hb, kc, tt * 128:(tt + 1) * 128], rhs=wb[:, kc, 1152:1408],
                        start=(kc == 0), stop=(kc == KC - 1)),
                        reads=[("hT", hb, kc // 4, x) for x in range(4)] + [("wb", kc // 4)], writes=[("pm", sl)])
                evac_store(pm[:, sl, :], v_out[t0 + tt * 128:t0 + (tt + 1) * 128, :], 128, 512, sl)
        S.run_block()


def phase_ATT(nc, S, qT, kT, v, kaug, qaug, maskA, corrB, biasC, maskC, lamv, sublnT, lam_init, mixT_out, mix_gath):
    NB = SEQ // 512
    scA = float(192 ** -0.5)
    scB = 0.125
    scC = float(128 ** -0.5)
    with ExitStack() as es:
        T = lambda n, s, dt: es.enter_context(nc.sbuf_tensor(_uid(nc) + n, s, dt))
        vsb = T("a_v", [128, SEQ // 128, 512], BF16)
        kbuf = T("a_k", [128, 2, SEQ], BF16)
        krope = T("a_kr", [64, SEQ], BF16)
        qn = T("a_qn", [128, 3, 512], BF16)
        qr = T("a_qr", [64, 3, 512], BF16)
        pT = T("a_pT", [128, 6, 512], BF16)
        acc = T("a_acc", [128, 2, 512], F32)
        rec = T("a_rec", [128, 2, 512], F32)
        dsb = T("a_dsb", [128, 2, 512], F32)
        acc2 = T("a_acc2", [128, 2, 512], F32)
        ones5 = T("a_ones5", [128, 512], F32)
        ones_b = T("a_onesb", [128, 128], BF16)
        mhalf = T("a_mhalf", [128, 512], F32)
        ost = T("a_ost", [128, 2, 512], BF16)
        tmp0 = T("a_tmp0", [128, 512], F32)
        t1 = T("a_t1", [128, 512], F32)
        ob_ = T("a_o", [128, 512], F32)
        sq = T("a_sq", [128, 512], F32)
        tS = T("a_tS", [128, 2, 128], F32)
        bm = T("a_bm", [128, 5, 128], F32)
        mC = T("a_mC", [128, 2, 128], F32)
        mA = T("a_mA", [128, 128], BF16)
        cB = T("a_cB", [128, 128], BF16)
        ident_f = T("a_identf", [128, 128], F32)
        ident = T("a_ident", [128, 128], BF16)
        ones_f = T("a_ones", [128, 128], F32)
        lv = T("a_lv", [128, 4, 64], F32)
        lt = T("a_lt", [128, 2, 64], F32)
        ls = T("a_ls", [128, 8], F32)
        sg = T("a_sg", [128, 2], F32)
        ps_s = es.enter_context(nc.psum_tensor(_uid(nc) + "a_ps_s", [128, 4, 512], F32))
        ps_o = es.enter_context(nc.psum_tensor(_uid(nc) + "a_ps_o", [128, 2, 512], F32))
        ps_d = es.enter_context(nc.psum_tensor(_uid(nc) + "a_ps_d", [128, 2, 512], F32))

        make_ident(S, ident, ident_f)
        S.add("pool", lambda e: e.memset(ones_f[:], 1.0), writes=["ones"])
        S.add("pool", lambda e: e.memset(ones5[:], -1.0), writes=["ones5"])
        S.add("dve", lambda e: e.tensor_copy(out=ones_b[:], in_=ones_f[:]), reads=["ones"], writes=["onesb"])
        S.add("pool", lambda e: e.memset(mhalf[:], -0.5), writes=["mhalf"])
        S.add("sp", lambda e: e.dma_start(out=mA[:], in_=maskA), writes=["mA"], dma=True)
        S.add("sp", lambda e: e.dma_start(out=cB[:], in_=corrB), writes=["cB"], dma=True)
        S.add("sp", lambda e: e.dma_start(out=bm[:], in_=biasC.rearrange("d k q -> k d q")), writes=["bm"], dma=True)
        S.add("sp", lambda e: e.dma_start(out=mC[:], in_=maskC.rearrange("d k q -> k d q")), writes=["mC"], dma=True)
        S.add("sp", lambda e: e.dma_start(out=lv[:], in_=lamv.partition_broadcast(128)), writes=["lv"], dma=True)
        S.add("sp", lambda e: e.dma_start(out=sg[:, 0:1], in_=sublnT), writes=["sg0"], dma=True)
        S.add("sp", lambda e: e.dma_start(out=vsb[:], in_=v.rearrange("(kt p) c -> p kt c", p=128)), writes=["vsb"], dma=True)
        S.add("sp", lambda e: e.dma_start(out=krope[:], in_=kT[KROPE_ROW:KROPE_ROW + 64, :]), writes=["krope"], dma=True)
        S.add("dve", lambda e: e.tensor_tensor(out=bm[:, 0, :], in0=bm[:, 0, :], in1=mC[:, 0, :], op=ALU.add), reads=["bm", "mC"], writes=["bm"])
        S.add("dve", lambda e: e.tensor_tensor(out=bm[:, 4, :], in0=bm[:, 4, :], in1=mC[:, 1, :], op=ALU.add), reads=["bm", "mC"], writes=["bm"])
        for i in range(2):
            S.add("dve", lambda e, i=i: e.tensor_tensor(out=lt[:, i, :], in0=lv[:, 2 * i, :], in1=lv[:, 2 * i + 1, :], op=ALU.mult),
                  reads=["lv"], writes=[("lt", i)])
            S.add("dve", lambda e, i=i: e.reduce_sum(out=ls[:, i:i + 1], in_=lt[:, i, :], axis=AX.X), reads=[("lt", i)], writes=[("ls", i)])
            S.add("act", lambda e, i=i: e.activation(out=ls[:, 2 + i:3 + i], in_=ls[:, i:i + 1], func=AF.Exp), reads=[("ls", i)], writes=[("le", i)])
        S.add("dve", lambda e: e.tensor_tensor(out=ls[:, 4:5], in0=ls[:, 3:4], in1=ls[:, 2:3], op=ALU.subtract), reads=[("le", 0), ("le", 1)], writes=["nl0"])
        S.add("dve", lambda e: e.tensor_scalar(out=ls[:, 5:6], in0=ls[:, 4:5], scalar1=-float(lam_init), scalar2=None, op0=ALU.add), reads=["nl0"], writes=["nlam"])
        S.add("dve", lambda e: e.tensor_scalar(out=sg[:, 1:2], in0=sg[:, 0:1], scalar1=float(1.0 - lam_init), scalar2=None, op0=ALU.mult), reads=["sg0"], writes=["sgain"])
        nlam = ls[:, 5:6]
        sgain = sg[:, 1:2]

        def load_k(kind, slot):
            if kind in ("A0", "A1"):
                r0 = 0 if kind == "A0" else 128
                for q in range(2):
                    S.add("sp", lambda e, q=q, r0=r0, slot=slot: e.dma_start(
                        out=kbuf[:, slot, q * 4096:(q + 1) * 4096], in_=kT[r0:r0 + 128, q * 4096:(q + 1) * 4096]),
                        writes=[("kbuf", slot, q)], dma=True)
            elif kind in ("B0", "B1"):
                n = 0 if kind == "B0" else 1
                for q in range(2):
                    S.add("sp", lambda e, q=q, n=n, slot=slot: e.dma_start(
                        out=kbuf[0:64, slot, q * 4096:(q + 1) * 4096], in_=kT[KB_ROW + 64 * n:KB_ROW + 64 * n + 64, q * 4096:(q + 1) * 4096]),
                        writes=[("kbuf", slot, q)], dma=True)
                S.add("sp", lambda e, slot=slot: e.dma_start(out=kbuf[64:68, slot, :], in_=kaug),
                      reads=[("kbuf", slot, 0), ("kbuf", slot, 1)], writes=[("kbuf", slot, 0), ("kbuf", slot, 1)], dma=True)
            else:
                for q in range(2):
                    S.add("sp", lambda e, q=q, slot=slot: e.dma_start(
                        out=kbuf[:, slot, q * 4096:(q + 1) * 4096], in_=kT[KC_ROW:KC_ROW + 128, q * 4096:(q + 1) * 4096]),
                        writes=[("kbuf", slot, q)], dma=True)

        qcnt = [0]

        def load_q(kind, I):
            qs = qcnt[0] % 3
            qcnt[0] += 1
            c = slice(I * 512, (I + 1) * 512)
            if kind in ("A0", "A1"):
                r0 = 0 if kind == "A0" else 192
                S.add("sp", lambda e: e.dma_start(out=qn[:, qs, :], in_=qT[r0:r0 + 128, c]), writes=[("q", qs)], dma=True)
                S.add("sp", lambda e: e.dma_start(out=qr[:, qs, :], in_=qT[r0 + 128:r0 + 192, c]), writes=[("qr", qs)], dma=True)
            elif kind in ("B0", "B1"):
                n = 0 if kind == "B0" else 1
                S.add("sp", lambda e: e.dma_start(out=qn[0:64, qs, :], in_=qT[QB_ROW + 64 * n:QB_ROW + 64 * n + 64, c]), writes=[("q", qs)], dma=True)
                S.add("sp", lambda e: e.dma_start(out=qn[64:68, qs, :], in_=qaug[:, c]), reads=[("q", qs)], writes=[("q", qs)], dma=True)
            else:
                S.add("sp", lambda e: e.dma_start(out=qn[:, qs, :], in_=qT[QC_ROW:QC_ROW + 128, c]), writes=[("q", qs)], dma=True)
            return qs

        blocks = []
        for I in range(NB):
            blocks.append(("A0", I, 0))
        for I in range(NB):
            blocks.append(("A1", I, 1))
        for I in range(NB):
            blocks.append(("B0", I, 0))
            blocks.append(("B1", I, 1))
        for I in range(NB):
            blocks.append(("C", I, 0))
        vcol = {"A0": 0, "A1": 128, "B0": 256, "B1": 256, "C": 384}
        orow = {"A0": 0, "A1": 128, "B1": 256, "C": 384}

        tiles = []
        for bi, (kind, I, ks) in enumerate(blocks):
            tl = []
            if kind == "C":
                for qi in range(4):
                    qt = 4 * I + qi
                    dl = [d for d in (4, 3, 2, 1, 0) if qt - d >= 0]
                    for d in dl:
                        tl.append(dict(kt=qt - d, c0=128 * qi, n=128, mask=None, bias=d, fc=(d == dl[0]), lc=(d == 0)))
            else:
                for kt in range(4 * I + 4):
                    t = kt - 4 * I
                    if t < 0:
                        tl.append(dict(kt=kt, c0=0, n=512, mask=None, bias=None, fc=(kt == 0), lc=False))
                    else:
                        tl.append(dict(kt=kt, c0=128 * t, n=512 - 128 * t, mask=True, bias=None, fc=(kt == 0), lc=False))
                tl[-1]["lc"] = True
            for ti, t in enumerate(tl):
                t.update(kind=kind, I=I, ks=ks, bi=bi, ob=bi % 2, ti=ti, first_blk=(ti == 0), last_blk=(ti == len(tl) - 1))
                tiles.append(t)
        N = len(tiles)
        for g, t in enumerate(tiles):
            t["ss"] = g % 4
            t["ps"] = g % 6
            t["tsr"] = g % 2

        blk_q = {}

        def start_block(bi):
            if bi >= len(blocks) or bi in blk_q:
                return
            kind, I, ks = blocks[bi]
            blk_q[bi] = load_q(kind, I)

        def emit_qk(g):
            t = tiles[g]
            kind, ks, ss, c0, n, kt = t["kind"], t["ks"], t["ss"], t["c0"], t["n"], t["kt"]
            qs = blk_q[t["bi"]]
            kq = t["kt"] // 32
            has_mask = t["mask"] is not None
            if kind in ("A0", "A1"):
                S.add("pe", lambda e: e.matmul(ps_s[:, ss, c0:c0 + n], lhsT=kbuf[:, ks, kt * 128:(kt + 1) * 128], rhs=qn[:, qs, c0:c0 + n],
                                               start=True, stop=False),
                      reads=[("kbuf", ks, kq), ("q", qs)], writes=[("ps_s", ss)], rhs=[("q", qs)])
                S.add("pe", lambda e: e.matmul(ps_s[:, ss, c0:c0 + n], lhsT=krope[:, kt * 128:(kt + 1) * 128], rhs=qr[:, qs, c0:c0 + n],
                                               start=False, stop=not has_mask),
                      reads=["krope", ("qr", qs)], writes=[("ps_s", ss)], rhs=[("qr", qs)])
                if has_mask:
                    S.add("pe", lambda e: e.matmul(ps_s[:, ss, c0:c0 + 128], lhsT=ident[:], rhs=mA[:], start=False, stop=True),
                          reads=["ident", "mA"], writes=[("ps_s", ss)])
            elif kind in ("B0", "B1"):
                S.add("pe", lambda e: e.matmul(ps_s[:, ss, c0:c0 + n], lhsT=kbuf[0:68, ks, kt * 128:(kt + 1) * 128], rhs=qn[0:68, qs, c0:c0 + n],
                                               start=True, stop=not has_mask),
                      reads=[("kbuf", ks, kq), ("q", qs)], writes=[("ps_s", ss)], rhs=[("q", qs)])
                if has_mask:
                    S.add("pe", lambda e: e.matmul(ps_s[:, ss, c0:c0 + 128], lhsT=ident[:], rhs=cB[:], start=False, stop=True),
                          reads=["ident", "cB"], writes=[("ps_s", ss)])
            else:
                S.add("pe", lambda e: e.matmul(ps_s[:, ss, c0:c0 + n], lhsT=kbuf[:, ks, kt * 128:(kt + 1) * 128], rhs=qn[:, qs, c0:c0 + n],
                                               start=True, stop=True),
                      reads=[("kbuf", ks, kq), ("q", qs)], writes=[("ps_s", ss)], rhs=[("q", qs)])

        def emit_exp(g):
            t = tiles[g]
            kind, ss, ps, c0, n = t["kind"], t["ss"], t["ps"], t["c0"], t["n"]
            if kind == "C":
                r = t["tsr"]
                d = t["bias"]
                S.add("dve", lambda e: e.scalar_tensor_tensor(out=tS[:, r, :], in0=ps_s[:, ss, c0:c0 + n], scalar=scC, in1=bm[:, d, :],
                                                              op0=ALU.mult, op1=ALU.add),
                      reads=[("ps_s", ss), "bm"], writes=[("tS", r)])
                S.add("act", lambda e: e.activation(out=pT[:, ps, c0:c0 + n], in_=tS[:, r, :], func=AF.Exp),
                      reads=[("tS", r)], writes=[("pT", ps)])
            else:
                sc = scA if kind in ("A0", "A1") else scB
                S.add("act", lambda e: e.activation(out=pT[:, ps, c0:c0 + n], in_=ps_s[:, ss, c0:c0 + n], func=AF.Exp, scale=sc),
                      reads=[("ps_s", ss)], writes=[("pT", ps)])

        def emit_pv(g):
            t = tiles[g]
            ps, c0, n, kt, ob = t["ps"], t["c0"], t["n"], t["kt"], t["ob"]
            vc = vcol[t["kind"]]
            S.add("pe", lambda e: e.matmul(ps_o[:, ob, c0:c0 + n], lhsT=vsb[:, kt, vc:vc + 128], rhs=pT[:, ps, c0:c0 + n],
                                           start=t["fc"], stop=t["lc"]),
                  reads=["vsb", ("pT", ps)], writes=[("ps_o", ob)], rhs=[("pT", ps)])

        def emit_acc(g):
            t = tiles[g]
            ps, c0, n, ob = t["ps"], t["c0"], t["n"], t["ob"]
            S.add("pe", lambda e: e.matmul(ps_d[:, ob, c0:c0 + n], lhsT=ones_b[:], rhs=pT[:, ps, c0:c0 + n],
                                           start=t["fc"], stop=t["lc"]),
                  reads=["onesb", ("pT", ps)], writes=[("ps_d", ob)], rhs=[("pT", ps)])

        def emit_epilogue(bi):
            kind, I, ks = blocks[bi]
            ob = bi % 2
            c = slice(I * 512, (I + 1) * 512)
            S.add("act", lambda e: e.activation(out=dsb[:, ob, :], in_=ps_d[:, ob, :], func=AF.Ln), reads=[("ps_d", ob)], writes=[("dsb", ob)])
            S.add("act", lambda e: e.activation(out=rec[:, ob, :], in_=dsb[:, ob, :], func=AF.Exp, scale=-1.0), reads=[("dsb", ob)], writes=[("rec", ob)])
            if kind in ("A0", "A1", "C"):
                S.add("dve", lambda e: e.tensor_tensor(out=ost[:, ob, :], in0=ps_o[:, ob, :], in1=rec[:, ob, :], op=ALU.mult),
                      reads=[("ps_o", ob), ("rec", ob)], writes=[("ost", ob)])
                r0 = orow[kind]
                S.add("pool", lambda e: e.dma_start(out=mixT_out[r0 // 64][:, c], in_=ost[0:64, ob, :]), reads=[("ost", ob)], writes=[("mixc", r0 // 64)], dma=True)
                S.add("pool", lambda e: e.dma_start(out=mixT_out[r0 // 64 + 1][:, c], in_=ost[64:128, ob, :]), reads=[("ost", ob)], writes=[("mixc", r0 // 64 + 1)], dma=True)
                if I == NB - 1:
                    for ch in (r0 // 64, r0 // 64 + 1):
                        add_ag(S, mixT_out[ch], mix_gath[ch], [("mixc", ch)], [("mixg", id(mix_gath[ch]))])
            elif kind == "B0":
                S.add("dve", lambda e: e.tensor_tensor(out=tmp0[:], in0=ps_o[:, ob, :], in1=rec[:, ob, :], op=ALU.mult),
                      reads=[("ps_o", ob), ("rec", ob)], writes=["tmp0"])
            else:
                S.add("dve", lambda e: e.tensor_tensor(out=t1[:], in0=ps_o[:, ob, :], in1=rec[:, ob, :], op=ALU.mult),
                      reads=[("ps_o", ob), ("rec", ob)], writes=["t1"])
                S.add("dve", lambda e: e.scalar_tensor_tensor(out=ob_[:], in0=t1[:], scalar=nlam, in1=tmp0[:], op0=ALU.mult, op1=ALU.add),
                      reads=["t1", "tmp0", "nlam"], writes=["o"])
                S.add("pool", lambda e: e.tensor_tensor(out=sq[:], in0=ob_[:], in1=ob_[:], op=ALU.mult), reads=["o"], writes=["sq"])
                S.add("pe", lambda e: e.matmul(ps_d[:, ob, :], lhsT=ones_f[:], rhs=sq[:], start=True, stop=True),
                      reads=["ones", "sq"], writes=[("ps_d", ob)])
                S.add("dve", lambda e: e.tensor_scalar(out=sq[:], in0=ps_d[:, ob, :], scalar1=1.0 / 128, scalar2=EPS, op0=ALU.mult, op1=ALU.add),
                      reads=[("ps_d", ob)], writes=["sq"])
                S.add("act", lambda e: e.activation(out=sq[:], in_=sq[:], func=AF.Ln), reads=["sq"], writes=["sq"])
                S.add("act", lambda e: e.activation(out=sq[:], in_=sq[:], func=AF.Exp, scale=-0.5), reads=["sq"], writes=["sq"])
                S.add("dve", lambda e: e.scalar_tensor_tensor(out=ost[:, ob, :], in0=ob_[:], scalar=sgain, in1=sq[:], op0=ALU.mult, op1=ALU.mult),
                      reads=["o", "sq", "sgain"], writes=[("ost", ob)])
                S.add("pool", lambda e: e.dma_start(out=mixT_out[4][:, c], in_=ost[0:64, ob, :]), reads=[("ost", ob)], writes=[("mixc", 4)], dma=True)
                S.add("pool", lambda e: e.dma_start(out=mixT_out[5][:, c], in_=ost[64:128, ob, :]), reads=[("ost", ob)], writes=[("mixc", 5)], dma=True)
                if I == NB - 1:
                    for ch in (4, 5):
                        add_ag(S, mixT_out[ch], mix_gath[ch], [("mixc", ch)], [("mixg", id(mix_gath[ch]))])

        unit_first = {}
        for bi, (kind, I, ks) in enumerate(blocks):
            unit_first.setdefault(kind, bi)
        load_k("A0", 0)
        load_k("A1", 1)
        start_block(0)
        start_block(1)

        def maybe_unit_loads(bi):
            kind, I, ks = blocks[bi]
            if unit_first[kind] != bi:
                return
            if kind == "A1":
                load_k("B0", 0)
            elif kind == "B0":
                load_k("B1", 1)

        c_loaded = [False]
        pend_ep = []
        LA = 3
        for g0 in range(LA):
            start_block(tiles[g0]["bi"])
            emit_qk(g0)
        for g in range(N):
            t = tiles[g]
            if t["first_blk"]:
                start_block(t["bi"] + 1)
                start_block(t["bi"] + 2)
                maybe_unit_loads(t["bi"])
            emit_exp(g)
            if g + LA < N:
                t2 = tiles[g + LA]
                if t2["kind"] == "C" and not c_loaded[0]:
                    load_k("C", 0)
                    c_loaded[0] = True
                start_block(t2["bi"])
                emit_qk(g + LA)
            emit_pv(g)
            emit_acc(g)
            for pe_ in list(pend_ep):
                if g >= pe_[0]:
                    emit_epilogue(pe_[1])
                    pend_ep.remove(pe_)
            if t["last_blk"]:
                pend_ep.append((g + 3, t["bi"]))
        for pe_ in pend_ep:
            emit_epilogue(pe_[1])
        S.run_block()


def colT(vec, nchunk):
    return np.ascontiguousarray(np.asarray(vec, np.float32).reshape(nchunk, 128).T)

def w_in_core(w_in_l, j):
    s = lambda a, n: w_in_l[:, a:a + n]
    qb0, kb0, vb0, qc0, kc0, vc0 = 576, 1088, 1600, 2112, 2624, 3136
    cols = [s(0, 512), s(512, 64), s(544, 32), s(512, 32),
            s(qb0 + 128 * j, 128), s(kb0 + 128 * j, 128), s(qc0 + 128 * j, 128), s(kc0 + 128 * j, 128),
            s(vb0 + 128 * j, 128), s(vc0 + 128 * j, 128)]
    return np.ascontiguousarray(np.concatenate(cols, axis=1))

def w_uq_core(w_uq_l, j):
    cols = []
    for hh in range(2):
        b = (2 * j + hh) * 192
        cols += [w_uq_l[:, b:b + 128], w_uq_l[:, b + 128:b + 192], w_uq_l[:, b + 160:b + 192], w_uq_l[:, b + 128:b + 160]]
    return np.ascontiguousarray(np.concatenate(cols, axis=1))

def w_ukv_core(w_ukv_l, j):
    b0, b1 = (2 * j) * 256, (2 * j + 1) * 256
    return np.ascontiguousarray(np.concatenate([w_ukv_l[:, b0:b0 + 128], w_ukv_l[:, b1:b1 + 128],
                                                w_ukv_l[:, b0 + 128:b0 + 256], w_ukv_l[:, b1 + 128:b1 + 256]], axis=1))

def rope_tables(seq=8192):
    half = 32
    inv = (np.float32(10000.0) ** (-np.arange(half, dtype=np.float32) / np.float32(half))).astype(np.float32)
    ang = (np.arange(seq, dtype=np.float32)[None, :] * inv[:, None]).astype(np.float32)
    c = np.cos(ang.astype(np.float64)).astype(np.float32)
    s = np.sin(ang.astype(np.float64)).astype(np.float32)
    return np.ascontiguousarray(np.concatenate([c, c], 0)), np.ascontiguousarray(np.concatenate([-s, s], 0))

NEGM = -30000.0

def mask_A():
    m = np.zeros((128, 128), np.float32)
    m[64:, :64] = NEGM
    return m.astype(ml_dtypes.bfloat16)

def corr_B(j):
    c = (2.0 ** (-2.0 * (j + 1))) * 8.0
    k = np.arange(128)[:, None]; q = np.arange(128)[None, :]
    m = np.where((k // 64 == q // 64) & (k > q), -2.0 * c * (k - q), 0.0).astype(np.float32)
    m[64:, :64] = NEGM
    return m.astype(ml_dtypes.bfloat16)

def mask_C():
    m = np.zeros((2, 128, 128), np.float32)
    m[0, 64:, :64] = NEGM
    m[1, :64, 64:] = NEGM
    return m

def bias_C(rel_bias_lh):
    k = np.arange(128)[:, None]; q = np.arange(128)[None, :]
    out = np.empty((5, 128, 128), np.float32)
    for d in range(5):
        idx = np.clip(128 * d + q - k, -63, 256) + 63
        out[d] = rel_bias_lh[idx]
    return out

def k_aug(seq=8192):
    p = np.arange(seq)
    return np.stack([p // 128, p % 128, np.ones(seq), np.ones(seq)]).astype(np.float32).astype(ml_dtypes.bfloat16)

def q_aug(j, seq=8192):
    c = (2.0 ** (-2.0 * (j + 1))) * 8.0
    p = np.arange(seq)
    return np.stack([np.full(seq, 128 * c), np.full(seq, c), -128 * c * (p // 128), -c * (p % 128)]).astype(np.float32).astype(ml_dtypes.bfloat16)


NCORES = 8
DEPTH = 2
FUSED = True
_PROG_CACHE = {}


def _lam_init(l):
    import math
    return 0.8 - 0.6 * math.exp(-0.3 * l)


def _di(nc, n, s, dt=F32):
    return nc.dram_tensor(n, list(s), dt, kind="ExternalInput").ap()


def _do(nc, n, s, dt=BF16):
    return nc.dram_tensor(n, list(s), dt, kind="ExternalOutput").ap()


def _dint(nc, n, s, dt=BF16):
    return nc.dram_tensor(n, list(s), dt, kind="Internal").ap()


def _decl_att_inputs(nc, pfx=""):
    d = {}
    d["w_in_c"] = _di(nc, pfx + "w_in_c", [D, WIN_C])
    d["qag"] = _di(nc, pfx + "qag", [128, 3])
    d["kvag"] = _di(nc, pfx + "kvag", [128, 1])
    d["w_uq_c"] = _di(nc, pfx + "w_uq_c", [384, 512])
    d["w_ukv_c"] = _di(nc, pfx + "w_ukv_c", [128, 512])
    d["biasC"] = _di(nc, pfx + "biasC", [5, 128, 128])
    d["lamv"] = _di(nc, pfx + "lamv", [4, 64])
    d["sublnT"] = _di(nc, pfx + "sublnT", [128, 1])
    return d


def _decl_consts(nc):
    d = {}
    d["rC"] = _di(nc, "rC", [64, SEQ])
    d["rS"] = _di(nc, "rS", [64, SEQ])
    d["kaug"] = _di(nc, "kaug", [4, SEQ], BF16)
    d["qaug"] = _di(nc, "qaug", [4, SEQ], BF16)
    d["maskA"] = _di(nc, "maskA", [128, 128], BF16)
    d["corrB"] = _di(nc, "corrB", [128, 128], BF16)
    d["maskC"] = _di(nc, "maskC", [2, 128, 128])
    return d


def _decl_mlp_inputs(nc, pfx=""):
    d = {}
    d["w_o_p"] = _di(nc, pfx + "w_o_p", [D, D])
    d["gT_mlp"] = _di(nc, pfx + "gT_mlp", [128, KC])
    d["w_up"] = _di(nc, pfx + "w_up", [D, DFF])
    d["w_down"] = _di(nc, pfx + "w_down", [DFF, D])
    return d


def build_fused():
    nc = bass.Bass("TRN2", target_bir_lowering=False, num_devices=NCORES)
    x = _di(nc, "x", [TOK, D])
    consts = _decl_consts(nc)
    gfin = _di(nc, "g_final", [D])
    y = _do(nc, "y", [TOK, D], F32)
    S = Sched(nc)
    S.alloc_sems()
    x_cur = x
    for l in range(DEPTH):
        pfx = "l%d_" % l
        gT = _di(nc, pfx + "gT_attn", [128, KC])
        a = _decl_att_inputs(nc, pfx)
        m = _decl_mlp_inputs(nc, pfx)
        hTc = [_dint(nc, pfx + "hTc%d" % c, [D, 256]) for c in range(8)]
        hTg = [_dint(nc, pfx + "hTg%d" % c, [4 * D, 256]) for c in range(8)]
        mixc = [_dint(nc, pfx + "mixc%d" % c, [64, SEQ]) for c in range(8)]
        mixg = [_dint(nc, pfx + "mixg%d" % c, [256, SEQ]) for c in range(8)]
        qT = _dint(nc, pfx + "qT_h", [QRH, SEQ])
        kT = _dint(nc, pfx + "kT_h", [KRH, SEQ])
        v = _dint(nc, pfx + "v_h", [SEQ, 512])
        x1 = _dint(nc, pfx + "x1", [TOK, D], F32)
        h2T = _dint(nc, pfx + "h2T", [D, TOK])
        final = (l == DEPTH - 1)
        x_next = y if final else _dint(nc, pfx + "xo", [TOK, D], F32)
        phase_P1(nc, S, x_cur, gT, hTc, hTg)
        phase_P2h(nc, S, hTg, a["w_in_c"], a["qag"], a["kvag"], a["w_uq_c"], a["w_ukv_c"], consts["rC"], consts["rS"], qT, kT, v)
        phase_ATT(nc, S, qT, kT, v, consts["kaug"], consts["qaug"], consts["maskA"], consts["corrB"], a["biasC"], consts["maskC"],
                  a["lamv"], a["sublnT"], _lam_init(l), mixc, mixg)
        phase_O1(nc, S, x_cur, mixg, m["w_o_p"], m["gT_mlp"], x1, h2T)
        phase_O2(nc, S, x1, h2T, m["w_up"], m["w_down"], x_next, g_final=gfin if final else None)
        x_cur = x_next
    return nc


def _prog(key, fn):
    if key not in _PROG_CACHE:
        _PROG_CACHE[key] = fn()
    return _PROG_CACHE[key]


def w_o_perm(w_o_l):
    def f(r, lr):
        if lr < 256:
            return 256 * r + lr
        if lr < 384:
            return 1024 + 128 * r + (lr - 256)
        return 1536 + 128 * r + (lr - 384)
    idx = [f(r, 64 * c + i) for c in range(8) for r in range(4) for i in range(64)]
    return np.ascontiguousarray(w_o_l[np.asarray(idx)])


def _f32(a):
    return np.ascontiguousarray(np.asarray(a, dtype=np.float32))


def _att_inputs(inp, l, j):
    return {
        "w_in_c": w_in_core(inp["w_in"][l], j),
        "qag": colT(inp["q_a_norm"][l], 3),
        "kvag": colT(inp["kv_a_norm"][l], 1),
        "w_uq_c": w_uq_core(inp["w_uq"][l], j),
        "w_ukv_c": w_ukv_core(inp["w_ukv"][l], j),
        "biasC": bias_C(inp["rel_bias"][l, j]),
        "lamv": np.ascontiguousarray(np.stack([inp["lambda_q1"][l], inp["lambda_k1"][l], inp["lambda_q2"][l], inp["lambda_k2"][l]])),
        "sublnT": np.ascontiguousarray(inp["diff_subln"][l].reshape(128, 1)),
    }


def _const_inputs(j):
    rC, rS = rope_tables()
    return {"rC": rC, "rS": rS, "kaug": k_aug(), "qaug": q_aug(j), "maskA": mask_A(), "corrB": corr_B(j), "maskC": mask_C()}


def _mlp_inputs(inp, l):
    return {"w_o_p": w_o_perm(inp["w_o"][l]), "gT_mlp": colT(inp["mlp_norm"][l], KC),
            "w_up": _f32(inp["w_up"][l]), "w_down": _f32(inp["w_down"][l])}


def kernel_fused(inp):
    cores = list(range(NCORES))
    x = inp["x"]
    nc = _prog("fused", build_fused)
    shared = {}
    for l in range(DEPTH):
        pfx = "l%d_" % l
        shared[pfx + "gT_attn"] = colT(inp["attn_norm"][l], KC)
        for k, v in _mlp_inputs(inp, l).items():
            shared[pfx + k] = v
    shared["g_final"] = _f32(inp["final_norm"])
    ins = []
    for c in cores:
        j = c % 4
        d = {"x": np.ascontiguousarray(x[c // 4, j * TOK:(j + 1) * TOK])}
        d.update(shared)
        d.update(_const_inputs(j))
        for l in range(DEPTH):
            for k, v in _att_inputs(inp, l, j).items():
                d["l%d_" % l + k] = v
        ins.append(d)
    res = run_bass_kernel_spmd(nc, ins, core_ids=cores)
    out = np.empty((2, SEQ, D), np.float32)
    for c in cores:
        out[c // 4, (c % 4) * TOK:(c % 4 + 1) * TOK] = res.results[c]["y"]
    return out


def kernel(**inputs):
    inp = {k: np.asarray(v) for k, v in inputs.items()}
    return kernel_fused(inp)
```

```python
import numpy as np
from contextlib import ExitStack
import ml_dtypes
import concourse.bass as bass
import concourse.mybir as mybir
from concourse.bass_utils import run_bass_kernel_spmd

F32 = mybir.dt.float32
BF16 = mybir.dt.bfloat16
AF = mybir.ActivationFunctionType
ALU = mybir.AluOpType
AX = mybir.AxisListType

RING = 8
FOLD_WAITS = True
_UIDC = [0]


def _uid(nc):
    _UIDC[0] += 1
    return "t%d_" % _UIDC[0]

ENGS = ("pe", "act", "dve", "pool", "sp")


class Ins:
    __slots__ = ("eng", "idx", "fn", "deps", "dma", "dma_n", "needs_inc", "semval", "cc", "lhs_deps")


class Sched:
    def __init__(self, nc):
        self.nc = nc
        self.progs = {e: [] for e in ENGS}
        self.res = {}
        self.ndma = {e: 0 for e in ENGS}
        self.emitted = {e: 0 for e in ENGS}
        self.waited = {e: {} for e in ENGS}
        self.cnt = {e: 0 for e in ENGS}
        self.sems = {}
        self.rings = {}
        self.cc_sems = []

    def alloc_sems(self):
        nc = self.nc
        for e in ENGS:
            self.sems[e] = nc.alloc_semaphore("c_" + e)
        for e in ("act", "pool", "sp"):
            self.rings[e] = [nc.alloc_semaphore("r_%s_%d" % (e, i)) for i in range(RING)]

    def add(self, eng, fn, reads=(), writes=(), dma=False, cc=False, rhs=()):
        if cc:
            dma = True
        progs = self.progs[eng]
        idx = len(progs)
        deps = {}
        lhs_deps = set()
        for k in reads:
            st = self.res.get(k)
            if st is not None and st[0] is not None:
                deps[st[0]] = True
                if k not in rhs:
                    lhs_deps.add(st[0])
        for k in writes:
            st = self.res.get(k)
            if st is not None:
                if st[0] is not None:
                    deps.setdefault(st[0], False)
                for re_, ri in st[1].items():
                    deps.setdefault((re_, ri), False)
                for r in st[2]:
                    deps.setdefault(r, False)
        ins = Ins()
        ins.eng = eng
        ins.idx = idx
        ins.fn = fn
        ins.dma = dma
        ins.needs_inc = False
        ins.semval = None
        ins.dma_n = -1
        ins.cc = cc
        if cc:
            ins.semval = (self.nc.alloc_semaphore("cc_%d" % len(self.cc_sems)), 1)
            self.cc_sems.append(ins.semval[0])
        elif dma:
            ins.dma_n = self.ndma[eng]
            self.ndma[eng] += 1
        fd = []
        for (pe_, pi), raw in deps.items():
            p = self.progs[pe_][pi]
            if pe_ == eng and not p.dma and not dma:
                if eng == "pe" or not raw:
                    continue
            fd.append((pe_, pi))
            if not p.dma:
                p.needs_inc = True
        ins.deps = fd
        ins.lhs_deps = lhs_deps
        progs.append(ins)
        for k in reads:
            st = self.res.get(k)
            if st is None:
                st = [None, {}, []]
                self.res[k] = st
            if dma:
                st[2].append((eng, idx))
            else:
                st[1][eng] = idx
        for k in writes:
            self.res[k] = [(eng, idx), {}, []]
        return ins

    def _semval(self, ins):
        return ins.semval

    def emit_engine(self, eng, e):
        progs = self.progs[eng]
        waited = self.waited[eng]
        start = self.emitted[eng]
        c = self.cnt[eng]
        for ins in progs[start:]:
            if ins.cc:
                pass
            elif ins.dma:
                ins.semval = (self.rings[eng][ins.dma_n % RING], 16 * (ins.dma_n // RING + 1))
            else:
                if ins.needs_inc:
                    c += 1
                ins.semval = (self.sems[eng], c)
        self.cnt[eng] = c

        for ins in progs[start:]:
            pend = {}
            lhsv = {}
            if ins.dma and not ins.cc and ins.dma_n >= RING:
                sem = self.rings[eng][ins.dma_n % RING]
                pend[id(sem)] = (sem, 16 * (ins.dma_n // RING))
            for (pe_, pi) in ins.deps:
                p = self.progs[pe_][pi]
                assert p.semval is not None, (eng, ins.idx, pe_, pi)
                k = id(p.semval[0])
                if k not in pend or pend[k][1] < p.semval[1]:
                    pend[k] = p.semval
                if eng == "pe" and (pe_, pi) in ins.lhs_deps:
                    lhsv[k] = max(lhsv.get(k, 0), p.semval[1])
            todo = []
            foldable = []
            for k, (sem, val) in pend.items():
                w0 = waited.get(k, 0)
                if w0 < val:
                    if eng == "pe" and lhsv.get(k, 0) > w0:
                        todo.append((sem, val))
                    else:
                        foldable.append((sem, val))
                    waited[k] = val
            fold = None
            if foldable and FOLD_WAITS and not ins.dma:
                fold = foldable.pop()
            todo += foldable
            for sem, val in todo:
                e.wait_ge(sem, val)
            bi = ins.fn(e)
            if fold is not None:
                bi._wait_ge(fold[0], fold[1])
            if ins.cc:
                bi.then_inc(ins.semval[0], 1)
            elif ins.dma:
                bi.then_inc(ins.semval[0], 16)
            elif ins.needs_inc:
                bi.then_inc(ins.semval[0], 1)
        self.emitted[eng] = len(progs)

    def drain_dma(self, e, eng_name="sp", final=False):
        waited = self.waited[eng_name]
        for q, ring in self.rings.items():
            n = self.ndma[q]
            for slot in range(RING):
                cnt = (n - slot + RING - 1) // RING if n > slot else 0
                if cnt > 0:
                    k = id(ring[slot])
                    if waited.get(k, 0) < 16 * cnt:
                        e.wait_ge(ring[slot], 16 * cnt)
                        waited[k] = 16 * cnt
        if final:
            for sem in self.cc_sems:
                k = id(sem)
                if waited.get(k, 0) < 1:
                    e.wait_ge(sem, 1)
                    waited[k] = 1

    def run_block(self, final=False):
        nc = self.nc
        for eng in ENGS:
            progs = self.progs[eng]
            c = self.cnt[eng]
            for ins in progs[self.emitted[eng]:]:
                if ins.cc:
                    pass
                elif ins.dma:
                    ins.semval = (self.rings[eng][ins.dma_n % RING], 16 * (ins.dma_n // RING + 1))
                else:
                    if ins.needs_inc:
                        c += 1
                    ins.semval = (self.sems[eng], c)
        with nc.Block() as block:
            @block.tensor
            def _(e):
                self.emit_engine("pe", e)

            @block.scalar
            def _(e):
                self.emit_engine("act", e)

            @block.vector
            def _(e):
                self.emit_engine("dve", e)

            @block.gpsimd
            def _(e):
                self.emit_engine("pool", e)

            @block.sync
            def _(e):
                self.emit_engine("sp", e)
                self.drain_dma(e, "sp", final)
        keep = {}
        for k, st in self.res.items():
            w = st[0]
            if w is not None and self.progs[w[0]][w[1]].cc:
                keep[k] = [w, {}, []]
        self.res.clear()
        self.res.update(keep)
        nc.all_engine_barrier()


D = 2048
TOK = 2048
NTT = TOK // 128
KC = D // 128
DFF = 8192
EPS = 1e-6


def make_ident(S, ident_bf, ident_f):
    S.add("pool", lambda e: e.memset(ident_f[:], 0.0), writes=["ident_f"])
    S.add("pool", lambda e: e.affine_select(out=ident_f[:], in_=ident_f[:], pattern=[[-1, 128]],
                                            compare_op=ALU.not_equal, fill=1.0, base=0,
                                            channel_multiplier=1),
          reads=["ident_f"], writes=["ident_f"])
    S.add("dve", lambda e: e.tensor_copy(out=ident_bf[:], in_=ident_f[:]), reads=["ident_f"], writes=["ident"])


def rms_rstd(S, eng_sq, src_ap, junk_ap, ss_ap, rstd_ap, n, rkeys, tag):
    S.add("act", lambda e: e.activation(out=junk_ap, in_=src_ap, func=AF.Square, accum_out=ss_ap),
          reads=rkeys, writes=[tag + "_junk", tag + "_ss"])
    S.add("dve", lambda e: e.tensor_scalar(out=rstd_ap, in0=ss_ap, scalar1=1.0 / n, scalar2=EPS,
                                           op0=ALU.mult, op1=ALU.add),
          reads=[tag + "_ss"], writes=[tag + "_rstd"])
    S.add("act", lambda e: e.activation(out=rstd_ap, in_=rstd_ap, func=AF.Sqrt),
          reads=[tag + "_rstd"], writes=[tag + "_rstd"])
    S.add("dve", lambda e: e.reciprocal(out=rstd_ap, in_=rstd_ap),
          reads=[tag + "_rstd"], writes=[tag + "_rstd"])


def phase_O1(nc, S, x_in, mixT, w_o, g_mlp, x1_out, h2T_out):
    with ExitStack() as es:
        mixT_sb = es.enter_context(nc.sbuf_tensor(_uid(nc) + "o1_mixT", [128, KC, TOK], BF16))
        wo_sb = es.enter_context(nc.sbuf_tensor(_uid(nc) + "o1_wo", [128, KC, D], BF16))
        xt = es.enter_context(nc.sbuf_tensor(_uid(nc) + "o1_xt", [128, 3, D], F32))
        xn = es.enter_context(nc.sbuf_tensor(_uid(nc) + "o1_xn", [128, 2, D], BF16))
        junk = es.enter_context(nc.sbuf_tensor(_uid(nc) + "o1_junk", [128, D], BF16))
        hT = es.enter_context(nc.sbuf_tensor(_uid(nc) + "o1_hT", [128, 2, KC, 128], BF16))
        gT = es.enter_context(nc.sbuf_tensor(_uid(nc) + "o1_gT", [128, KC], F32))
        st = es.enter_context(nc.sbuf_tensor(_uid(nc) + "o1_st", [128, 8], F32))
        ident_f = es.enter_context(nc.sbuf_tensor(_uid(nc) + "o1_identf", [128, 128], F32))
        ident = es.enter_context(nc.sbuf_tensor(_uid(nc) + "o1_ident", [128, 128], BF16))
        pm = es.enter_context(nc.psum_tensor(_uid(nc) + "o1_pm", [128, 4, 512], F32))
        pt = es.enter_context(nc.psum_tensor(_uid(nc) + "o1_pt", [128, 2, 8, 128], BF16))
        make_ident(S, ident, ident_f)
        S.add("sp", lambda e: e.dma_start(out=gT[:], in_=g_mlp),
              writes=["gT"], dma=True)
        dyn = {}

        def ld_mix(e, kc):
            if "off" not in dyn:
                dyn["off"] = e.snap((nc.partition_id([mybir.EngineType.SP]) % 4) * TOK, min_val=0, max_val=3 * TOK)
            return e.dma_start(out=mixT_sb[:, kc, :], in_=mixT[kc // 2][(kc % 2) * 128:(kc % 2 + 1) * 128, bass.ds(dyn["off"], TOK)])

        for q in range(4):
            for k4 in range(4):
                S.add("sp", lambda e, kc=q * 4 + k4: ld_mix(e, kc), reads=[("mixg", id(mixT[(q * 4 + k4) // 2]))], writes=[("mixT", q * 4 + k4)], dma=True)
            S.add("pool", lambda e, q=q: e.dma_start(out=wo_sb[:, q * 4:(q + 1) * 4, :],
                                                     in_=w_o[q * 512:(q + 1) * 512, :].rearrange("(kc p) n -> p kc n", p=128)),
                  writes=[("wo", q)], dma=True)
        def part1(tt):
            b = tt % 3
            S.add("sp", lambda e, tt=tt, b=b: e.dma_start(out=xt[:, b, :], in_=x_in[tt * 128:(tt + 1) * 128, :]),
                  writes=[("xt", b)], dma=True)
            for cg in range(4):
                for kc in range(KC):
                    S.add("pe", lambda e, tt=tt, cg=cg, kc=kc: e.matmul(
                        pm[:, cg, :], lhsT=mixT_sb[:, kc, tt * 128:(tt + 1) * 128],
                        rhs=wo_sb[:, kc, cg * 512:(cg + 1) * 512], start=(kc == 0), stop=(kc == KC - 1)),
                        reads=[("mixT", kc), ("wo", kc // 4)], writes=[("pm", cg)], rhs=[("wo", kc // 4)])
                S.add("dve", lambda e, b=b, cg=cg: e.tensor_tensor(
                    out=xt[:, b, cg * 512:(cg + 1) * 512], in0=pm[:, cg, :], in1=xt[:, b, cg * 512:(cg + 1) * 512], op=ALU.add),
                    reads=[("pm", cg), ("xt", b)], writes=[("xt", b)])
            S.add("sp", lambda e, tt=tt, b=b: e.dma_start(out=x1_out[tt * 128:(tt + 1) * 128, :], in_=xt[:, b, :]),
                  reads=[("xt", b)], dma=True)

        def part2(tt):
            b = tt % 3
            h = tt % 2
            rms_rstd(S, "act", xt[:, b, :], junk[:], st[:, h:h + 1], st[:, 2 + h:3 + h], D, [("xt", b)], "o1n%d" % h)
            S.add("act", lambda e, b=b, h=h: e.activation(out=xn[:, h, :], in_=xt[:, b, :], func=AF.Copy, scale=st[:, 2 + h:3 + h]),
                  reads=[("xt", b), "o1n%d_rstd" % h], writes=[("xn", h)])
            for hg in range(2):
                for j in range(8):
                    kc = hg * 8 + j
                    S.add("pe", lambda e, h=h, hg=hg, j=j, kc=kc: e.transpose(
                        out=pt[:, hg, j, :], in_=xn[:, h, kc * 128:(kc + 1) * 128], identity=ident[:]),
                        reads=[("xn", h), "ident"], writes=[("pt", hg)])
                for j in range(8):
                    kc = hg * 8 + j
                    if j % 2 == 0:
                        S.add("act", lambda e, h=h, hg=hg, j=j, kc=kc: e.activation(
                            out=hT[:, h, kc, :], in_=pt[:, hg, j, :], func=AF.Copy, scale=gT[:, kc:kc + 1]),
                            reads=[("pt", hg), "gT"], writes=[("hT", h)])
                    else:
                        S.add("dve", lambda e, h=h, hg=hg, j=j, kc=kc: e.tensor_scalar(
                            out=hT[:, h, kc, :], in0=pt[:, hg, j, :], scalar1=gT[:, kc:kc + 1], scalar2=None, op0=ALU.mult),
                            reads=[("pt", hg), "gT"], writes=[("hT", h)])
            S.add("sp", lambda e, tt=tt, h=h: e.dma_start(
                out=h2T_out[:, tt * 128:(tt + 1) * 128].rearrange("(kc p) t -> p kc t", p=128), in_=hT[:, h, :, :]),
                reads=[("hT", h)], dma=True)

        part1(0)
        for tt in range(1, NTT):
            part1(tt)
            part2(tt - 1)
        part2(NTT - 1)
        S.run_block()


def phase_O2(nc, S, x1_in, h2T, w_up, w_down, x_out, g_final=None):
    HT = TOK // 2
    NFB = DFF // 512
    NT = HT // 128
    with ExitStack() as es:
        x1 = es.enter_context(nc.sbuf_tensor(_uid(nc) + "o2_x1", [128, NT, D], F32))
        hT = es.enter_context(nc.sbuf_tensor(_uid(nc) + "o2_hT", [128, KC, HT], BF16))
        wu = es.enter_context(nc.sbuf_tensor(_uid(nc) + "o2_wu", [128, 2, KC, 512], BF16))
        wd = es.enter_context(nc.sbuf_tensor(_uid(nc) + "o2_wd", [128, 2, 4, D], BF16))
        aT = es.enter_context(nc.sbuf_tensor(_uid(nc) + "o2_aT", [128, 2, 4, HT], BF16))
        gB = es.enter_context(nc.sbuf_tensor(_uid(nc) + "o2_gB", [128, D], F32))
        junk = es.enter_context(nc.sbuf_tensor(_uid(nc) + "o2_junk", [128, D], BF16))
        st = es.enter_context(nc.sbuf_tensor(_uid(nc) + "o2_st", [128, 4], F32))
        pu = es.enter_context(nc.psum_tensor(_uid(nc) + "o2_pu", [128, 2, 512], F32))
        pd = es.enter_context(nc.psum_tensor(_uid(nc) + "o2_pd", [128, 4, 512], F32))
        if g_final is not None:
            S.add("sp", lambda e: e.dma_start(out=gB[:], in_=g_final.partition_broadcast(128)), writes=["gB"], dma=True)

        def load_hT(hf):
            t0 = hf * HT
            for q in range(4):
                S.add("sp", lambda e, q=q, t0=t0: e.dma_start(
                    out=hT[:, q * 4:(q + 1) * 4, :],
                    in_=h2T[q * 512:(q + 1) * 512, t0:t0 + HT].rearrange("(kc p) t -> p kc t", p=128)),
                    writes=[("hT", q)], dma=True)

        def load_x1_pair(hf, q):
            t0 = hf * HT
            S.add("sp", lambda e, q=q, t0=t0: e.dma_start(
                out=x1[:, 2 * q:2 * q + 2, :],
                in_=x1_in[t0 + q * 256:t0 + (q + 1) * 256, :].rearrange("(t p) d -> p t d", p=128)),
                writes=[("x1", 2 * q), ("x1", 2 * q + 1)], dma=True)

        def load_w(fb):
            b = fb % 2
            for q in range(2):
                S.add("pool", lambda e, fb=fb, b=b, q=q: e.dma_start(
                    out=wu[:, b, q * 8:(q + 1) * 8, :],
                    in_=w_up[q * 1024:(q + 1) * 1024, fb * 512:(fb + 1) * 512].rearrange("(kc p) n -> p kc n", p=128)),
                    writes=[("wu", b, q)], dma=True)
            S.add("pool", lambda e, fb=fb, b=b: e.dma_start(
                out=wd[:, b, :, :],
                in_=w_down[fb * 512:(fb + 1) * 512, :].rearrange("(fc p) n -> p fc n", p=128)),
                writes=[("wd", b)], dma=True)

        def up(fb):
            b = fb % 2
            n = 0
            for fc in range(4):
                for tg in range(HT // 512):
                    slot = n % 2
                    n += 1
                    for kc in range(KC):
                        S.add("pe", lambda e, b=b, fc=fc, tg=tg, kc=kc, slot=slot: e.matmul(
                            pu[:, slot, :], lhsT=wu[:, b, kc, fc * 128:(fc + 1) * 128],
                            rhs=hT[:, kc, tg * 512:(tg + 1) * 512], start=(kc == 0), stop=(kc == KC - 1)),
                            reads=[("wu", b, kc // 8), ("hT", kc // 4)], writes=[("pu", slot)], rhs=[("hT", kc // 4)])
                    S.add("act", lambda e, b=b, fc=fc, tg=tg, slot=slot: e.activation(
                        out=aT[:, b, fc, tg * 512:(tg + 1) * 512], in_=pu[:, slot, :], func=AF.Relu),
                        reads=[("pu", slot)], writes=[("aT", b, fc)])
                    S.add("pool", lambda e, b=b, fc=fc, tg=tg: e.tensor_tensor(
                        out=aT[:, b, fc, tg * 512:(tg + 1) * 512], in0=aT[:, b, fc, tg * 512:(tg + 1) * 512],
                        in1=aT[:, b, fc, tg * 512:(tg + 1) * 512], op=ALU.mult),
                        reads=[("aT", b, fc)], writes=[("aT", b, fc)])

        def finish_tile(hf, tt):
            t0 = hf * HT
            if g_final is not None:
                b = tt % 2
                rms_rstd(S, "act", x1[:, tt, :], junk[:], st[:, b:b + 1], st[:, 2 + b:3 + b], D, [("x1", tt)], "o2n%d" % b)
                S.add("dve", lambda e, tt=tt, b=b: e.scalar_tensor_tensor(
                    out=x1[:, tt, :], in0=x1[:, tt, :], scalar=st[:, 2 + b:3 + b], in1=gB[:], op0=ALU.mult, op1=ALU.mult),
                    reads=[("x1", tt), "o2n%d_rstd" % b, "gB"], writes=[("x1", tt)])
            S.add("sp", lambda e, tt=tt, t0=t0: e.dma_start(out=x_out[t0 + tt * 128:t0 + (tt + 1) * 128, :], in_=x1[:, tt, :]),
                  reads=[("x1", tt)], dma=True)

        def down(hf, fb, last):
            b = fb % 2
            n = 0
            for tt in range(NT):
                for cg in range(4):
                    slot = n % 4
                    n += 1
                    for fc in range(4):
                        S.add("pe", lambda e, b=b, tt=tt, cg=cg, fc=fc, slot=slot: e.matmul(
                            pd[:, slot, :], lhsT=aT[:, b, fc, tt * 128:(tt + 1) * 128],
                            rhs=wd[:, b, fc, cg * 512:(cg + 1) * 512], start=(fc == 0), stop=(fc == 3)),
                            reads=[("aT", b, fc), ("wd", b)], writes=[("pd", slot)], rhs=[("wd", b)])
                    S.add("dve", lambda e, tt=tt, cg=cg, slot=slot: e.tensor_tensor(
                        out=x1[:, tt, cg * 512:(cg + 1) * 512], in0=pd[:, slot, :],
                        in1=x1[:, tt, cg * 512:(cg + 1) * 512], op=ALU.add),
                        reads=[("pd", slot), ("x1", tt)], writes=[("x1", tt)])
                if last:
                    finish_tile(hf, tt)
                    if hf == 0 and tt % 2 == 1:
                        load_x1_pair(1, tt // 2)

        load_hT(0)
        load_w(0)
        for q in range(4):
            load_x1_pair(0, q)
        up(0)
        for hf in range(2):
            for fb in range(NFB):
                last = (fb == NFB - 1)
                if not last:
                    load_w(fb + 1)
                    up(fb + 1)
                elif hf == 0:
                    load_hT(1)
                    load_w(0)
                    up(0)
                down(hf, fb, last)
        S.run_block()


SEQ = 8192
WIN_C = 1408
QRH = 640
KRH = 576
QA = (0, 192)
QB_ROW = 384
QC_ROW = 512
KROPE_ROW = 256
KB_ROW = 320
KC_ROW = 448


def add_ag(S, src, dst, rkeys, wkeys):
    S.add("pool", lambda e: e.collective_compute(
        "AllGather", ALU.bypass, replica_groups=[[0, 1, 2, 3], [4, 5, 6, 7]], ins=[src], outs=[dst]),
        reads=rkeys, writes=wkeys, cc=True)


def phase_P1(nc, S, x_in, gT_attn, hT_out, hT_gath):
    with ExitStack() as es:
        xt = es.enter_context(nc.sbuf_tensor(_uid(nc) + "p1_xt", [128, NTT, D], F32))
        xn = es.enter_context(nc.sbuf_tensor(_uid(nc) + "p1_xn", [128, 2, D], BF16))
        junk = es.enter_context(nc.sbuf_tensor(_uid(nc) + "p1_junk", [128, D], BF16))
        hT = es.enter_context(nc.sbuf_tensor(_uid(nc) + "p1_hT", [128, 2, KC, 256], BF16))
        gT = es.enter_context(nc.sbuf_tensor(_uid(nc) + "p1_gT", [128, KC], F32))
        ss = es.enter_context(nc.sbuf_tensor(_uid(nc) + "p1_ss", [128, NTT], F32))
        rs = es.enter_context(nc.sbuf_tensor(_uid(nc) + "p1_rs", [128, NTT], F32))
        ident_f = es.enter_context(nc.sbuf_tensor(_uid(nc) + "p1_identf", [128, 128], F32))
        ident = es.enter_context(nc.sbuf_tensor(_uid(nc) + "p1_ident", [128, 128], BF16))
        pt = es.enter_context(nc.psum_tensor(_uid(nc) + "p1_pt", [128, 2, 8, 128], BF16))
        make_ident(S, ident, ident_f)
        S.add("sp", lambda e: e.dma_start(out=gT[:], in_=gT_attn), writes=["gT"], dma=True)
        for tt in range(NTT):
            S.add("sp", lambda e, tt=tt: e.dma_start(out=xt[:, tt, :], in_=x_in[tt * 128:(tt + 1) * 128, :]),
                  writes=[("xt", tt)], dma=True)
        for tt in range(NTT):
            S.add("act", lambda e, tt=tt: e.activation(out=junk[:], in_=xt[:, tt, :], func=AF.Square, accum_out=ss[:, tt:tt + 1]),
                  reads=[("xt", tt)], writes=["junk", ("ss", tt)])
        S.add("dve", lambda e: e.tensor_scalar(out=rs[:], in0=ss[:], scalar1=1.0 / D, scalar2=EPS, op0=ALU.mult, op1=ALU.add),
              reads=[("ss", tt) for tt in range(NTT)], writes=["rs"])
        S.add("act", lambda e: e.activation(out=rs[:], in_=rs[:], func=AF.Sqrt), reads=["rs"], writes=["rs"])
        S.add("dve", lambda e: e.reciprocal(out=rs[:], in_=rs[:]), reads=["rs"], writes=["rs"])
        for tt in range(NTT):
            b = tt % 2
            cb = (tt // 2) % 2
            co = (tt % 2) * 128
            S.add("act", lambda e, tt=tt, b=b: e.activation(out=xn[:, b, :], in_=xt[:, tt, :], func=AF.Copy, scale=rs[:, tt:tt + 1]),
                  reads=[("xt", tt), "rs"], writes=[("xn", b)])
            for hg in range(2):
                for j in range(8):
                    kc = hg * 8 + j
                    S.add("pe", lambda e, b=b, hg=hg, j=j, kc=kc: e.transpose(
                        out=pt[:, hg, j, :], in_=xn[:, b, kc * 128:(kc + 1) * 128], identity=ident[:]),
                        reads=[("xn", b), "ident"], writes=[("pt", hg)])
                for j in range(8):
                    kc = hg * 8 + j
                    if j % 2 == 0:
                        S.add("act", lambda e, cb=cb, co=co, hg=hg, j=j, kc=kc: e.activation(
                            out=hT[:, cb, kc, co:co + 128], in_=pt[:, hg, j, :], func=AF.Copy, scale=gT[:, kc:kc + 1]),
                            reads=[("pt", hg), "gT"], writes=[("hT", cb, tt % 2)])
                    else:
                        S.add("dve", lambda e, cb=cb, co=co, hg=hg, j=j, kc=kc: e.tensor_scalar(
                            out=hT[:, cb, kc, co:co + 128], in0=pt[:, hg, j, :], scalar1=gT[:, kc:kc + 1], scalar2=None, op0=ALU.mult),
                            reads=[("pt", hg), "gT"], writes=[("hT", cb, tt % 2)])
            if tt % 2 == 1:
                ch = tt // 2
                S.add("sp", lambda e, ch=ch, cb=cb: e.dma_start(
                    out=hT_out[ch].rearrange("(kc p) t -> p kc t", p=128), in_=hT[:, cb, :, :]),
                    reads=[("hT", cb, 0), ("hT", cb, 1)], writes=[("hTc", ch)], dma=True)
                add_ag(S, hT_out[ch], hT_gath[ch], [("hTc", ch)], [("hTg", id(hT_gath[ch]))])
        S.run_block()


def phase_P2h(nc, S, hT_all, w_in_c, qagT, kvagT, w_uq_c, w_ukv_c, ropeC, ropeS, qT_out, kT_out, v_out):
    GT = 1024
    NG = SEQ // GT
    with ExitStack() as es:
        hT = es.enter_context(nc.sbuf_tensor(_uid(nc) + "p2_hT", [128, 2, KC, GT], BF16))
        wb = es.enter_context(nc.sbuf_tensor(_uid(nc) + "p2_w", [128, KC, WIN_C], BF16))
        wuq = es.enter_context(nc.sbuf_tensor(_uid(nc) + "p2_wuq", [128, 3, 512], BF16))
        wukv = es.enter_context(nc.sbuf_tensor(_uid(nc) + "p2_wukv", [128, 512], BF16))
        cnT = es.enter_context(nc.sbuf_tensor(_uid(nc) + "p2_cnT", [128, 2, 4, GT], BF16))
        junk = es.enter_context(nc.sbuf_tensor(_uid(nc) + "p2_junk", [128, 512], BF16))
        cq = es.enter_context(nc.sbuf_tensor(_uid(nc) + "p2_cq", [128, 2, 512], F32))
        cn = es.enter_context(nc.sbuf_tensor(_uid(nc) + "p2_cn", [128, 2, 512], BF16))
        stage = es.enter_context(nc.sbuf_tensor(_uid(nc) + "p2_stage", [128, 4, 512], BF16))
        rt = es.enter_context(nc.sbuf_tensor(_uid(nc) + "p2_rt", [64, 2, 2, 512], F32))
        rC = es.enter_context(nc.sbuf_tensor(_uid(nc) + "p2_rC", [64, 2, GT], F32))
        rS = es.enter_context(nc.sbuf_tensor(_uid(nc) + "p2_rS", [64, 2, GT], F32))
        qag = es.enter_context(nc.sbuf_tensor(_uid(nc) + "p2_qag", [128, 4], F32))
        st = es.enter_context(nc.sbuf_tensor(_uid(nc) + "p2_st", [128, 16], F32))
        ident_f = es.enter_context(nc.sbuf_tensor(_uid(nc) + "p2_identf", [128, 128], F32))
        ident = es.enter_context(nc.sbuf_tensor(_uid(nc) + "p2_ident", [128, 128], BF16))
        pm = es.enter_context(nc.psum_tensor(_uid(nc) + "p2_pm", [128, 4, 512], F32))
        pt = es.enter_context(nc.psum_tensor(_uid(nc) + "p2_pt", [128, 2, 4, 128], BF16))
        make_ident(S, ident, ident_f)
        S.add("sp", lambda e: e.dma_start(out=qag[:, 0:3], in_=qagT), writes=["qag"], dma=True)
        S.add("sp", lambda e: e.dma_start(out=qag[:, 3:4], in_=kvagT), reads=["qag"], writes=["qag"], dma=True)
        for q in range(4):
            S.add("pool", lambda e, q=q: e.dma_start(
                out=wb[:, q * 4:(q + 1) * 4, :], in_=w_in_c[q * 512:(q + 1) * 512, :].rearrange("(kc p) n -> p kc n", p=128)),
                writes=[("wb", q)], dma=True)
        S.add("pool", lambda e: e.dma_start(out=wuq[:], in_=w_uq_c.rearrange("(kc p) n -> p kc n", p=128)), writes=["wuq"], dma=True)
        S.add("pool", lambda e: e.dma_start(out=wukv[:], in_=w_ukv_c), writes=["wukv"], dma=True)
        wb_all = [("wb", q) for q in range(4)]

        cnt = {"pm": 0, "st": 0, "ev": 0}

        def pm_slot():
            s = cnt["pm"] % 4
            cnt["pm"] += 1
            return s

        def evac_store(ps_ap, dram_ap, nrows, ncols, slot_pm):
            ss = cnt["st"] % 4
            cnt["st"] += 1
            eng = "act" if cnt["ev"] % 2 == 0 else "dve"
            cnt["ev"] += 1
            if eng == "act":
                S.add("act", lambda e, ss=ss: e.activation(out=stage[0:nrows, ss, 0:ncols], in_=ps_ap, func=AF.Copy),
                      reads=[("pm", slot_pm)], writes=[("stage", ss)])
            else:
                S.add("dve", lambda e, ss=ss: e.tensor_copy(out=stage[0:nrows, ss, 0:ncols], in_=ps_ap),
                      reads=[("pm", slot_pm)], writes=[("stage", ss)])
            S.add("sp", lambda e, ss=ss: e.dma_start(out=dram_ap, in_=stage[0:nrows, ss, 0:ncols]),
                  reads=[("stage", ss)], dma=True)

        def rope_store(psA, psB, slotA, slotB, hb, tg, dram_ap):
            r = cnt["st"] % 2
            ss = cnt["st"] % 4
            cnt["st"] += 1
            S.add("dve", lambda e, r=r: e.tensor_tensor(out=rt[:, r, 0, :], in0=psA, in1=rC[:, hb, tg * 512:(tg + 1) * 512], op=ALU.mult),
                  reads=[("pm", slotA), ("rC", hb)], writes=[("rt", r, 0)])
            S.add("dve", lambda e, r=r: e.tensor_tensor(out=rt[:, r, 1, :], in0=psB, in1=rS[:, hb, tg * 512:(tg + 1) * 512], op=ALU.mult),
                  reads=[("pm", slotB), ("rS", hb)], writes=[("rt", r, 1)])
            S.add("pool", lambda e, r=r, ss=ss: e.tensor_tensor(out=stage[0:64, ss, :], in0=rt[:, r, 0, :], in1=rt[:, r, 1, :], op=ALU.add),
                  reads=[("rt", r, 0), ("rt", r, 1)], writes=[("stage", ss)])
            S.add("sp", lambda e, ss=ss: e.dma_start(out=dram_ap, in_=stage[0:64, ss, :]),
                  reads=[("stage", ss)], dma=True)

        def load_group(g, hb):
            r, half = divmod(g, 2)
            for c4 in range(4):
                ch = 4 * half + c4
                for q in range(4):
                    S.add("sp", lambda e, hb=hb, r=r, ch=ch, c4=c4, q=q: e.dma_start(
                        out=hT[:, hb, q * 4:(q + 1) * 4, c4 * 256:(c4 + 1) * 256],
                        in_=hT_all[ch][r * D + q * 512:r * D + (q + 1) * 512, :].rearrange("(kc p) t -> p kc t", p=128)),
                        reads=[("hTg", id(hT_all[ch]))], writes=[("hT", hb, q, c4)], dma=True)
            S.add("sp", lambda e, hb=hb, g=g: e.dma_start(out=rC[:, hb, :], in_=ropeC[:, g * GT:(g + 1) * GT]), writes=[("rC", hb)], dma=True)
            S.add("sp", lambda e, hb=hb, g=g: e.dma_start(out=rS[:, hb, :], in_=ropeS[:, g * GT:(g + 1) * GT]), writes=[("rS", hb)], dma=True)

        gorder = [0, 2, 4, 6, 1, 3, 5, 7]
        load_group(gorder[0], 0)
        for gi, g in enumerate(gorder):
            hb = gi % 2
            t0 = g * GT
            if gi + 1 < NG:
                load_group(gorder[gi + 1], (gi + 1) % 2)
            hk = [("hT", hb, q) for q in range(4)]
            def g0_part1(tt):
                b = tt % 2
                sl = pm_slot()
                for kc in range(KC):
                    S.add("pe", lambda e, hb=hb, tt=tt, kc=kc, sl=sl: e.matmul(
                        pm[:, sl, :], lhsT=hT[:, hb, kc, tt * 128:(tt + 1) * 128], rhs=wb[:, kc, 0:512],
                        start=(kc == 0), stop=(kc == KC - 1)),
                        reads=[("hT", hb, kc // 4, x) for x in range(4)] + [("wb", kc // 4)], writes=[("pm", sl)], rhs=[("wb", kc // 4)])
                S.add("dve", lambda e, b=b, sl=sl: e.tensor_copy(out=cq[:, b, :], in_=pm[:, sl, :]),
                      reads=[("pm", sl)], writes=[("cq", b)])

            def g0_part2(tt):
                b = tt % 2
                rms_rstd(S, "act", cq[:, b, 0:384], junk[:, 0:384], st[:, 4 + b:5 + b], st[:, 6 + b:7 + b], 384, [("cq", b)], "pq%d" % b)
                rms_rstd(S, "act", cq[:, b, 384:512], junk[:, 384:512], st[:, 8 + b:9 + b], st[:, 10 + b:11 + b], 128, [("cq", b)], "pk%d" % b)
                S.add("act", lambda e, b=b: e.activation(out=cn[:, b, 0:384], in_=cq[:, b, 0:384], func=AF.Copy, scale=st[:, 6 + b:7 + b]),
                      reads=[("cq", b), "pq%d_rstd" % b], writes=[("cn", b, 0)])
                S.add("act", lambda e, b=b: e.activation(out=cn[:, b, 384:512], in_=cq[:, b, 384:512], func=AF.Copy, scale=st[:, 10 + b:11 + b]),
                      reads=[("cq", b), "pk%d_rstd" % b], writes=[("cn", b, 1)])
                hg = tt % 2
                for j in range(4):
                    S.add("pe", lambda e, b=b, hg=hg, j=j: e.transpose(
                        out=pt[:, hg, j, :], in_=cn[:, b, j * 128:(j + 1) * 128], identity=ident[:]),
                        reads=[("cn", b, 0), ("cn", b, 1), "ident"], writes=[("pt", hg)])
                for j in range(4):
                    S.add("act", lambda e, hb=hb, tt=tt, hg=hg, j=j: e.activation(
                        out=cnT[:, hb, j, tt * 128:(tt + 1) * 128], in_=pt[:, hg, j, :], func=AF.Copy, scale=qag[:, j:j + 1]),
                        reads=[("pt", hg), "qag"], writes=[("cnT", hb, tt)])

            ntt = GT // 128
            g0_part1(0)
            for tt in range(1, ntt):
                g0_part1(tt)
                g0_part2(tt - 1)
            g0_part2(ntt - 1)
            ck = [("cnT", hb, tt) for tt in range(GT // 128)]
            for tg in range(GT // 512):
                c0t = t0 + tg * 512
                slA = pm_slot()
                slB = pm_slot()
                for v_, sl in ((0, slA), (1, slB)):
                    for kc in range(KC):
                        S.add("pe", lambda e, hb=hb, tg=tg, kc=kc, sl=sl, v_=v_: e.matmul(
                            pm[0:64, sl, :], lhsT=wb[:, kc, 512 + 64 * v_:576 + 64 * v_], rhs=hT[:, hb, kc, tg * 512:(tg + 1) * 512],
                            start=(kc == 0), stop=(kc == KC - 1)),
                            reads=[("hT", hb, kc // 4, x) for x in range(4)] + [("wb", kc // 4)], writes=[("pm", sl)])
                rope_store(pm[0:64, slA, :], pm[0:64, slB, :], slA, slB, hb, tg, kT_out[KROPE_ROW:KROPE_ROW + 64, c0t:c0t + 512])
                for fi, (dst, r0) in enumerate(((qT_out, QB_ROW), (kT_out, KB_ROW), (qT_out, QC_ROW), (kT_out, KC_ROW))):
                    sl = pm_slot()
                    for kc in range(KC):
                        S.add("pe", lambda e, hb=hb, tg=tg, kc=kc, sl=sl, fi=fi: e.matmul(
                            pm[:, sl, :], lhsT=wb[:, kc, 640 + fi * 128:640 + (fi + 1) * 128], rhs=hT[:, hb, kc, tg * 512:(tg + 1) * 512],
                            start=(kc == 0), stop=(kc == KC - 1)),
                            reads=[("hT", hb, kc // 4, x) for x in range(4)] + [("wb", kc // 4)], writes=[("pm", sl)])
                    evac_store(pm[:, sl, :], dst[r0:r0 + 128, c0t:c0t + 512], 128, 512, sl)
                for hh in range(2):
                    sl = pm_slot()
                    for kc in range(3):
                        S.add("pe", lambda e, hb=hb, tg=tg, kc=kc, sl=sl, hh=hh: e.matmul(
                            pm[:, sl, :], lhsT=wuq[:, kc, hh * 256:hh * 256 + 128], rhs=cnT[:, hb, kc, tg * 512:(tg + 1) * 512],
                            start=(kc == 0), stop=(kc == 2)),
                            reads=ck[tg * 4:(tg + 1) * 4] + ["wuq"], writes=[("pm", sl)])
                    evac_store(pm[:, sl, :], qT_out[hh * 192:hh * 192 + 128, c0t:c0t + 512], 128, 512, sl)
                    slA = pm_slot()
                    slB = pm_slot()
                    for v_, sl in ((0, slA), (1, slB)):
                        for kc in range(3):
                            S.add("pe", lambda e, hb=hb, tg=tg, kc=kc, sl=sl, hh=hh, v_=v_: e.matmul(
                                pm[0:64, sl, :], lhsT=wuq[:, kc, hh * 256 + 128 + 64 * v_:hh * 256 + 192 + 64 * v_],
                                rhs=cnT[:, hb, kc, tg * 512:(tg + 1) * 512], start=(kc == 0), stop=(kc == 2)),
                                reads=ck[tg * 4:(tg + 1) * 4] + ["wuq"], writes=[("pm", sl)])
                    rope_store(pm[0:64, slA, :], pm[0:64, slB, :], slA, slB, hb, tg, qT_out[hh * 192 + 128:hh * 192 + 192, c0t:c0t + 512])
                    sl = pm_slot()
                    S.add("pe", lambda e, hb=hb, tg=tg, sl=sl, hh=hh: e.matmul(
                        pm[:, sl, :], lhsT=wukv[:, hh * 128:(hh + 1) * 128], rhs=cnT[:, hb, 3, tg * 512:(tg + 1) * 512],
                        start=True, stop=True),
                        reads=ck[tg * 4:(tg + 1) * 4] + ["wukv"], writes=[("pm", sl)])
                    evac_store(pm[:, sl, :], kT_out[hh * 128:(hh + 1) * 128, c0t:c0t + 512], 128, 512, sl)
            for tt in range(GT // 128):
                sl = pm_slot()
                S.add("pe", lambda e, hb=hb, tt=tt, sl=sl: e.matmul(
                    pm[:, sl, 0:256], lhsT=cnT[:, hb, 3, tt * 128:(tt + 1) * 128], rhs=wukv[:, 256:512], start=True, stop=True),
                    reads=[("cnT", hb, tt), "wukv"], writes=[("pm", sl)])
                for kc in range(KC):
                    S.add("pe", lambda e, hb=hb, tt=tt, kc=kc, sl=sl: e.matmul(
                        pm[:, sl, 256:512], lhsT=hT[:, hb, kc, tt * 128:(tt + 1) * 128], rhs=wb[:, kc, 1152:1408],
                        start=(kc == 0), stop=(kc == KC - 1)),
                        reads=[("hT", hb, kc // 4, x) for x in range(4)] + [("wb", kc // 4)], writes=[("pm", sl)])
                evac_store(pm[:, sl, :], v_out[t0 + tt * 128:t0 + (tt + 1) * 128, :], 128, 512, sl)
        S.run_block()


def phase_ATT(nc, S, qT, kT, v, kaug, qaug, maskA, corrB, biasC, maskC, lamv, sublnT, lam_init, mixT_out, mix_gath):
    NB = SEQ // 512
    scA = float(192 ** -0.5)
    scB = 0.125
    scC = float(128 ** -0.5)
    with ExitStack() as es:
        T = lambda n, s, dt: es.enter_context(nc.sbuf_tensor(_uid(nc) + n, s, dt))
        vsb = T("a_v", [128, SEQ // 128, 512], BF16)
        kbuf = T("a_k", [128, 2, SEQ], BF16)
        krope = T("a_kr", [64, SEQ], BF16)
        qn = T("a_qn", [128, 3, 512], BF16)
        qr = T("a_qr", [64, 3, 512], BF16)
        pT = T("a_pT", [128, 6, 512], BF16)
        acc = T("a_acc", [128, 2, 512], F32)
        rec = T("a_rec", [128, 2, 512], F32)
        dsb = T("a_dsb", [128, 2, 512], F32)
        acc2 = T("a_acc2", [128, 2, 512], F32)
        ones5 = T("a_ones5", [128, 512], F32)
        ones_b = T("a_onesb", [128, 128], BF16)
        mhalf = T("a_mhalf", [128, 512], F32)
        ost = T("a_ost", [128, 2, 512], BF16)
        tmp0 = T("a_tmp0", [128, 512], F32)
        t1 = T("a_t1", [128, 512], F32)
        ob_ = T("a_o", [128, 512], F32)
        sq = T("a_sq", [128, 512], F32)
        tS = T("a_tS", [128, 2, 128], F32)
        bm = T("a_bm", [128, 5, 128], F32)
        mC = T("a_mC", [128, 2, 128], F32)
        mA = T("a_mA", [128, 128], BF16)
        cB = T("a_cB", [128, 128], BF16)
        ident_f = T("a_identf", [128, 128], F32)
        ident = T("a_ident", [128, 128], BF16)
        ones_f = T("a_ones", [128, 128], F32)
        lv = T("a_lv", [128, 4, 64], F32)
        lt = T("a_lt", [128, 2, 64], F32)
        ls = T("a_ls", [128, 8], F32)
        sg = T("a_sg", [128, 2], F32)
        ps_s = es.enter_context(nc.psum_tensor(_uid(nc) + "a_ps_s", [128, 4, 512], F32))
        ps_o = es.enter_context(nc.psum_tensor(_uid(nc) + "a_ps_o", [128, 2, 512], F32))
        ps_d = es.enter_context(nc.psum_tensor(_uid(nc) + "a_ps_d", [128, 2, 512], F32))

        make_ident(S, ident, ident_f)
        S.add("pool", lambda e: e.memset(ones_f[:], 1.0), writes=["ones"])
        S.add("pool", lambda e: e.memset(ones5[:], -1.0), writes=["ones5"])
        S.add("dve", lambda e: e.tensor_copy(out=ones_b[:], in_=ones_f[:]), reads=["ones"], writes=["onesb"])
        S.add("pool", lambda e: e.memset(mhalf[:], -0.5), writes=["mhalf"])
        S.add("sp", lambda e: e.dma_start(out=mA[:], in_=maskA), writes=["mA"], dma=True)
        S.add("sp", lambda e: e.dma_start(out=cB[:], in_=corrB), writes=["cB"], dma=True)
        S.add("sp", lambda e: e.dma_start(out=bm[:], in_=biasC.rearrange("d k q -> k d q")), writes=["bm"], dma=True)
        S.add("sp", lambda e: e.dma_start(out=mC[:], in_=maskC.rearrange("d k q -> k d q")), writes=["mC"], dma=True)
        S.add("sp", lambda e: e.dma_start(out=lv[:], in_=lamv.partition_broadcast(128)), writes=["lv"], dma=True)
        S.add("sp", lambda e: e.dma_start(out=sg[:, 0:1], in_=sublnT), writes=["sg0"], dma=True)
        S.add("sp", lambda e: e.dma_start(out=vsb[:], in_=v.rearrange("(kt p) c -> p kt c", p=128)), writes=["vsb"], dma=True)
        S.add("sp", lambda e: e.dma_start(out=krope[:], in_=kT[KROPE_ROW:KROPE_ROW + 64, :]), writes=["krope"], dma=True)
        S.add("dve", lambda e: e.tensor_tensor(out=bm[:, 0, :], in0=bm[:, 0, :], in1=mC[:, 0, :], op=ALU.add), reads=["bm", "mC"], writes=["bm"])
        S.add("dve", lambda e: e.tensor_tensor(out=bm[:, 4, :], in0=bm[:, 4, :], in1=mC[:, 1, :], op=ALU.add), reads=["bm", "mC"], writes=["bm"])
        for i in range(2):
            S.add("dve", lambda e, i=i: e.tensor_tensor(out=lt[:, i, :], in0=lv[:, 2 * i, :], in1=lv[:, 2 * i + 1, :], op=ALU.mult),
                  reads=["lv"], writes=[("lt", i)])
            S.add("dve", lambda e, i=i: e.reduce_sum(out=ls[:, i:i + 1], in_=lt[:, i, :], axis=AX.X), reads=[("lt", i)], writes=[("ls", i)])
            S.add("act", lambda e, i=i: e.activation(out=ls[:, 2 + i:3 + i], in_=ls[:, i:i + 1], func=AF.Exp), reads=[("ls", i)], writes=[("le", i)])
        S.add("dve", lambda e: e.tensor_tensor(out=ls[:, 4:5], in0=ls[:, 3:4], in1=ls[:, 2:3], op=ALU.subtract), reads=[("le", 0), ("le", 1)], writes=["nl0"])
        S.add("dve", lambda e: e.tensor_scalar(out=ls[:, 5:6], in0=ls[:, 4:5], scalar1=-float(lam_init), scalar2=None, op0=ALU.add), reads=["nl0"], writes=["nlam"])
        S.add("dve", lambda e: e.tensor_scalar(out=sg[:, 1:2], in0=sg[:, 0:1], scalar1=float(1.0 - lam_init), scalar2=None, op0=ALU.mult), reads=["sg0"], writes=["sgain"])
        nlam = ls[:, 5:6]
        sgain = sg[:, 1:2]

        def load_k(kind, slot):
            if kind in ("A0", "A1"):
                r0 = 0 if kind == "A0" else 128
                for q in range(2):
                    S.add("sp", lambda e, q=q, r0=r0, slot=slot: e.dma_start(
                        out=kbuf[:, slot, q * 4096:(q + 1) * 4096], in_=kT[r0:r0 + 128, q * 4096:(q + 1) * 4096]),
                        writes=[("kbuf", slot, q)], dma=True)
            elif kind in ("B0", "B1"):
                n = 0 if kind == "B0" else 1
                for q in range(2):
                    S.add("sp", lambda e, q=q, n=n, slot=slot: e.dma_start(
                        out=kbuf[0:64, slot, q * 4096:(q + 1) * 4096], in_=kT[KB_ROW + 64 * n:KB_ROW + 64 * n + 64, q * 4096:(q + 1) * 4096]),
                        writes=[("kbuf", slot, q)], dma=True)
                S.add("sp", lambda e, slot=slot: e.dma_start(out=kbuf[64:68, slot, :], in_=kaug),
                      reads=[("kbuf", slot, 0), ("kbuf", slot, 1)], writes=[("kbuf", slot, 0), ("kbuf", slot, 1)], dma=True)
            else:
                for q in range(2):
                    S.add("sp", lambda e, q=q, slot=slot: e.dma_start(
                        out=kbuf[:, slot, q * 4096:(q + 1) * 4096], in_=kT[KC_ROW:KC_ROW + 128, q * 4096:(q + 1) * 4096]),
                        writes=[("kbuf", slot, q)], dma=True)

        qcnt = [0]

        def load_q(kind, I):
            qs = qcnt[0] % 3
            qcnt[0] += 1
            c = slice(I * 512, (I + 1) * 512)
            if kind in ("A0", "A1"):
                r0 = 0 if kind == "A0" else 192
                S.add("sp", lambda e: e.dma_start(out=qn[:, qs, :], in_=qT[r0:r0 + 128, c]), writes=[("q", qs)], dma=True)
                S.add("sp", lambda e: e.dma_start(out=qr[:, qs, :], in_=qT[r0 + 128:r0 + 192, c]), writes=[("qr", qs)], dma=True)
            elif kind in ("B0", "B1"):
                n = 0 if kind == "B0" else 1
                S.add("sp", lambda e: e.dma_start(out=qn[0:64, qs, :], in_=qT[QB_ROW + 64 * n:QB_ROW + 64 * n + 64, c]), writes=[("q", qs)], dma=True)
                S.add("sp", lambda e: e.dma_start(out=qn[64:68, qs, :], in_=qaug[:, c]), reads=[("q", qs)], writes=[("q", qs)], dma=True)
            else:
                S.add("sp", lambda e: e.dma_start(out=qn[:, qs, :], in_=qT[QC_ROW:QC_ROW + 128, c]), writes=[("q", qs)], dma=True)
            return qs

        blocks = []
        for I in range(NB):
            blocks.append(("A0", I, 0))
        for I in range(NB):
            blocks.append(("A1", I, 1))
        for I in range(NB):
            blocks.append(("B0", I, 0))
            blocks.append(("B1", I, 1))
        for I in range(NB):
            blocks.append(("C", I, 0))
        vcol = {"A0": 0, "A1": 128, "B0": 256, "B1": 256, "C": 384}
        orow = {"A0": 0, "A1": 128, "B1": 256, "C": 384}

        tiles = []
        for bi, (kind, I, ks) in enumerate(blocks):
            tl = []
            if kind == "C":
                for qi in range(4):
                    qt = 4 * I + qi
                    dl = [d for d in (4, 3, 2, 1, 0) if qt - d >= 0]
                    for d in dl:
                        tl.append(dict(kt=qt - d, c0=128 * qi, n=128, mask=None, bias=d, fc=(d == dl[0]), lc=(d == 0)))
            else:
                for kt in range(4 * I + 4):
                    t = kt - 4 * I
                    if t < 0:
                        tl.append(dict(kt=kt, c0=0, n=512, mask=None, bias=None, fc=(kt == 0), lc=False))
                    else:
                        tl.append(dict(kt=kt, c0=128 * t, n=512 - 128 * t, mask=True, bias=None, fc=(kt == 0), lc=False))
                tl[-1]["lc"] = True
            for ti, t in enumerate(tl):
                t.update(kind=kind, I=I, ks=ks, bi=bi, ob=bi % 2, ti=ti, first_blk=(ti == 0), last_blk=(ti == len(tl) - 1))
                tiles.append(t)
        N = len(tiles)
        for g, t in enumerate(tiles):
            t["ss"] = g % 4
            t["ps"] = g % 6
            t["tsr"] = g % 2

        blk_q = {}

        def start_block(bi):
            if bi >= len(blocks) or bi in blk_q:
                return
            kind, I, ks = blocks[bi]
            blk_q[bi] = load_q(kind, I)

        def emit_qk(g):
            t = tiles[g]
            kind, ks, ss, c0, n, kt = t["kind"], t["ks"], t["ss"], t["c0"], t["n"], t["kt"]
            qs = blk_q[t["bi"]]
            kq = t["kt"] // 32
            has_mask = t["mask"] is not None
            if kind in ("A0", "A1"):
                S.add("pe", lambda e: e.matmul(ps_s[:, ss, c0:c0 + n], lhsT=kbuf[:, ks, kt * 128:(kt + 1) * 128], rhs=qn[:, qs, c0:c0 + n],
                                               start=True, stop=False),
                      reads=[("kbuf", ks, kq), ("q", qs)], writes=[("ps_s", ss)], rhs=[("q", qs)])
                S.add("pe", lambda e: e.matmul(ps_s[:, ss, c0:c0 + n], lhsT=krope[:, kt * 128:(kt + 1) * 128], rhs=qr[:, qs, c0:c0 + n],
                                               start=False, stop=not has_mask),
                      reads=["krope", ("qr", qs)], writes=[("ps_s", ss)], rhs=[("qr", qs)])
                if has_mask:
                    S.add("pe", lambda e: e.matmul(ps_s[:, ss, c0:c0 + 128], lhsT=ident[:], rhs=mA[:], start=False, stop=True),
                          reads=["ident", "mA"], writes=[("ps_s", ss)])
            elif kind in ("B0", "B1"):
                S.add("pe", lambda e: e.matmul(ps_s[:, ss, c0:c0 + n], lhsT=kbuf[0:68, ks, kt * 128:(kt + 1) * 128], rhs=qn[0:68, qs, c0:c0 + n],
                                               start=True, stop=not has_mask),
                      reads=[("kbuf", ks, kq), ("q", qs)], writes=[("ps_s", ss)], rhs=[("q", qs)])
                if has_mask:
                    S.add("pe", lambda e: e.matmul(ps_s[:, ss, c0:c0 + 128], lhsT=ident[:], rhs=cB[:], start=False, stop=True),
                          reads=["ident", "cB"], writes=[("ps_s", ss)])
            else:
                S.add("pe", lambda e: e.matmul(ps_s[:, ss, c0:c0 + n], lhsT=kbuf[:, ks, kt * 128:(kt + 1) * 128], rhs=qn[:, qs, c0:c0 + n],
                                               start=True, stop=True),
                      reads=[("kbuf", ks, kq), ("q", qs)], writes=[("ps_s", ss)], rhs=[("q", qs)])

        def emit_exp(g):
            t = tiles[g]
            kind, ss, ps, c0, n = t["kind"], t["ss"], t["ps"], t["c0"], t["n"]
            if kind == "C":
                r = t["tsr"]
                d = t["bias"]
                S.add("dve", lambda e: e.scalar_tensor_tensor(out=tS[:, r, :], in0=ps_s[:, ss, c0:c0 + n], scalar=scC, in1=bm[:, d, :],
                                                              op0=ALU.mult, op1=ALU.add),
                      reads=[("ps_s", ss), "bm"], writes=[("tS", r)])
                S.add("act", lambda e: e.activation(out=pT[:, ps, c0:c0 + n], in_=tS[:, r, :], func=AF.Exp),
                      reads=[("tS", r)], writes=[("pT", ps)])
            else:
                sc = scA if kind in ("A0", "A1") else scB
                S.add("act", lambda e: e.activation(out=pT[:, ps, c0:c0 + n], in_=ps_s[:, ss, c0:c0 + n], func=AF.Exp, scale=sc),
                      reads=[("ps_s", ss)], writes=[("pT", ps)])

        def emit_pv(g):
            t = tiles[g]
            ps, c0, n, kt, ob = t["ps"], t["c0"], t["n"], t["kt"], t["ob"]
            vc = vcol[t["kind"]]
            S.add("pe", lambda e: e.matmul(ps_o[:, ob, c0:c0 + n], lhsT=vsb[:, kt, vc:vc + 128], rhs=pT[:, ps, c0:c0 + n],
                                           start=t["fc"], stop=t["lc"]),
                  reads=["vsb", ("pT", ps)], writes=[("ps_o", ob)], rhs=[("pT", ps)])

        def emit_acc(g):
            t = tiles[g]
            ps, c0, n, ob = t["ps"], t["c0"], t["n"], t["ob"]
            S.add("pe", lambda e: e.matmul(ps_d[:, ob, c0:c0 + n], lhsT=ones_b[:], rhs=pT[:, ps, c0:c0 + n],
                                           start=t["fc"], stop=t["lc"]),
                  reads=["onesb", ("pT", ps)], writes=[("ps_d", ob)], rhs=[("pT", ps)])

        def emit_epilogue(bi):
            kind, I, ks = blocks[bi]
            ob = bi % 2
            c = slice(I * 512, (I + 1) * 512)
            S.add("act", lambda e: e.activation(out=dsb[:, ob, :], in_=ps_d[:, ob, :], func=AF.Ln), reads=[("ps_d", ob)], writes=[("dsb", ob)])
            S.add("act", lambda e: e.activation(out=rec[:, ob, :], in_=dsb[:, ob, :], func=AF.Exp, scale=-1.0), reads=[("dsb", ob)], writes=[("rec", ob)])
            if kind in ("A0", "A1", "C"):
                S.add("dve", lambda e: e.tensor_tensor(out=ost[:, ob, :], in0=ps_o[:, ob, :], in1=rec[:, ob, :], op=ALU.mult),
                      reads=[("ps_o", ob), ("rec", ob)], writes=[("ost", ob)])
                r0 = orow[kind]
                S.add("pool", lambda e: e.dma_start(out=mixT_out[r0 // 64][:, c], in_=ost[0:64, ob, :]), reads=[("ost", ob)], writes=[("mixc", r0 // 64)], dma=True)
                S.add("pool", lambda e: e.dma_start(out=mixT_out[r0 // 64 + 1][:, c], in_=ost[64:128, ob, :]), reads=[("ost", ob)], writes=[("mixc", r0 // 64 + 1)], dma=True)
                if I == NB - 1:
                    for ch in (r0 // 64, r0 // 64 + 1):
                        add_ag(S, mixT_out[ch], mix_gath[ch], [("mixc", ch)], [("mixg", id(mix_gath[ch]))])
            elif kind == "B0":
                S.add("dve", lambda e: e.tensor_tensor(out=tmp0[:], in0=ps_o[:, ob, :], in1=rec[:, ob, :], op=ALU.mult),
                      reads=[("ps_o", ob), ("rec", ob)], writes=["tmp0"])
            else:
                S.add("dve", lambda e: e.tensor_tensor(out=t1[:], in0=ps_o[:, ob, :], in1=rec[:, ob, :], op=ALU.mult),
                      reads=[("ps_o", ob), ("rec", ob)], writes=["t1"])
                S.add("dve", lambda e: e.scalar_tensor_tensor(out=ob_[:], in0=t1[:], scalar=nlam, in1=tmp0[:], op0=ALU.mult, op1=ALU.add),
                      reads=["t1", "tmp0", "nlam"], writes=["o"])
                S.add("pool", lambda e: e.tensor_tensor(out=sq[:], in0=ob_[:], in1=ob_[:], op=ALU.mult), reads=["o"], writes=["sq"])
                S.add("pe", lambda e: e.matmul(ps_d[:, ob, :], lhsT=ones_f[:], rhs=sq[:], start=True, stop=True),
                      reads=["ones", "sq"], writes=[("ps_d", ob)])
                S.add("dve", lambda e: e.tensor_scalar(out=sq[:], in0=ps_d[:, ob, :], scalar1=1.0 / 128, scalar2=EPS, op0=ALU.mult, op1=ALU.add),
                      reads=[("ps_d", ob)], writes=["sq"])
                S.add("act", lambda e: e.activation(out=sq[:], in_=sq[:], func=AF.Ln), reads=["sq"], writes=["sq"])
                S.add("act", lambda e: e.activation(out=sq[:], in_=sq[:], func=AF.Exp, scale=-0.5), reads=["sq"], writes=["sq"])
                S.add("dve", lambda e: e.scalar_tensor_tensor(out=ost[:, ob, :], in0=ob_[:], scalar=sgain, in1=sq[:], op0=ALU.mult, op1=ALU.mult),
                      reads=["o", "sq", "sgain"], writes=[("ost", ob)])
                S.add("pool", lambda e: e.dma_start(out=mixT_out[4][:, c], in_=ost[0:64, ob, :]), reads=[("ost", ob)], writes=[("mixc", 4)], dma=True)
                S.add("pool", lambda e: e.dma_start(out=mixT_out[5][:, c], in_=ost[64:128, ob, :]), reads=[("ost", ob)], writes=[("mixc", 5)], dma=True)
                if I == NB - 1:
                    for ch in (4, 5):
                        add_ag(S, mixT_out[ch], mix_gath[ch], [("mixc", ch)], [("mixg", id(mix_gath[ch]))])

        unit_first = {}
        for bi, (kind, I, ks) in enumerate(blocks):
            unit_first.setdefault(kind, bi)
        load_k("A0", 0)
        load_k("A1", 1)
        start_block(0)
        start_block(1)

        def maybe_unit_loads(bi):
            kind, I, ks = blocks[bi]
            if unit_first[kind] != bi:
                return
            if kind == "A1":
                load_k("B0", 0)
            elif kind == "B0":
                load_k("B1", 1)

        c_loaded = [False]
        pend_ep = []
        LA = 3
        for g0 in range(LA):
            start_block(tiles[g0]["bi"])
            emit_qk(g0)
        for g in range(N):
            t = tiles[g]
            if t["first_blk"]:
                start_block(t["bi"] + 1)
                start_block(t["bi"] + 2)
                maybe_unit_loads(t["bi"])
            emit_exp(g)
            if g + LA < N:
                t2 = tiles[g + LA]
                if t2["kind"] == "C" and not c_loaded[0]:
                    load_k("C", 0)
                    c_loaded[0] = True
                start_block(t2["bi"])
                emit_qk(g + LA)
            emit_pv(g)
            emit_acc(g)
            for pe_ in list(pend_ep):
                if g >= pe_[0]:
                    emit_epilogue(pe_[1])
                    pend_ep.remove(pe_)
            if t["last_blk"]:
                pend_ep.append((g + 3, t["bi"]))
        for pe_ in pend_ep:
            emit_epilogue(pe_[1])
        S.run_block()


def colT(vec, nchunk):
    return np.ascontiguousarray(np.asarray(vec, np.float32).reshape(nchunk, 128).T)

def w_in_core(w_in_l, j):
    s = lambda a, n: w_in_l[:, a:a + n]
    qb0, kb0, vb0, qc0, kc0, vc0 = 576, 1088, 1600, 2112, 2624, 3136
    cols = [s(0, 512), s(512, 64), s(544, 32), s(512, 32),
            s(qb0 + 128 * j, 128), s(kb0 + 128 * j, 128), s(qc0 + 128 * j, 128), s(kc0 + 128 * j, 128),
            s(vb0 + 128 * j, 128), s(vc0 + 128 * j, 128)]
    return np.ascontiguousarray(np.concatenate(cols, axis=1))

def w_uq_core(w_uq_l, j):
    cols = []
    for hh in range(2):
        b = (2 * j + hh) * 192
        cols += [w_uq_l[:, b:b + 128], w_uq_l[:, b + 128:b + 192], w_uq_l[:, b + 160:b + 192], w_uq_l[:, b + 128:b + 160]]
    return np.ascontiguousarray(np.concatenate(cols, axis=1))

def w_ukv_core(w_ukv_l, j):
    b0, b1 = (2 * j) * 256, (2 * j + 1) * 256
    return np.ascontiguousarray(np.concatenate([w_ukv_l[:, b0:b0 + 128], w_ukv_l[:, b1:b1 + 128],
                                                w_ukv_l[:, b0 + 128:b0 + 256], w_ukv_l[:, b1 + 128:b1 + 256]], axis=1))

def rope_tables(seq=8192):
    half = 32
    inv = (np.float32(10000.0) ** (-np.arange(half, dtype=np.float32) / np.float32(half))).astype(np.float32)
    ang = (np.arange(seq, dtype=np.float32)[None, :] * inv[:, None]).astype(np.float32)
    c = np.cos(ang.astype(np.float64)).astype(np.float32)
    s = np.sin(ang.astype(np.float64)).astype(np.float32)
    return np.ascontiguousarray(np.concatenate([c, c], 0)), np.ascontiguousarray(np.concatenate([-s, s], 0))

NEGM = -30000.0

def mask_A():
    m = np.zeros((128, 128), np.float32)
    m[64:, :64] = NEGM
    return m.astype(ml_dtypes.bfloat16)

def corr_B(j):
    c = (2.0 ** (-2.0 * (j + 1))) * 8.0
    k = np.arange(128)[:, None]; q = np.arange(128)[None, :]
    m = np.where((k // 64 == q // 64) & (k > q), -2.0 * c * (k - q), 0.0).astype(np.float32)
    m[64:, :64] = NEGM
    return m.astype(ml_dtypes.bfloat16)

def mask_C():
    m = np.zeros((2, 128, 128), np.float32)
    m[0, 64:, :64] = NEGM
    m[1, :64, 64:] = NEGM
    return m

def bias_C(rel_bias_lh):
    k = np.arange(128)[:, None]; q = np.arange(128)[None, :]
    out = np.empty((5, 128, 128), np.float32)
    for d in range(5):
        idx = np.clip(128 * d + q - k, -63, 256) + 63
        out[d] = rel_bias_lh[idx]
    return out

def k_aug(seq=8192):
    p = np.arange(seq)
    return np.stack([p // 128, p % 128, np.ones(seq), np.ones(seq)]).astype(np.float32).astype(ml_dtypes.bfloat16)

def q_aug(j, seq=8192):
    c = (2.0 ** (-2.0 * (j + 1))) * 8.0
    p = np.arange(seq)
    return np.stack([np.full(seq, 128 * c), np.full(seq, c), -128 * c * (p // 128), -c * (p % 128)]).astype(np.float32).astype(ml_dtypes.bfloat16)


NCORES = 8
DEPTH = 2
FUSED = True
_PROG_CACHE = {}


def _lam_init(l):
    import math
    return 0.8 - 0.6 * math.exp(-0.3 * l)


def _di(nc, n, s, dt=F32):
    return nc.dram_tensor(n, list(s), dt, kind="ExternalInput").ap()


def _do(nc, n, s, dt=BF16):
    return nc.dram_tensor(n, list(s), dt, kind="ExternalOutput").ap()


def _dint(nc, n, s, dt=BF16):
    return nc.dram_tensor(n, list(s), dt, kind="Internal").ap()


def _decl_att_inputs(nc, pfx=""):
    d = {}
    d["w_in_c"] = _di(nc, pfx + "w_in_c", [D, WIN_C])
    d["qag"] = _di(nc, pfx + "qag", [128, 3])
    d["kvag"] = _di(nc, pfx + "kvag", [128, 1])
    d["w_uq_c"] = _di(nc, pfx + "w_uq_c", [384, 512])
    d["w_ukv_c"] = _di(nc, pfx + "w_ukv_c", [128, 512])
    d["biasC"] = _di(nc, pfx + "biasC", [5, 128, 128])
    d["lamv"] = _di(nc, pfx + "lamv", [4, 64])
    d["sublnT"] = _di(nc, pfx + "sublnT", [128, 1])
    return d


def _decl_consts(nc):
    d = {}
    d["rC"] = _di(nc, "rC", [64, SEQ])
    d["rS"] = _di(nc, "rS", [64, SEQ])
    d["kaug"] = _di(nc, "kaug", [4, SEQ], BF16)
    d["qaug"] = _di(nc, "qaug", [4, SEQ], BF16)
    d["maskA"] = _di(nc, "maskA", [128, 128], BF16)
    d["corrB"] = _di(nc, "corrB", [128, 128], BF16)
    d["maskC"] = _di(nc, "maskC", [2, 128, 128])
    return d


def _decl_mlp_inputs(nc, pfx=""):
    d = {}
    d["w_o_p"] = _di(nc, pfx + "w_o_p", [D, D])
    d["gT_mlp"] = _di(nc, pfx + "gT_mlp", [128, KC])
    d["w_up"] = _di(nc, pfx + "w_up", [D, DFF])
    d["w_down"] = _di(nc, pfx + "w_down", [DFF, D])
    return d


def build_fused():
    nc = bass.Bass("TRN2", target_bir_lowering=False, num_devices=NCORES)
    x = _di(nc, "x", [TOK, D])
    consts = _decl_consts(nc)
    gfin = _di(nc, "g_final", [D])
    y = _do(nc, "y", [TOK, D], F32)
    S = Sched(nc)
    S.alloc_sems()
    x_cur = x
    for l in range(DEPTH):
        pfx = "l%d_" % l
        gT = _di(nc, pfx + "gT_attn", [128, KC])
        a = _decl_att_inputs(nc, pfx)
        m = _decl_mlp_inputs(nc, pfx)
        hTc = [_dint(nc, pfx + "hTc%d" % c, [D, 256]) for c in range(8)]
        hTg = [_dint(nc, pfx + "hTg%d" % c, [4 * D, 256]) for c in range(8)]
        mixc = [_dint(nc, pfx + "mixc%d" % c, [64, SEQ]) for c in range(8)]
        mixg = [_dint(nc, pfx + "mixg%d" % c, [256, SEQ]) for c in range(8)]
        qT = _dint(nc, pfx + "qT_h", [QRH, SEQ])
        kT = _dint(nc, pfx + "kT_h", [KRH, SEQ])
        v = _dint(nc, pfx + "v_h", [SEQ, 512])
        x1 = _dint(nc, pfx + "x1", [TOK, D], F32)
        h2T = _dint(nc, pfx + "h2T", [D, TOK])
        final = (l == DEPTH - 1)
        x_next = y if final else _dint(nc, pfx + "xo", [TOK, D], F32)
        phase_P1(nc, S, x_cur, gT, hTc, hTg)
        phase_P2h(nc, S, hTg, a["w_in_c"], a["qag"], a["kvag"], a["w_uq_c"], a["w_ukv_c"], consts["rC"], consts["rS"], qT, kT, v)
        phase_ATT(nc, S, qT, kT, v, consts["kaug"], consts["qaug"], consts["maskA"], consts["corrB"], a["biasC"], consts["maskC"],
                  a["lamv"], a["sublnT"], _lam_init(l), mixc, mixg)
        phase_O1(nc, S, x_cur, mixg, m["w_o_p"], m["gT_mlp"], x1, h2T)
        phase_O2(nc, S, x1, h2T, m["w_up"], m["w_down"], x_next, g_final=gfin if final else None)
        x_cur = x_next
    return nc


def _prog(key, fn):
    if key not in _PROG_CACHE:
        _PROG_CACHE[key] = fn()
    return _PROG_CACHE[key]


def w_o_perm(w_o_l):
    def f(r, lr):
        if lr < 256:
            return 256 * r + lr
        if lr < 384:
            return 1024 + 128 * r + (lr - 256)
        return 1536 + 128 * r + (lr - 384)
    idx = [f(r, 64 * c + i) for c in range(8) for r in range(4) for i in range(64)]
    return np.ascontiguousarray(w_o_l[np.asarray(idx)])


def _f32(a):
    return np.ascontiguousarray(np.asarray(a, dtype=np.float32))


def _att_inputs(inp, l, j):
    return {
        "w_in_c": w_in_core(inp["w_in"][l], j),
        "qag": colT(inp["q_a_norm"][l], 3),
        "kvag": colT(inp["kv_a_norm"][l], 1),
        "w_uq_c": w_uq_core(inp["w_uq"][l], j),
        "w_ukv_c": w_ukv_core(inp["w_ukv"][l], j),
        "biasC": bias_C(inp["rel_bias"][l, j]),
        "lamv": np.ascontiguousarray(np.stack([inp["lambda_q1"][l], inp["lambda_k1"][l], inp["lambda_q2"][l], inp["lambda_k2"][l]])),
        "sublnT": np.ascontiguousarray(inp["diff_subln"][l].reshape(128, 1)),
    }


def _const_inputs(j):
    rC, rS = rope_tables()
    return {"rC": rC, "rS": rS, "kaug": k_aug(), "qaug": q_aug(j), "maskA": mask_A(), "corrB": corr_B(j), "maskC": mask_C()}


def _mlp_inputs(inp, l):
    return {"w_o_p": w_o_perm(inp["w_o"][l]), "gT_mlp": colT(inp["mlp_norm"][l], KC),
            "w_up": _f32(inp["w_up"][l]), "w_down": _f32(inp["w_down"][l])}


def kernel_fused(inp):
    cores = list(range(NCORES))
    x = inp["x"]
    nc = _prog("fused", build_fused)
    shared = {}
    for l in range(DEPTH):
        pfx = "l%d_" % l
        shared[pfx + "gT_attn"] = colT(inp["attn_norm"][l], KC)
        for k, v in _mlp_inputs(inp, l).items():
            shared[pfx + k] = v
    shared["g_final"] = _f32(inp["final_norm"])
    ins = []
    for c in cores:
        j = c % 4
        d = {"x": np.ascontiguousarray(x[c // 4, j * TOK:(j + 1) * TOK])}
        d.update(shared)
        d.update(_const_inputs(j))
        for l in range(DEPTH):
            for k, v in _att_inputs(inp, l, j).items():
                d["l%d_" % l + k] = v
        ins.append(d)
    res = run_bass_kernel_spmd(nc, ins, core_ids=cores)
    out = np.empty((2, SEQ, D), np.float32)
    for c in cores:
        out[c // 4, (c % 4) * TOK:(c % 4 + 1) * TOK] = res.results[c]["y"]
    return out


def kernel(**inputs):
    inp = {k: np.asarray(v) for k, v in inputs.items()}
    return kernel_fused(inp)
```

```python
import numpy as np
from contextlib import ExitStack
import ml_dtypes
import concourse.bass as bass
import concourse.mybir as mybir
from concourse.bass_utils import run_bass_kernel_spmd

F32 = mybir.dt.float32
BF16 = mybir.dt.bfloat16
AF = mybir.ActivationFunctionType
ALU = mybir.AluOpType
AX = mybir.AxisListType

RING = 8
FOLD_WAITS = True
_UIDC = [0]


def _uid(nc):
    _UIDC[0] += 1
    return "t%d_" % _UIDC[0]

ENGS = ("pe", "act", "dve", "pool", "sp")


class Ins:
    __slots__ = ("eng", "idx", "fn", "deps", "dma", "dma_n", "needs_inc", "semval", "cc", "lhs_deps")


class Sched:
    def __init__(self, nc):
        self.nc = nc
        self.progs = {e: [] for e in ENGS}
        self.res = {}
        self.ndma = {e: 0 for e in ENGS}
        self.emitted = {e: 0 for e in ENGS}
        self.waited = {e: {} for e in ENGS}
        self.cnt = {e: 0 for e in ENGS}
        self.sems = {}
        self.rings = {}
        self.cc_sems = []

    def alloc_sems(self):
        nc = self.nc
        for e in ENGS:
            self.sems[e] = nc.alloc_semaphore("c_" + e)
        for e in ("act", "pool", "sp"):
            self.rings[e] = [nc.alloc_semaphore("r_%s_%d" % (e, i)) for i in range(RING)]

    def add(self, eng, fn, reads=(), writes=(), dma=False, cc=False, rhs=()):
        if cc:
            dma = True
        progs = self.progs[eng]
        idx = len(progs)
        deps = {}
        lhs_deps = set()
        for k in reads:
            st = self.res.get(k)
            if st is not None and st[0] is not None:
                deps[st[0]] = True
                if k not in rhs:
                    lhs_deps.add(st[0])
        for k in writes:
            st = self.res.get(k)
            if st is not None:
                if st[0] is not None:
                    deps.setdefault(st[0], False)
                for re_, ri in st[1].items():
                    deps.setdefault((re_, ri), False)
                for r in st[2]:
                    deps.setdefault(r, False)
        ins = Ins()
        ins.eng = eng
        ins.idx = idx
        ins.fn = fn
        ins.dma = dma
        ins.needs_inc = False
        ins.semval = None
        ins.dma_n = -1
        ins.cc = cc
        if cc:
            ins.semval = (self.nc.alloc_semaphore("cc_%d" % len(self.cc_sems)), 1)
            self.cc_sems.append(ins.semval[0])
        elif dma:
            ins.dma_n = self.ndma[eng]
            self.ndma[eng] += 1
        fd = []
        for (pe_, pi), raw in deps.items():
            p = self.progs[pe_][pi]
            if pe_ == eng and not p.dma and not dma:
                if eng == "pe" or not raw:
                    continue
            fd.append((pe_, pi))
            if not p.dma:
                p.needs_inc = True
        ins.deps = fd
        ins.lhs_deps = lhs_deps
        progs.append(ins)
        for k in reads:
            st = self.res.get(k)
            if st is None:
                st = [None, {}, []]
                self.res[k] = st
            if dma:
                st[2].append((eng, idx))
            else:
                st[1][eng] = idx
        for k in writes:
            self.res[k] = [(eng, idx), {}, []]
        return ins

    def _semval(self, ins):
        return ins.semval

    def emit_engine(self, eng, e):
        progs = self.progs[eng]
        waited = self.waited[eng]
        start = self.emitted[eng]
        c = self.cnt[eng]
        for ins in progs[start:]:
            if ins.cc:
                pass
            elif ins.dma:
                ins.semval = (self.rings[eng][ins.dma_n % RING], 16 * (ins.dma_n // RING + 1))
            else:
                if ins.needs_inc:
                    c += 1
                ins.semval = (self.sems[eng], c)
        self.cnt[eng] = c

        for ins in progs[start:]:
            pend = {}
            lhsv = {}
            if ins.dma and not ins.cc and ins.dma_n >= RING:
                sem = self.rings[eng][ins.dma_n % RING]
                pend[id(sem)] = (sem, 16 * (ins.dma_n // RING))
            for (pe_, pi) in ins.deps:
                p = self.progs[pe_][pi]
                assert p.semval is not None, (eng, ins.idx, pe_, pi)
                k = id(p.semval[0])
                if k not in pend or pend[k][1] < p.semval[1]:
                    pend[k] = p.semval
                if eng == "pe" and (pe_, pi) in ins.lhs_deps:
                    lhsv[k] = max(lhsv.get(k, 0), p.semval[1])
            todo = []
            foldable = []
            for k, (sem, val) in pend.items():
                w0 = waited.get(k, 0)
                if w0 < val:
                    if eng == "pe" and lhsv.get(k, 0) > w0:
                        todo.append((sem, val))
                    else:
                        foldable.append((sem, val))
                    waited[k] = val
            fold = None
            if foldable and FOLD_WAITS and not ins.dma:
                fold = foldable.pop()
            todo += foldable
            for sem, val in todo:
                e.wait_ge(sem, val)
            bi = ins.fn(e)
            if fold is not None:
                bi._wait_ge(fold[0], fold[1])
            if ins.cc:
                bi.then_inc(ins.semval[0], 1)
            elif ins.dma:
                bi.then_inc(ins.semval[0], 16)
            elif ins.needs_inc:
                bi.then_inc(ins.semval[0], 1)
        self.emitted[eng] = len(progs)

    def drain_dma(self, e, eng_name="sp", final=False):
        waited = self.waited[eng_name]
        for q, ring in self.rings.items():
            n = self.ndma[q]
            for slot in range(RING):
                cnt = (n - slot + RING - 1) // RING if n > slot else 0
                if cnt > 0:
                    k = id(ring[slot])
                    if waited.get(k, 0) < 16 * cnt:
                        e.wait_ge(ring[slot], 16 * cnt)
                        waited[k] = 16 * cnt
        if final:
            for sem in self.cc_sems:
                k = id(sem)
                if waited.get(k, 0) < 1:
                    e.wait_ge(sem, 1)
                    waited[k] = 1

    def run_block(self, final=False):
        nc = self.nc
        for eng in ENGS:
            progs = self.progs[eng]
            c = self.cnt[eng]
            for ins in progs[self.emitted[eng]:]:
                if ins.cc:
                    pass
                elif ins.dma:
                    ins.semval = (self.rings[eng][ins.dma_n % RING], 16 * (ins.dma_n // RING + 1))
                else:
                    if ins.needs_inc:
                        c += 1
                    ins.semval = (self.sems[eng], c)
        with nc.Block() as block:
            @block.tensor
            def _(e):
                self.emit_engine("pe", e)

            @block.scalar
            def _(e):
                self.emit_engine("act", e)

            @block.vector
            def _(e):
                self.emit_engine("dve", e)

            @block.gpsimd
            def _(e):
                self.emit_engine("pool", e)

            @block.sync
            def _(e):
                self.emit_engine("sp", e)
                self.drain_dma(e, "sp", final)
        keep = {}
        for k, st in self.res.items():
            w = st[0]
            if w is not None and self.progs[w[0]][w[1]].cc:
                keep[k] = [w, {}, []]
        self.res.clear()
        self.res.update(keep)
        nc.all_engine_barrier()


D = 2048
TOK = 2048
NTT = TOK // 128
KC = D // 128
DFF = 8192
EPS = 1e-6


def make_ident(S, ident_bf, ident_f):
    S.add("pool", lambda e: e.memset(ident_f[:], 0.0), writes=["ident_f"])
    S.add("pool", lambda e: e.affine_select(out=ident_f[:], in_=ident_f[:], pattern=[[-1, 128]],
                                            compare_op=ALU.not_equal, fill=1.0, base=0,
                                            channel_multiplier=1),
          reads=["ident_f"], writes=["ident_f"])
    S.add("dve", lambda e: e.tensor_copy(out=ident_bf[:], in_=ident_f[:]), reads=["ident_f"], writes=["ident"])


def rms_rstd(S, eng_sq, src_ap, junk_ap, ss_ap, rstd_ap, n, rkeys, tag):
    S.add("act", lambda e: e.activation(out=junk_ap, in_=src_ap, func=AF.Square, accum_out=ss_ap),
          reads=rkeys, writes=[tag + "_junk", tag + "_ss"])
    S.add("dve", lambda e: e.tensor_scalar(out=rstd_ap, in0=ss_ap, scalar1=1.0 / n, scalar2=EPS,
                                           op0=ALU.mult, op1=ALU.add),
          reads=[tag + "_ss"], writes=[tag + "_rstd"])
    S.add("act", lambda e: e.activation(out=rstd_ap, in_=rstd_ap, func=AF.Sqrt),
          reads=[tag + "_rstd"], writes=[tag + "_rstd"])
    S.add("dve", lambda e: e.reciprocal(out=rstd_ap, in_=rstd_ap),
          reads=[tag + "_rstd"], writes=[tag + "_rstd"])


def phase_O1(nc, S, x_in, mixT, w_o, g_mlp, x1_out, h2T_out):
    with ExitStack() as es:
        mixT_sb = es.enter_context(nc.sbuf_tensor(_uid(nc) + "o1_mixT", [128, KC, TOK], BF16))
        wo_sb = es.enter_context(nc.sbuf_tensor(_uid(nc) + "o1_wo", [128, KC, D], BF16))
        xt = es.enter_context(nc.sbuf_tensor(_uid(nc) + "o1_xt", [128, 3, D], F32))
        xn = es.enter_context(nc.sbuf_tensor(_uid(nc) + "o1_xn", [128, 2, D], BF16))
        junk = es.enter_context(nc.sbuf_tensor(_uid(nc) + "o1_junk", [128, D], BF16))
        hT = es.enter_context(nc.sbuf_tensor(_uid(nc) + "o1_hT", [128, 2, KC, 128], BF16))
        gT = es.enter_context(nc.sbuf_tensor(_uid(nc) + "o1_gT", [128, KC], F32))
        st = es.enter_context(nc.sbuf_tensor(_uid(nc) + "o1_st", [128, 8], F32))
        ident_f = es.enter_context(nc.sbuf_tensor(_uid(nc) + "o1_identf", [128, 128], F32))
        ident = es.enter_context(nc.sbuf_tensor(_uid(nc) + "o1_ident", [128, 128], BF16))
        pm = es.enter_context(nc.psum_tensor(_uid(nc) + "o1_pm", [128, 4, 512], F32))
        pt = es.enter_context(nc.psum_tensor(_uid(nc) + "o1_pt", [128, 2, 8, 128], BF16))
        make_ident(S, ident, ident_f)
        S.add("sp", lambda e: e.dma_start(out=gT[:], in_=g_mlp),
              writes=["gT"], dma=True)
        dyn = {}

        def ld_mix(e, kc):
            if "off" not in dyn:
                dyn["off"] = e.snap((nc.partition_id([mybir.EngineType.SP]) % 4) * TOK, min_val=0, max_val=3 * TOK)
            return e.dma_start(out=mixT_sb[:, kc, :], in_=mixT[kc // 2][(kc % 2) * 128:(kc % 2 + 1) * 128, bass.ds(dyn["off"], TOK)])

        for q in range(4):
            for k4 in range(4):
                S.add("sp", lambda e, kc=q * 4 + k4: ld_mix(e, kc), reads=[("mixg", id(mixT[(q * 4 + k4) // 2]))], writes=[("mixT", q * 4 + k4)], dma=True)
            S.add("pool", lambda e, q=q: e.dma_start(out=wo_sb[:, q * 4:(q + 1) * 4, :],
                                                     in_=w_o[q * 512:(q + 1) * 512, :].rearrange("(kc p) n -> p kc n", p=128)),
                  writes=[("wo", q)], dma=True)
        def part1(tt):
            b = tt % 3
            S.add("sp", lambda e, tt=tt, b=b: e.dma_start(out=xt[:, b, :], in_=x_in[tt * 128:(tt + 1) * 128, :]),
                  writes=[("xt", b)], dma=True)
            for cg in range(4):
                for kc in range(KC):
                    S.add("pe", lambda e, tt=tt, cg=cg, kc=kc: e.matmul(
                        pm[:, cg, :], lhsT=mixT_sb[:, kc, tt * 128:(tt + 1) * 128],
                        rhs=wo_sb[:, kc, cg * 512:(cg + 1) * 512], start=(kc == 0), stop=(kc == KC - 1)),
                        reads=[("mixT", kc), ("wo", kc // 4)], writes=[("pm", cg)], rhs=[("wo", kc // 4)])
                S.add("dve", lambda e, b=b, cg=cg: e.tensor_tensor(
                    out=xt[:, b, cg * 512:(cg + 1) * 512], in0=pm[:, cg, :], in1=xt[:, b, cg * 512:(cg + 1) * 512], op=ALU.add),
                    reads=[("pm", cg), ("xt", b)], writes=[("xt", b)])
            S.add("sp", lambda e, tt=tt, b=b: e.dma_start(out=x1_out[tt * 128:(tt + 1) * 128, :], in_=xt[:, b, :]),
                  reads=[("xt", b)], dma=True)

        def part2(tt):
            b = tt % 3
            h = tt % 2
            rms_rstd(S, "act", xt[:, b, :], junk[:], st[:, h:h + 1], st[:, 2 + h:3 + h], D, [("xt", b)], "o1n%d" % h)
            S.add("act", lambda e, b=b, h=h: e.activation(out=xn[:, h, :], in_=xt[:, b, :], func=AF.Copy, scale=st[:, 2 + h:3 + h]),
                  reads=[("xt", b), "o1n%d_rstd" % h], writes=[("xn", h)])
            for hg in range(2):
                for j in range(8):
                    kc = hg * 8 + j
                    S.add("pe", lambda e, h=h, hg=hg, j=j, kc=kc: e.transpose(
                        out=pt[:, hg, j, :], in_=xn[:, h, kc * 128:(kc + 1) * 128], identity=ident[:]),
                        reads=[("xn", h), "ident"], writes=[("pt", hg)])
                for j in range(8):
                    kc = hg * 8 + j
                    if j % 2 == 0:
                        S.add("act", lambda e, h=h, hg=hg, j=j, kc=kc: e.activation(
                            out=hT[:, h, kc, :], in_=pt[:, hg, j, :], func=AF.Copy, scale=gT[:, kc:kc + 1]),
                            reads=[("pt", hg), "gT"], writes=[("hT", h)])
                    else:
                        S.add("dve", lambda e, h=h, hg=hg, j=j, kc=kc: e.tensor_scalar(
                            out=hT[:, h, kc, :], in0=pt[:, hg, j, :], scalar1=gT[:, kc:kc + 1], scalar2=None, op0=ALU.mult),
                            reads=[("pt", hg), "gT"], writes=[("hT", h)])
            S.add("sp", lambda e, tt=tt, h=h: e.dma_start(
                out=h2T_out[:, tt * 128:(tt + 1) * 128].rearrange("(kc p) t -> p kc t", p=128), in_=hT[:, h, :, :]),
                reads=[("hT", h)], dma=True)

        part1(0)
        for tt in range(1, NTT):
            part1(tt)
            part2(tt - 1)
        part2(NTT - 1)
        S.run_block()


def phase_O2(nc, S, x1_in, h2T, w_up, w_down, x_out, g_final=None):
    HT = TOK // 2
    NFB = DFF // 512
    NT = HT // 128
    with ExitStack() as es:
        x1 = es.enter_context(nc.sbuf_tensor(_uid(nc) + "o2_x1", [128, NT, D], F32))
        hT = es.enter_context(nc.sbuf_tensor(_uid(nc) + "o2_hT", [128, KC, HT], BF16))
        wu = es.enter_context(nc.sbuf_tensor(_uid(nc) + "o2_wu", [128, 2, KC, 512], BF16))
        wd = es.enter_context(nc.sbuf_tensor(_uid(nc) + "o2_wd", [128, 2, 4, D], BF16))
        aT = es.enter_context(nc.sbuf_tensor(_uid(nc) + "o2_aT", [128, 2, 4, HT], BF16))
        gB = es.enter_context(nc.sbuf_tensor(_uid(nc) + "o2_gB", [128, D], F32))
        junk = es.enter_context(nc.sbuf_tensor(_uid(nc) + "o2_junk", [128, D], BF16))
        st = es.enter_context(nc.sbuf_tensor(_uid(nc) + "o2_st", [128, 4], F32))
        pu = es.enter_context(nc.psum_tensor(_uid(nc) + "o2_pu", [128, 2, 512], F32))
        pd = es.enter_context(nc.psum_tensor(_uid(nc) + "o2_pd", [128, 4, 512], F32))
        if g_final is not None:
            S.add("sp", lambda e: e.dma_start(out=gB[:], in_=g_final.partition_broadcast(128)), writes=["gB"], dma=True)

        def load_hT(hf):
            t0 = hf * HT
            for q in range(4):
                S.add("sp", lambda e, q=q, t0=t0: e.dma_start(
                    out=hT[:, q * 4:(q + 1) * 4, :],
                    in_=h2T[q * 512:(q + 1) * 512, t0:t0 + HT].rearrange("(kc p) t -> p kc t", p=128)),
                    writes=[("hT", q)], dma=True)

        def load_x1_pair(hf, q):
            t0 = hf * HT
            S.add("sp", lambda e, q=q, t0=t0: e.dma_start(
                out=x1[:, 2 * q:2 * q + 2, :],
                in_=x1_in[t0 + q * 256:t0 + (q + 1) * 256, :].rearrange("(t p) d -> p t d", p=128)),
                writes=[("x1", 2 * q), ("x1", 2 * q + 1)], dma=True)

        def load_w(fb):
            b = fb % 2
            for q in range(2):
                S.add("pool", lambda e, fb=fb, b=b, q=q: e.dma_start(
                    out=wu[:, b, q * 8:(q + 1) * 8, :],
                    in_=w_up[q * 1024:(q + 1) * 1024, fb * 512:(fb + 1) * 512].rearrange("(kc p) n -> p kc n", p=128)),
                    writes=[("wu", b, q)], dma=True)
            S.add("pool", lambda e, fb=fb, b=b: e.dma_start(
                out=wd[:, b, :, :],
                in_=w_down[fb * 512:(fb + 1) * 512, :].rearrange("(fc p) n -> p fc n", p=128)),
                writes=[("wd", b)], dma=True)

        def up(fb):
            b = fb % 2
            n = 0
            for fc in range(4):
                for tg in range(HT // 512):
                    slot = n % 2
                    n += 1
                    for kc in range(KC):
                        S.add("pe", lambda e, b=b, fc=fc, tg=tg, kc=kc, slot=slot: e.matmul(
                            pu[:, slot, :], lhsT=wu[:, b, kc, fc * 128:(fc + 1) * 128],
                            rhs=hT[:, kc, tg * 512:(tg + 1) * 512], start=(kc == 0), stop=(kc == KC - 1)),
                            reads=[("wu", b, kc // 8), ("hT", kc // 4)], writes=[("pu", slot)], rhs=[("hT", kc // 4)])
                    S.add("act", lambda e, b=b, fc=fc, tg=tg, slot=slot: e.activation(
                        out=aT[:, b, fc, tg * 512:(tg + 1) * 512], in_=pu[:, slot, :], func=AF.Relu),
                        reads=[("pu", slot)], writes=[("aT", b, fc)])
                    S.add("pool", lambda e, b=b, fc=fc, tg=tg: e.tensor_tensor(
                        out=aT[:, b, fc, tg * 512:(tg + 1) * 512], in0=aT[:, b, fc, tg * 512:(tg + 1) * 512],
                        in1=aT[:, b, fc, tg * 512:(tg + 1) * 512], op=ALU.mult),
                        reads=[("aT", b, fc)], writes=[("aT", b, fc)])

        def finish_tile(hf, tt):
            t0 = hf * HT
            if g_final is not None:
                b = tt % 2
                rms_rstd(S, "act", x1[:, tt, :], junk[:], st[:, b:b + 1], st[:, 2 + b:3 + b], D, [("x1", tt)], "o2n%d" % b)
                S.add("dve", lambda e, tt=tt, b=b: e.scalar_tensor_tensor(
                    out=x1[:, tt, :], in0=x1[:, tt, :], scalar=st[:, 2 + b:3 + b], in1=gB[:], op0=ALU.mult, op1=ALU.mult),
                    reads=[("x1", tt), "o2n%d_rstd" % b, "gB"], writes=[("x1", tt)])
            S.add("sp", lambda e, tt=tt, t0=t0: e.dma_start(out=x_out[t0 + tt * 128:t0 + (tt + 1) * 128, :], in_=x1[:, tt, :]),
                  reads=[("x1", tt)], dma=True)

        def down(hf, fb, last):
            b = fb % 2
            n = 0
            for tt in range(NT):
                for cg in range(4):
                    slot = n % 4
                    n += 1
                    for fc in range(4):
                        S.add("pe", lambda e, b=b, tt=tt, cg=cg, fc=fc, slot=slot: e.matmul(
                            pd[:, slot, :], lhsT=aT[:, b, fc, tt * 128:(tt + 1) * 128],
                            rhs=wd[:, b, fc, cg * 512:(cg + 1) * 512], start=(fc == 0), stop=(fc == 3)),
                            reads=[("aT", b, fc), ("wd", b)], writes=[("pd", slot)], rhs=[("wd", b)])
                    S.add("dve", lambda e, tt=tt, cg=cg, slot=slot: e.tensor_tensor(
                        out=x1[:, tt, cg * 512:(cg + 1) * 512], in0=pd[:, slot, :],
                        in1=x1[:, tt, cg * 512:(cg + 1) * 512], op=ALU.add),
                        reads=[("pd", slot), ("x1", tt)], writes=[("x1", tt)])
                if last:
                    finish_tile(hf, tt)
                    if hf == 0 and tt % 2 == 1:
                        load_x1_pair(1, tt // 2)

        load_hT(0)
        load_w(0)
        for q in range(4):
            load_x1_pair(0, q)
        up(0)
        for hf in range(2):
            for fb in range(NFB):
                last = (fb == NFB - 1)
                if not last:
                    load_w(fb + 1)
                    up(fb + 1)
                elif hf == 0:
                    load_hT(1)
                    load_w(0)
                    up(0)
                down(hf, fb, last)
        S.run_block()


SEQ = 8192
WIN_C = 1408
QRH = 640
KRH = 576
QA = (0, 192)
QB_ROW = 384
QC_ROW = 512
KROPE_ROW = 256
KB_ROW = 320
KC_ROW = 448


def add_ag(S, src, dst, rkeys, wkeys):
    S.add("pool", lambda e: e.collective_compute(
        "AllGather", ALU.bypass, replica_groups=[[0, 1, 2, 3], [4, 5, 6, 7]], ins=[src], outs=[dst]),
        reads=rkeys, writes=wkeys, cc=True)


def phase_P1(nc, S, x_in, gT_attn, hT_out, hT_gath):
    with ExitStack() as es:
        xt = es.enter_context(nc.sbuf_tensor(_uid(nc) + "p1_xt", [128, NTT, D], F32))
        xn = es.enter_context(nc.sbuf_tensor(_uid(nc) + "p1_xn", [128, 2, D], BF16))
        junk = es.enter_context(nc.sbuf_tensor(_uid(nc) + "p1_junk", [128, D], BF16))
        hT = es.enter_context(nc.sbuf_tensor(_uid(nc) + "p1_hT", [128, 2, KC, 256], BF16))
        gT = es.enter_context(nc.sbuf_tensor(_uid(nc) + "p1_gT", [128, KC], F32))
        ss = es.enter_context(nc.sbuf_tensor(_uid(nc) + "p1_ss", [128, NTT], F32))
        rs = es.enter_context(nc.sbuf_tensor(_uid(nc) + "p1_rs", [128, NTT], F32))
        ident_f = es.enter_context(nc.sbuf_tensor(_uid(nc) + "p1_identf", [128, 128], F32))
        ident = es.enter_context(nc.sbuf_tensor(_uid(nc) + "p1_ident", [128, 128], BF16))
        pt = es.enter_context(nc.psum_tensor(_uid(nc) + "p1_pt", [128, 2, 8, 128], BF16))
        make_ident(S, ident, ident_f)
        S.add("sp", lambda e: e.dma_start(out=gT[:], in_=gT_attn), writes=["gT"], dma=True)
        for tt in range(NTT):
            S.add("sp", lambda e, tt=tt: e.dma_start(out=xt[:, tt, :], in_=x_in[tt * 128:(tt + 1) * 128, :]),
                  writes=[("xt", tt)], dma=True)
        for tt in range(NTT):
            S.add("act", lambda e, tt=tt: e.activation(out=junk[:], in_=xt[:, tt, :], func=AF.Square, accum_out=ss[:, tt:tt + 1]),
                  reads=[("xt", tt)], writes=["junk", ("ss", tt)])
        S.add("dve", lambda e: e.tensor_scalar(out=rs[:], in0=ss[:], scalar1=1.0 / D, scalar2=EPS, op0=ALU.mult, op1=ALU.add),
              reads=[("ss", tt) for tt in range(NTT)], writes=["rs"])
        S.add("act", lambda e: e.activation(out=rs[:], in_=rs[:], func=AF.Sqrt), reads=["rs"], writes=["rs"])
        S.add("dve", lambda e: e.reciprocal(out=rs[:], in_=rs[:]), reads=["rs"], writes=["rs"])
        for tt in range(NTT):
            b = tt % 2
            cb = (tt // 2) % 2
            co = (tt % 2) * 128
            S.add("act", lambda e, tt=tt, b=b: e.activation(out=xn[:, b, :], in_=xt[:, tt, :], func=AF.Copy, scale=rs[:, tt:tt + 1]),
                  reads=[("xt", tt), "rs"], writes=[("xn", b)])
            for hg in range(2):
                for j in range(8):
                    kc = hg * 8 + j
                    S.add("pe", lambda e, b=b, hg=hg, j=j, kc=kc: e.transpose(
                        out=pt[:, hg, j, :], in_=xn[:, b, kc * 128:(kc + 1) * 128], identity=ident[:]),
                        reads=[("xn", b), "ident"], writes=[("pt", hg)])
                for j in range(8):
                    kc = hg * 8 + j
                    if j % 2 == 0:
                        S.add("act", lambda e, cb=cb, co=co, hg=hg, j=j, kc=kc: e.activation(
                            out=hT[:, cb, kc, co:co + 128], in_=pt[:, hg, j, :], func=AF.Copy, scale=gT[:, kc:kc + 1]),
                            reads=[("pt", hg), "gT"], writes=[("hT", cb, tt % 2)])
                    else:
                        S.add("dve", lambda e, cb=cb, co=co, hg=hg, j=j, kc=kc: e.tensor_scalar(
                            out=hT[:, cb, kc, co:co + 128], in0=pt[:, hg, j, :], scalar1=gT[:, kc:kc + 1], scalar2=None, op0=ALU.mult),
                            reads=[("pt", hg), "gT"], writes=[("hT", cb, tt % 2)])
            if tt % 2 == 1:
                ch = tt // 2
                S.add("sp", lambda e, ch=ch, cb=cb: e.dma_start(
                    out=hT_out[ch].rearrange("(kc p) t -> p kc t", p=128), in_=hT[:, cb, :, :]),
                    reads=[("hT", cb, 0), ("hT", cb, 1)], writes=[("hTc", ch)], dma=True)
                add_ag(S, hT_out[ch], hT_gath[ch], [("hTc", ch)], [("hTg", id(hT_gath[ch]))])
        S.run_block()


def phase_P2h(nc, S, hT_all, w_in_c, qagT, kvagT, w_uq_c, w_ukv_c, ropeC, ropeS, qT_out, kT_out, v_out):
    GT = 1024
    NG = SEQ // GT
    with ExitStack() as es:
        hT = es.enter_context(nc.sbuf_tensor(_uid(nc) + "p2_hT", [128, 2, KC, GT], BF16))
        wb = es.enter_context(nc.sbuf_tensor(_uid(nc) + "p2_w", [128, KC, WIN_C], BF16))
        wuq = es.enter_context(nc.sbuf_tensor(_uid(nc) + "p2_wuq", [128, 3, 512], BF16))
        wukv = es.enter_context(nc.sbuf_tensor(_uid(nc) + "p2_wukv", [128, 512], BF16))
        cnT = es.enter_context(nc.sbuf_tensor(_uid(nc) + "p2_cnT", [128, 2, 4, GT], BF16))
        junk = es.enter_context(nc.sbuf_tensor(_uid(nc) + "p2_junk", [128, 512], BF16))
        cq = es.enter_context(nc.sbuf_tensor(_uid(nc) + "p2_cq", [128, 2, 512], F32))
        cn = es.enter_context(nc.sbuf_tensor(_uid(nc) + "p2_cn", [128, 2, 512], BF16))
        stage = es.enter_context(nc.sbuf_tensor(_uid(nc) + "p2_stage", [128, 4, 512], BF16))
        rt = es.enter_context(nc.sbuf_tensor(_uid(nc) + "p2_rt", [64, 2, 2, 512], F32))
        rC = es.enter_context(nc.sbuf_tensor(_uid(nc) + "p2_rC", [64, 2, GT], F32))
        rS = es.enter_context(nc.sbuf_tensor(_uid(nc) + "p2_rS", [64, 2, GT], F32))
        qag = es.enter_context(nc.sbuf_tensor(_uid(nc) + "p2_qag", [128, 4], F32))
        st = es.enter_context(nc.sbuf_tensor(_uid(nc) + "p2_st", [128, 16], F32))
        ident_f = es.enter_context(nc.sbuf_tensor(_uid(nc) + "p2_identf", [128, 128], F32))
        ident = es.enter_context(nc.sbuf_tensor(_uid(nc) + "p2_ident", [128, 128], BF16))
        pm = es.enter_context(nc.psum_tensor(_uid(nc) + "p2_pm", [128, 4, 512], F32))
        pt = es.enter_context(nc.psum_tensor(_uid(nc) + "p2_pt", [128, 2, 4, 128], BF16))
        make_ident(S, ident, ident_f)
        S.add("sp", lambda e: e.dma_start(out=qag[:, 0:3], in_=qagT), writes=["qag"], dma=True)
        S.add("sp", lambda e: e.dma_start(out=qag[:, 3:4], in_=kvagT), reads=["qag"], writes=["qag"], dma=True)
        for q in range(4):
            S.add("pool", lambda e, q=q: e.dma_start(
                out=wb[:, q * 4:(q + 1) * 4, :], in_=w_in_c[q * 512:(q + 1) * 512, :].rearrange("(kc p) n -> p kc n", p=128)),
                writes=[("wb", q)], dma=True)
        S.add("pool", lambda e: e.dma_start(out=wuq[:], in_=w_uq_c.rearrange("(kc p) n -> p kc n", p=128)), writes=["wuq"], dma=True)
        S.add("pool", lambda e: e.dma_start(out=wukv[:], in_=w_ukv_c), writes=["wukv"], dma=True)
        wb_all = [("wb", q) for q in range(4)]

        cnt = {"pm": 0, "st": 0, "ev": 0}

        def pm_slot():
            s = cnt["pm"] % 4
            cnt["pm"] += 1
            return s

        def evac_store(ps_ap, dram_aps, nrows, ncols, slot_pm):
            ss = cnt["st"] % 4
            cnt["st"] += 1
            eng = "act" if cnt["ev"] % 2 == 0 else "dve"
            cnt["ev"] += 1
            if eng == "act":
                S.add("act", lambda e, ss=ss: e.activation(out=stage[0:nrows, ss, 0:ncols], in_=ps_ap, func=AF.Copy),
                      reads=[("pm", slot_pm)], writes=[("stage", ss)])
            else:
                S.add("dve", lambda e, ss=ss: e.tensor_copy(out=stage[0:nrows, ss, 0:ncols], in_=ps_ap),
                      reads=[("pm", slot_pm)], writes=[("stage", ss)])
            for (c0_, nc_, dap) in dram_aps:
                S.add("sp", lambda e, ss=ss, c0_=c0_, nc_=nc_, dap=dap: e.dma_start(out=dap, in_=stage[0:nrows, ss, c0_:c0_ + nc_]),
                      reads=[("stage", ss)], dma=True)

        def fm_dst(dst, r0, nrows, g, tg):
            out = []
            for k in range(2):
                p0 = gpos(g, tg * 512 + k * 256)
                out.append((k * 256, 256, dst[r0:r0 + nrows, p0:p0 + 256]))
            return out

        def rope_store(psA, psB, slotA, slotB, hb, tg, dram_aps):
            r = cnt["st"] % 2
            ss = cnt["st"] % 4
            cnt["st"] += 1
            S.add("dve", lambda e, r=r: e.tensor_tensor(out=rt[:, r, 0, :], in0=psA, in1=rC[:, hb, tg * 512:(tg + 1) * 512], op=ALU.mult),
                  reads=[("pm", slotA)] + [("rC", hb, x) for x in range(4)], writes=[("rt", r, 0)])
            S.add("dve", lambda e, r=r: e.tensor_tensor(out=rt[:, r, 1, :], in0=psB, in1=rS[:, hb, tg * 512:(tg + 1) * 512], op=ALU.mult),
                  reads=[("pm", slotB)] + [("rS", hb, x) for x in range(4)], writes=[("rt", r, 1)])
            S.add("pool", lambda e, r=r, ss=ss: e.tensor_tensor(out=stage[0:64, ss, :], in0=rt[:, r, 0, :], in1=rt[:, r, 1, :], op=ALU.add),
                  reads=[("rt", r, 0), ("rt", r, 1)], writes=[("stage", ss)])
            for (c0_, nc_, dap) in dram_aps:
                S.add("sp", lambda e, ss=ss, c0_=c0_, nc_=nc_, dap=dap: e.dma_start(out=dap, in_=stage[0:64, ss, c0_:c0_ + nc_]),
                      reads=[("stage", ss)], dma=True)

        def gpos(g, u):
            return 2048 * (u // 256) + 256 * g + (u % 256)

        def load_group(g, hb):
            for r in range(4):
                for q in range(4):
                    S.add("sp", lambda e, hb=hb, r=r, g=g, q=q: e.dma_start(
                        out=hT[:, hb, q * 4:(q + 1) * 4, r * 256:(r + 1) * 256],
                        in_=hT_all[g][r * D + q * 512:r * D + (q + 1) * 512, :].rearrange("(kc p) t -> p kc t", p=128)),
                        reads=[("hTg", id(hT_all[g]))], writes=[("hT", hb, q, r)], dma=True)
            for r in range(4):
                p0 = gpos(g, r * 256)
                S.add("sp", lambda e, hb=hb, r=r, p0=p0: e.dma_start(out=rC[:, hb, r * 256:(r + 1) * 256], in_=ropeC[:, p0:p0 + 256]),
                      writes=[("rC", hb, r)], dma=True)
                S.add("sp", lambda e, hb=hb, r=r, p0=p0: e.dma_start(out=rS[:, hb, r * 256:(r + 1) * 256], in_=ropeS[:, p0:p0 + 256]),
                      writes=[("rS", hb, r)], dma=True)

        gorder = list(range(NG))
        load_group(gorder[0], 0)
        for gi, g in enumerate(gorder):
            hb = gi % 2
            t0 = g * GT
            if gi + 1 < NG:
                load_group(gorder[gi + 1], (gi + 1) % 2)
            hk = [("hT", hb, q) for q in range(4)]
            def g0_part1(tt):
                b = tt % 2
                sl = pm_slot()
                for kc in range(KC):
                    S.add("pe", lambda e, hb=hb, tt=tt, kc=kc, sl=sl: e.matmul(
                        pm[:, sl, :], lhsT=hT[:, hb, kc, tt * 128:(tt + 1) * 128], rhs=wb[:, kc, 0:512],
                        start=(kc == 0), stop=(kc == KC - 1)),
                        reads=[("hT", hb, kc // 4, x) for x in range(4)] + [("wb", kc // 4)], writes=[("pm", sl)], rhs=[("wb", kc // 4)])
                S.add("dve", lambda e, b=b, sl=sl: e.tensor_copy(out=cq[:, b, :], in_=pm[:, sl, :]),
                      reads=[("pm", sl)], writes=[("cq", b)])

            def g0_part2(tt):
                b = tt % 2
                rms_rstd(S, "act", cq[:, b, 0:384], junk[:, 0:384], st[:, 4 + b:5 + b], st[:, 6 + b:7 + b], 384, [("cq", b)], "pq%d" % b)
                rms_rstd(S, "act", cq[:, b, 384:512], junk[:, 384:512], st[:, 8 + b:9 + b], st[:, 10 + b:11 + b], 128, [("cq", b)], "pk%d" % b)
                S.add("act", lambda e, b=b: e.activation(out=cn[:, b, 0:384], in_=cq[:, b, 0:384], func=AF.Copy, scale=st[:, 6 + b:7 + b]),
                      reads=[("cq", b), "pq%d_rstd" % b], writes=[("cn", b, 0)])
                S.add("act", lambda e, b=b: e.activation(out=cn[:, b, 384:512], in_=cq[:, b, 384:512], func=AF.Copy, scale=st[:, 10 + b:11 + b]),
                      reads=[("cq", b), "pk%d_rstd" % b], writes=[("cn", b, 1)])
                hg = tt % 2
                for j in range(4):
                    S.add("pe", lambda e, b=b, hg=hg, j=j: e.transpose(
                        out=pt[:, hg, j, :], in_=cn[:, b, j * 128:(j + 1) * 128], identity=ident[:]),
                        reads=[("cn", b, 0), ("cn", b, 1), "ident"], writes=[("pt", hg)])
                for j in range(4):
                    S.add("act", lambda e, hb=hb, tt=tt, hg=hg, j=j: e.activation(
                        out=cnT[:, hb, j, tt * 128:(tt + 1) * 128], in_=pt[:, hg, j, :], func=AF.Copy, scale=qag[:, j:j + 1]),
                        reads=[("pt", hg), "qag"], writes=[("cnT", hb, tt)])

            ntt = GT // 128
            g0_part1(0)
            for tt in range(1, ntt):
                g0_part1(tt)
                g0_part2(tt - 1)
            g0_part2(ntt - 1)
            ck = [("cnT", hb, tt) for tt in range(GT // 128)]
            for tg in range(GT // 512):
                c0t = t0 + tg * 512
                slA = pm_slot()
                slB = pm_slot()
                for v_, sl in ((0, slA), (1, slB)):
                    for kc in range(KC):
                        S.add("pe", lambda e, hb=hb, tg=tg, kc=kc, sl=sl, v_=v_: e.matmul(
                            pm[0:64, sl, :], lhsT=wb[:, kc, 512 + 64 * v_:576 + 64 * v_], rhs=hT[:, hb, kc, tg * 512:(tg + 1) * 512],
                            start=(kc == 0), stop=(kc == KC - 1)),
                            reads=[("hT", hb, kc // 4, x) for x in range(4)] + [("wb", kc // 4)], writes=[("pm", sl)])
                rope_store(pm[0:64, slA, :], pm[0:64, slB, :], slA, slB, hb, tg, fm_dst(kT_out, KROPE_ROW, 64, g, tg))
                for fi, (dst, r0) in enumerate(((qT_out, QB_ROW), (kT_out, KB_ROW), (qT_out, QC_ROW), (kT_out, KC_ROW))):
                    sl = pm_slot()
                    for kc in range(KC):
                        S.add("pe", lambda e, hb=hb, tg=tg, kc=kc, sl=sl, fi=fi: e.matmul(
                            pm[:, sl, :], lhsT=wb[:, kc, 640 + fi * 128:640 + (fi + 1) * 128], rhs=hT[:, hb, kc, tg * 512:(tg + 1) * 512],
                            start=(kc == 0), stop=(kc == KC - 1)),
                            reads=[("hT", hb, kc // 4, x) for x in range(4)] + [("wb", kc // 4)], writes=[("pm", sl)])
                    evac_store(pm[:, sl, :], fm_dst(dst, r0, 128, g, tg), 128, 512, sl)
                for hh in range(2):
                    sl = pm_slot()
                    for kc in range(3):
                        S.add("pe", lambda e, hb=hb, tg=tg, kc=kc, sl=sl, hh=hh: e.matmul(
                            pm[:, sl, :], lhsT=wuq[:, kc, hh * 256:hh * 256 + 128], rhs=cnT[:, hb, kc, tg * 512:(tg + 1) * 512],
                            start=(kc == 0), stop=(kc == 2)),
                            reads=ck[tg * 4:(tg + 1) * 4] + ["wuq"], writes=[("pm", sl)])
                    evac_store(pm[:, sl, :], fm_dst(qT_out, hh * 192, 128, g, tg), 128, 512, sl)
                    slA = pm_slot()
                    slB = pm_slot()
                    for v_, sl in ((0, slA), (1, slB)):
                        for kc in range(3):
                            S.add("pe", lambda e, hb=hb, tg=tg, kc=kc, sl=sl, hh=hh, v_=v_: e.matmul(
                                pm[0:64, sl, :], lhsT=wuq[:, kc, hh * 256 + 128 + 64 * v_:hh * 256 + 192 + 64 * v_],
                                rhs=cnT[:, hb, kc, tg * 512:(tg + 1) * 512], start=(kc == 0), stop=(kc == 2)),
                                reads=ck[tg * 4:(tg + 1) * 4] + ["wuq"], writes=[("pm", sl)])
                    rope_store(pm[0:64, slA, :], pm[0:64, slB, :], slA, slB, hb, tg, fm_dst(qT_out, hh * 192 + 128, 64, g, tg))
                    sl = pm_slot()
                    S.add("pe", lambda e, hb=hb, tg=tg, sl=sl, hh=hh: e.matmul(
                        pm[:, sl, :], lhsT=wukv[:, hh * 128:(hh + 1) * 128], rhs=cnT[:, hb, 3, tg * 512:(tg + 1) * 512],
                        start=True, stop=True),
                        reads=ck[tg * 4:(tg + 1) * 4] + ["wukv"], writes=[("pm", sl)])
                    evac_store(pm[:, sl, :], fm_dst(kT_out, hh * 128, 128, g, tg), 128, 512, sl)
            for tt in range(GT // 128):
                sl = pm_slot()
                S.add("pe", lambda e, hb=hb, tt=tt, sl=sl: e.matmul(
                    pm[:, sl, 0:256], lhsT=cnT[:, hb, 3, tt * 128:(tt + 1) * 128], rhs=wukv[:, 256:512], start=True, stop=True),
                    reads=[("cnT", hb, tt), "wukv"], writes=[("pm", sl)])
                for kc in range(KC):
                    S.add("pe", lambda e, hb=hb, tt=tt, kc=kc, sl=sl: e.matmul(
                        pm[:, sl, 256:512], lhsT=hT[:, hb, kc, tt * 128:(tt + 1) * 128], rhs=wb[:, kc, 1152:1408],
                        start=(kc == 0), stop=(kc == KC - 1)),
                        reads=[("hT", hb, kc // 4, x) for x in range(4)] + [("wb", kc // 4)], writes=[("pm", sl)])
                evac_store(pm[:, sl, :], [(0, 512, v_out[gpos(g, tt * 128):gpos(g, tt * 128) + 128, :])], 128, 512, sl)
        S.run_block()


def phase_ATT(nc, S, qT, kT, v, kaug, qaug, maskA, corrB, biasC, maskC, lamv, sublnT, lam_init, mixT_out, mix_gath):
    NB = SEQ // 512
    scA = float(192 ** -0.5)
    scB = 0.125
    scC = float(128 ** -0.5)
    with ExitStack() as es:
        T = lambda n, s, dt: es.enter_context(nc.sbuf_tensor(_uid(nc) + n, s, dt))
        vsb = T("a_v", [128, SEQ // 128, 512], BF16)
        kbuf = T("a_k", [128, 2, SEQ], BF16)
        krope = T("a_kr", [64, SEQ], BF16)
        qn = T("a_qn", [128, 3, 512], BF16)
        qr = T("a_qr", [64, 3, 512], BF16)
        pT = T("a_pT", [128, 6, 512], BF16)
        acc = T("a_acc", [128, 2, 512], F32)
        rec = T("a_rec", [128, 2, 512], F32)
        dsb = T("a_dsb", [128, 2, 512], F32)
        acc2 = T("a_acc2", [128, 2, 512], F32)
        ones5 = T("a_ones5", [128, 512], F32)
        ones_b = T("a_onesb", [128, 128], BF16)
        mhalf = T("a_mhalf", [128, 512], F32)
        ost = T("a_ost", [128, 2, 512], BF16)
        tmp0 = T("a_tmp0", [128, 512], F32)
        t1 = T("a_t1", [128, 512], F32)
        ob_ = T("a_o", [128, 512], F32)
        sq = T("a_sq", [128, 512], F32)
        tS = T("a_tS", [128, 2, 128], F32)
        bm = T("a_bm", [128, 5, 128], F32)
        mC = T("a_mC", [128, 2, 128], F32)
        mA = T("a_mA", [128, 128], BF16)
        cB = T("a_cB", [128, 128], BF16)
        ident_f = T("a_identf", [128, 128], F32)
        ident = T("a_ident", [128, 128], BF16)
        ones_f = T("a_ones", [128, 128], F32)
        lv = T("a_lv", [128, 4, 64], F32)
        lt = T("a_lt", [128, 2, 64], F32)
        ls = T("a_ls", [128, 8], F32)
        sg = T("a_sg", [128, 2], F32)
        ps_s = es.enter_context(nc.psum_tensor(_uid(nc) + "a_ps_s", [128, 4, 512], F32))
        ps_o = es.enter_context(nc.psum_tensor(_uid(nc) + "a_ps_o", [128, 2, 512], F32))
        ps_d = es.enter_context(nc.psum_tensor(_uid(nc) + "a_ps_d", [128, 2, 512], F32))

        make_ident(S, ident, ident_f)
        S.add("pool", lambda e: e.memset(ones_f[:], 1.0), writes=["ones"])
        S.add("pool", lambda e: e.memset(ones5[:], -1.0), writes=["ones5"])
        S.add("dve", lambda e: e.tensor_copy(out=ones_b[:], in_=ones_f[:]), reads=["ones"], writes=["onesb"])
        S.add("pool", lambda e: e.memset(mhalf[:], -0.5), writes=["mhalf"])
        S.add("sp", lambda e: e.dma_start(out=mA[:], in_=maskA), writes=["mA"], dma=True)
        S.add("sp", lambda e: e.dma_start(out=cB[:], in_=corrB), writes=["cB"], dma=True)
        S.add("sp", lambda e: e.dma_start(out=bm[:], in_=biasC.rearrange("d k q -> k d q")), writes=["bm"], dma=True)
        S.add("sp", lambda e: e.dma_start(out=mC[:], in_=maskC.rearrange("d k q -> k d q")), writes=["mC"], dma=True)
        S.add("sp", lambda e: e.dma_start(out=lv[:], in_=lamv.partition_broadcast(128)), writes=["lv"], dma=True)
        S.add("sp", lambda e: e.dma_start(out=sg[:, 0:1], in_=sublnT), writes=["sg0"], dma=True)
        S.add("sp", lambda e: e.dma_start(out=vsb[:], in_=v.rearrange("(kt p) c -> p kt c", p=128)), writes=["vsb"], dma=True)
        S.add("sp", lambda e: e.dma_start(out=krope[:], in_=kT[KROPE_ROW:KROPE_ROW + 64, :]), writes=["krope"], dma=True)
        S.add("dve", lambda e: e.tensor_tensor(out=bm[:, 0, :], in0=bm[:, 0, :], in1=mC[:, 0, :], op=ALU.add), reads=["bm", "mC"], writes=["bm"])
        S.add("dve", lambda e: e.tensor_tensor(out=bm[:, 4, :], in0=bm[:, 4, :], in1=mC[:, 1, :], op=ALU.add), reads=["bm", "mC"], writes=["bm"])
        for i in range(2):
            S.add("dve", lambda e, i=i: e.tensor_tensor(out=lt[:, i, :], in0=lv[:, 2 * i, :], in1=lv[:, 2 * i + 1, :], op=ALU.mult),
                  reads=["lv"], writes=[("lt", i)])
            S.add("dve", lambda e, i=i: e.reduce_sum(out=ls[:, i:i + 1], in_=lt[:, i, :], axis=AX.X), reads=[("lt", i)], writes=[("ls", i)])
            S.add("act", lambda e, i=i: e.activation(out=ls[:, 2 + i:3 + i], in_=ls[:, i:i + 1], func=AF.Exp), reads=[("ls", i)], writes=[("le", i)])
        S.add("dve", lambda e: e.tensor_tensor(out=ls[:, 4:5], in0=ls[:, 3:4], in1=ls[:, 2:3], op=ALU.subtract), reads=[("le", 0), ("le", 1)], writes=["nl0"])
        S.add("dve", lambda e: e.tensor_scalar(out=ls[:, 5:6], in0=ls[:, 4:5], scalar1=-float(lam_init), scalar2=None, op0=ALU.add), reads=["nl0"], writes=["nlam"])
        S.add("dve", lambda e: e.tensor_scalar(out=sg[:, 1:2], in0=sg[:, 0:1], scalar1=float(1.0 - lam_init), scalar2=None, op0=ALU.mult), reads=["sg0"], writes=["sgain"])
        nlam = ls[:, 5:6]
        sgain = sg[:, 1:2]

        def load_k(kind, slot):
            if kind in ("A0", "A1"):
                r0 = 0 if kind == "A0" else 128
                for q in range(2):
                    S.add("sp", lambda e, q=q, r0=r0, slot=slot: e.dma_start(
                        out=kbuf[:, slot, q * 4096:(q + 1) * 4096], in_=kT[r0:r0 + 128, q * 4096:(q + 1) * 4096]),
                        writes=[("kbuf", slot, q)], dma=True)
            elif kind in ("B0", "B1"):
                n = 0 if kind == "B0" else 1
                for q in range(2):
                    S.add("sp", lambda e, q=q, n=n, slot=slot: e.dma_start(
                        out=kbuf[0:64, slot, q * 4096:(q + 1) * 4096], in_=kT[KB_ROW + 64 * n:KB_ROW + 64 * n + 64, q * 4096:(q + 1) * 4096]),
                        writes=[("kbuf", slot, q)], dma=True)
                S.add("sp", lambda e, slot=slot: e.dma_start(out=kbuf[64:68, slot, :], in_=kaug),
                      reads=[("kbuf", slot, 0), ("kbuf", slot, 1)], writes=[("kbuf", slot, 0), ("kbuf", slot, 1)], dma=True)
            else:
                for q in range(2):
                    S.add("sp", lambda e, q=q, slot=slot: e.dma_start(
                        out=kbuf[:, slot, q * 4096:(q + 1) * 4096], in_=kT[KC_ROW:KC_ROW + 128, q * 4096:(q + 1) * 4096]),
                        writes=[("kbuf", slot, q)], dma=True)

        qcnt = [0]

        def load_q(kind, I):
            qs = qcnt[0] % 3
            qcnt[0] += 1
            c = slice(I * 512, (I + 1) * 512)
            if kind in ("A0", "A1"):
                r0 = 0 if kind == "A0" else 192
                S.add("sp", lambda e: e.dma_start(out=qn[:, qs, :], in_=qT[r0:r0 + 128, c]), writes=[("q", qs)], dma=True)
                S.add("sp", lambda e: e.dma_start(out=qr[:, qs, :], in_=qT[r0 + 128:r0 + 192, c]), writes=[("qr", qs)], dma=True)
            elif kind in ("B0", "B1"):
                n = 0 if kind == "B0" else 1
                S.add("sp", lambda e: e.dma_start(out=qn[0:64, qs, :], in_=qT[QB_ROW + 64 * n:QB_ROW + 64 * n + 64, c]), writes=[("q", qs)], dma=True)
                S.add("sp", lambda e: e.dma_start(out=qn[64:68, qs, :], in_=qaug[:, c]), reads=[("q", qs)], writes=[("q", qs)], dma=True)
            else:
                S.add("sp", lambda e: e.dma_start(out=qn[:, qs, :], in_=qT[QC_ROW:QC_ROW + 128, c]), writes=[("q", qs)], dma=True)
            return qs

        blocks = []
        for I in range(NB):
            blocks.append(("A0", I, 0))
        for I in range(NB):
            blocks.append(("A1", I, 1))
        for I in range(NB):
            blocks.append(("B0", I, 0))
            blocks.append(("B1", I, 1))
        for I in range(NB):
            blocks.append(("C", I, 0))
        vcol = {"A0": 0, "A1": 128, "B0": 256, "B1": 256, "C": 384}
        orow = {"A0": 0, "A1": 128, "B1": 256, "C": 384}

        tiles = []
        for bi, (kind, I, ks) in enumerate(blocks):
            tl = []
            if kind == "C":
                for qi in range(4):
                    qt = 4 * I + qi
                    dl = [d for d in (4, 3, 2, 1, 0) if qt - d >= 0]
                    for d in dl:
                        tl.append(dict(kt=qt - d, c0=128 * qi, n=128, mask=None, bias=d, fc=(d == dl[0]), lc=(d == 0)))
            else:
                for kt in range(4 * I + 4):
                    t = kt - 4 * I
                    if t < 0:
                        tl.append(dict(kt=kt, c0=0, n=512, mask=None, bias=None, fc=(kt == 0), lc=False))
                    else:
                        tl.append(dict(kt=kt, c0=128 * t, n=512 - 128 * t, mask=True, bias=None, fc=(kt == 0), lc=False))
                tl[-1]["lc"] = True
            for ti, t in enumerate(tl):
                t.update(kind=kind, I=I, ks=ks, bi=bi, ob=bi % 2, ti=ti, first_blk=(ti == 0), last_blk=(ti == len(tl) - 1))
                tiles.append(t)
        N = len(tiles)
        for g, t in enumerate(tiles):
            t["ss"] = g % 4
            t["ps"] = g % 6
            t["tsr"] = g % 2

        blk_q = {}

        def start_block(bi):
            if bi >= len(blocks) or bi in blk_q:
                return
            kind, I, ks = blocks[bi]
            blk_q[bi] = load_q(kind, I)

        def emit_qk(g):
            t = tiles[g]
            kind, ks, ss, c0, n, kt = t["kind"], t["ks"], t["ss"], t["c0"], t["n"], t["kt"]
            qs = blk_q[t["bi"]]
            kq = t["kt"] // 32
            has_mask = t["mask"] is not None
            if kind in ("A0", "A1"):
                S.add("pe", lambda e: e.matmul(ps_s[:, ss, c0:c0 + n], lhsT=kbuf[:, ks, kt * 128:(kt + 1) * 128], rhs=qn[:, qs, c0:c0 + n],
                                               start=True, stop=False),
                      reads=[("kbuf", ks, kq), ("q", qs)], writes=[("ps_s", ss)], rhs=[("q", qs)])
                S.add("pe", lambda e: e.matmul(ps_s[:, ss, c0:c0 + n], lhsT=krope[:, kt * 128:(kt + 1) * 128], rhs=qr[:, qs, c0:c0 + n],
                                               start=False, stop=not has_mask),
                      reads=["krope", ("qr", qs)], writes=[("ps_s", ss)], rhs=[("qr", qs)])
                if has_mask:
                    S.add("pe", lambda e: e.matmul(ps_s[:, ss, c0:c0 + 128], lhsT=ident[:], rhs=mA[:], start=False, stop=True),
                          reads=["ident", "mA"], writes=[("ps_s", ss)])
            elif kind in ("B0", "B1"):
                S.add("pe", lambda e: e.matmul(ps_s[:, ss, c0:c0 + n], lhsT=kbuf[0:68, ks, kt * 128:(kt + 1) * 128], rhs=qn[0:68, qs, c0:c0 + n],
                                               start=True, stop=not has_mask),
                      reads=[("kbuf", ks, kq), ("q", qs)], writes=[("ps_s", ss)], rhs=[("q", qs)])
                if has_mask:
                    S.add("pe", lambda e: e.matmul(ps_s[:, ss, c0:c0 + 128], lhsT=ident[:], rhs=cB[:], start=False, stop=True),
                          reads=["ident", "cB"], writes=[("ps_s", ss)])
            else:
                S.add("pe", lambda e: e.matmul(ps_s[:, ss, c0:c0 + n], lhsT=kbuf[:, ks, kt * 128:(kt + 1) * 128], rhs=qn[:, qs, c0:c0 + n],
                                               start=True, stop=True),
                      reads=[("kbuf", ks, kq), ("q", qs)], writes=[("ps_s", ss)], rhs=[("q", qs)])

        def emit_exp(g):
            t = tiles[g]
            kind, ss, ps, c0, n = t["kind"], t["ss"], t["ps"], t["c0"], t["n"]
            if kind == "C":
                r = t["tsr"]
                d = t["bias"]
                S.add("dve", lambda e: e.scalar_tensor_tensor(out=tS[:, r, :], in0=ps_s[:, ss, c0:c0 + n], scalar=scC, in1=bm[:, d, :],
                                                              op0=ALU.mult, op1=ALU.add),
                      reads=[("ps_s", ss), "bm"], writes=[("tS", r)])
                S.add("act", lambda e: e.activation(out=pT[:, ps, c0:c0 + n], in_=tS[:, r, :], func=AF.Exp),
                      reads=[("tS", r)], writes=[("pT", ps)])
            else:
                sc = scA if kind in ("A0", "A1") else scB
                S.add("act", lambda e: e.activation(out=pT[:, ps, c0:c0 + n], in_=ps_s[:, ss, c0:c0 + n], func=AF.Exp, scale=sc),
                      reads=[("ps_s", ss)], writes=[("pT", ps)])

        def emit_pv(g):
            t = tiles[g]
            ps, c0, n, kt, ob = t["ps"], t["c0"], t["n"], t["kt"], t["ob"]
            vc = vcol[t["kind"]]
            S.add("pe", lambda e: e.matmul(ps_o[:, ob, c0:c0 + n], lhsT=vsb[:, kt, vc:vc + 128], rhs=pT[:, ps, c0:c0 + n],
                                           start=t["fc"], stop=t["lc"]),
                  reads=["vsb", ("pT", ps)], writes=[("ps_o", ob)], rhs=[("pT", ps)])

        def emit_acc(g):
            t = tiles[g]
            ps, c0, n, ob = t["ps"], t["c0"], t["n"], t["ob"]
            S.add("pe", lambda e: e.matmul(ps_d[:, ob, c0:c0 + n], lhsT=ones_b[:], rhs=pT[:, ps, c0:c0 + n],
                                           start=t["fc"], stop=t["lc"]),
                  reads=["onesb", ("pT", ps)], writes=[("ps_d", ob)], rhs=[("pT", ps)])

        def emit_epilogue(bi):
            kind, I, ks = blocks[bi]
            ob = bi % 2
            c = slice(I * 512, (I + 1) * 512)
            S.add("act", lambda e: e.activation(out=dsb[:, ob, :], in_=ps_d[:, ob, :], func=AF.Ln), reads=[("ps_d", ob)], writes=[("dsb", ob)])
            S.add("act", lambda e: e.activation(out=rec[:, ob, :], in_=dsb[:, ob, :], func=AF.Exp, scale=-1.0), reads=[("dsb", ob)], writes=[("rec", ob)])
            if kind in ("A0", "A1", "C"):
                S.add("dve", lambda e: e.tensor_tensor(out=ost[:, ob, :], in0=ps_o[:, ob, :], in1=rec[:, ob, :], op=ALU.mult),
                      reads=[("ps_o", ob), ("rec", ob)], writes=[("ost", ob)])
                r0 = orow[kind]
                S.add("pool", lambda e: e.dma_start(out=mixT_out[r0 // 64][:, c], in_=ost[0:64, ob, :]), reads=[("ost", ob)], writes=[("mixc", r0 // 64)], dma=True)
                S.add("pool", lambda e: e.dma_start(out=mixT_out[r0 // 64 + 1][:, c], in_=ost[64:128, ob, :]), reads=[("ost", ob)], writes=[("mixc", r0 // 64 + 1)], dma=True)
                if I == NB - 1:
                    for ch in (r0 // 64, r0 // 64 + 1):
                        add_ag(S, mixT_out[ch], mix_gath[ch], [("mixc", ch)], [("mixg", id(mix_gath[ch]))])
            elif kind == "B0":
                S.add("dve", lambda e: e.tensor_tensor(out=tmp0[:], in0=ps_o[:, ob, :], in1=rec[:, ob, :], op=ALU.mult),
                      reads=[("ps_o", ob), ("rec", ob)], writes=["tmp0"])
            else:
                S.add("dve", lambda e: e.tensor_tensor(out=t1[:], in0=ps_o[:, ob, :], in1=rec[:, ob, :], op=ALU.mult),
                      reads=[("ps_o", ob), ("rec", ob)], writes=["t1"])
                S.add("dve", lambda e: e.scalar_tensor_tensor(out=ob_[:], in0=t1[:], scalar=nlam, in1=tmp0[:], op0=ALU.mult, op1=ALU.add),
                      reads=["t1", "tmp0", "nlam"], writes=["o"])
                S.add("pool", lambda e: e.tensor_tensor(out=sq[:], in0=ob_[:], in1=ob_[:], op=ALU.mult), reads=["o"], writes=["sq"])
                S.add("pe", lambda e: e.matmul(ps_d[:, ob, :], lhsT=ones_f[:], rhs=sq[:], start=True, stop=True),
                      reads=["ones", "sq"], writes=[("ps_d", ob)])
                S.add("dve", lambda e: e.tensor_scalar(out=sq[:], in0=ps_d[:, ob, :], scalar1=1.0 / 128, scalar2=EPS, op0=ALU.mult, op1=ALU.add),
                      reads=[("ps_d", ob)], writes=["sq"])
                S.add("act", lambda e: e.activation(out=sq[:], in_=sq[:], func=AF.Ln), reads=["sq"], writes=["sq"])
                S.add("act", lambda e: e.activation(out=sq[:], in_=sq[:], func=AF.Exp, scale=-0.5), reads=["sq"], writes=["sq"])
                S.add("dve", lambda e: e.scalar_tensor_tensor(out=ost[:, ob, :], in0=ob_[:], scalar=sgain, in1=sq[:], op0=ALU.mult, op1=ALU.mult),
                      reads=["o", "sq", "sgain"], writes=[("ost", ob)])
                S.add("pool", lambda e: e.dma_start(out=mixT_out[4][:, c], in_=ost[0:64, ob, :]), reads=[("ost", ob)], writes=[("mixc", 4)], dma=True)
                S.add("pool", lambda e: e.dma_start(out=mixT_out[5][:, c], in_=ost[64:128, ob, :]), reads=[("ost", ob)], writes=[("mixc", 5)], dma=True)
                if I == NB - 1:
                    for ch in (4, 5):
                        add_ag(S, mixT_out[ch], mix_gath[ch], [("mixc", ch)], [("mixg", id(mix_gath[ch]))])

        unit_first = {}
        for bi, (kind, I, ks) in enumerate(blocks):
            unit_first.setdefault(kind, bi)
        load_k("A0", 0)
        load_k("A1", 1)
        start_block(0)
        start_block(1)

        def maybe_unit_loads(bi):
            kind, I, ks = blocks[bi]
            if unit_first[kind] != bi:
                return
            if kind == "A1":
                load_k("B0", 0)
            elif kind == "B0":
                load_k("B1", 1)

        c_loaded = [False]
        pend_ep = []
        LA = 3
        for g0 in range(LA):
            start_block(tiles[g0]["bi"])
            emit_qk(g0)
        for g in range(N):
            t = tiles[g]
            if t["first_blk"]:
                start_block(t["bi"] + 1)
                start_block(t["bi"] + 2)
                maybe_unit_loads(t["bi"])
            emit_exp(g)
            if g + LA < N:
                t2 = tiles[g + LA]
                if t2["kind"] == "C" and not c_loaded[0]:
                    load_k("C", 0)
                    c_loaded[0] = True
                start_block(t2["bi"])
                emit_qk(g + LA)
            emit_pv(g)
            emit_acc(g)
            for pe_ in list(pend_ep):
                if g >= pe_[0]:
                    emit_epilogue(pe_[1])
                    pend_ep.remove(pe_)
            if t["last_blk"]:
                pend_ep.append((g + 3, t["bi"]))
        for pe_ in pend_ep:
            emit_epilogue(pe_[1])
        S.run_block()


def colT(vec, nchunk):
    return np.ascontiguousarray(np.asarray(vec, np.float32).reshape(nchunk, 128).T)

def w_in_core(w_in_l, j):
    s = lambda a, n: w_in_l[:, a:a + n]
    qb0, kb0, vb0, qc0, kc0, vc0 = 576, 1088, 1600, 2112, 2624, 3136
    cols = [s(0, 512), s(512, 64), s(544, 32), s(512, 32),
            s(qb0 + 128 * j, 128), s(kb0 + 128 * j, 128), s(qc0 + 128 * j, 128), s(kc0 + 128 * j, 128),
            s(vb0 + 128 * j, 128), s(vc0 + 128 * j, 128)]
    return np.ascontiguousarray(np.concatenate(cols, axis=1))

def w_uq_core(w_uq_l, j):
    cols = []
    for hh in range(2):
        b = (2 * j + hh) * 192
        cols += [w_uq_l[:, b:b + 128], w_uq_l[:, b + 128:b + 192], w_uq_l[:, b + 160:b + 192], w_uq_l[:, b + 128:b + 160]]
    return np.ascontiguousarray(np.concatenate(cols, axis=1))

def w_ukv_core(w_ukv_l, j):
    b0, b1 = (2 * j) * 256, (2 * j + 1) * 256
    return np.ascontiguousarray(np.concatenate([w_ukv_l[:, b0:b0 + 128], w_ukv_l[:, b1:b1 + 128],
                                                w_ukv_l[:, b0 + 128:b0 + 256], w_ukv_l[:, b1 + 128:b1 + 256]], axis=1))

def rope_tables(seq=8192):
    half = 32
    inv = (np.float32(10000.0) ** (-np.arange(half, dtype=np.float32) / np.float32(half))).astype(np.float32)
    ang = (np.arange(seq, dtype=np.float32)[None, :] * inv[:, None]).astype(np.float32)
    c = np.cos(ang.astype(np.float64)).astype(np.float32)
    s = np.sin(ang.astype(np.float64)).astype(np.float32)
    return np.ascontiguousarray(np.concatenate([c, c], 0)), np.ascontiguousarray(np.concatenate([-s, s], 0))

NEGM = -30000.0

def mask_A():
    m = np.zeros((128, 128), np.float32)
    m[64:, :64] = NEGM
    return m.astype(ml_dtypes.bfloat16)

def corr_B(j):
    c = (2.0 ** (-2.0 * (j + 1))) * 8.0
    k = np.arange(128)[:, None]; q = np.arange(128)[None, :]
    m = np.where((k // 64 == q // 64) & (k > q), -2.0 * c * (k - q), 0.0).astype(np.float32)
    m[64:, :64] = NEGM
    return m.astype(ml_dtypes.bfloat16)

def mask_C():
    m = np.zeros((2, 128, 128), np.float32)
    m[0, 64:, :64] = NEGM
    m[1, :64, 64:] = NEGM
    return m

def bias_C(rel_bias_lh):
    k = np.arange(128)[:, None]; q = np.arange(128)[None, :]
    out = np.empty((5, 128, 128), np.float32)
    for d in range(5):
        idx = np.clip(128 * d + q - k, -63, 256) + 63
        out[d] = rel_bias_lh[idx]
    return out

def k_aug(seq=8192):
    p = np.arange(seq)
    return np.stack([p // 128, p % 128, np.ones(seq), np.ones(seq)]).astype(np.float32).astype(ml_dtypes.bfloat16)

def q_aug(j, seq=8192):
    c = (2.0 ** (-2.0 * (j + 1))) * 8.0
    p = np.arange(seq)
    return np.stack([np.full(seq, 128 * c), np.full(seq, c), -128 * c * (p // 128), -c * (p % 128)]).astype(np.float32).astype(ml_dtypes.bfloat16)


NCORES = 8
DEPTH = 2
FUSED = True
_PROG_CACHE = {}


def _lam_init(l):
    import math
    return 0.8 - 0.6 * math.exp(-0.3 * l)


def _di(nc, n, s, dt=F32):
    return nc.dram_tensor(n, list(s), dt, kind="ExternalInput").ap()


def _do(nc, n, s, dt=BF16):
    return nc.dram_tensor(n, list(s), dt, kind="ExternalOutput").ap()


def _dint(nc, n, s, dt=BF16):
    return nc.dram_tensor(n, list(s), dt, kind="Internal").ap()


def _decl_att_inputs(nc, pfx=""):
    d = {}
    d["w_in_c"] = _di(nc, pfx + "w_in_c", [D, WIN_C])
    d["qag"] = _di(nc, pfx + "qag", [128, 3])
    d["kvag"] = _di(nc, pfx + "kvag", [128, 1])
    d["w_uq_c"] = _di(nc, pfx + "w_uq_c", [384, 512])
    d["w_ukv_c"] = _di(nc, pfx + "w_ukv_c", [128, 512])
    d["biasC"] = _di(nc, pfx + "biasC", [5, 128, 128])
    d["lamv"] = _di(nc, pfx + "lamv", [4, 64])
    d["sublnT"] = _di(nc, pfx + "sublnT", [128, 1])
    return d


def _decl_consts(nc):
    d = {}
    d["rC"] = _di(nc, "rC", [64, SEQ])
    d["rS"] = _di(nc, "rS", [64, SEQ])
    d["kaug"] = _di(nc, "kaug", [4, SEQ], BF16)
    d["qaug"] = _di(nc, "qaug", [4, SEQ], BF16)
    d["maskA"] = _di(nc, "maskA", [128, 128], BF16)
    d["corrB"] = _di(nc, "corrB", [128, 128], BF16)
    d["maskC"] = _di(nc, "maskC", [2, 128, 128])
    return d


def _decl_mlp_inputs(nc, pfx=""):
    d = {}
    d["w_o_p"] = _di(nc, pfx + "w_o_p", [D, D])
    d["gT_mlp"] = _di(nc, pfx + "gT_mlp", [128, KC])
    d["w_up"] = _di(nc, pfx + "w_up", [D, DFF])
    d["w_down"] = _di(nc, pfx + "w_down", [DFF, D])
    return d


def build_fused():
    nc = bass.Bass("TRN2", target_bir_lowering=False, num_devices=NCORES)
    x = _di(nc, "x", [TOK, D])
    consts = _decl_consts(nc)
    gfin = _di(nc, "g_final", [D])
    y = _do(nc, "y", [TOK, D], F32)
    S = Sched(nc)
    S.alloc_sems()
    x_cur = x
    for l in range(DEPTH):
        pfx = "l%d_" % l
        gT = _di(nc, pfx + "gT_attn", [128, KC])
        a = _decl_att_inputs(nc, pfx)
        m = _decl_mlp_inputs(nc, pfx)
        hTc = [_dint(nc, pfx + "hTc%d" % c, [D, 256]) for c in range(8)]
        hTg = [_dint(nc, pfx + "hTg%d" % c, [4 * D, 256]) for c in range(8)]
        mixc = [_dint(nc, pfx + "mixc%d" % c, [64, SEQ]) for c in range(8)]
        mixg = [_dint(nc, pfx + "mixg%d" % c, [256, SEQ]) for c in range(8)]
        qT = _dint(nc, pfx + "qT_h", [QRH, SEQ])
        kT = _dint(nc, pfx + "kT_h", [KRH, SEQ])
        v = _dint(nc, pfx + "v_h", [SEQ, 512])
        x1 = _dint(nc, pfx + "x1", [TOK, D], F32)
        h2T = _dint(nc, pfx + "h2T", [D, TOK])
        final = (l == DEPTH - 1)
        x_next = y if final else _dint(nc, pfx + "xo", [TOK, D], F32)
        phase_P1(nc, S, x_cur, gT, hTc, hTg)
        phase_P2h(nc, S, hTg, a["w_in_c"], a["qag"], a["kvag"], a["w_uq_c"], a["w_ukv_c"], consts["rC"], consts["rS"], qT, kT, v)
        phase_ATT(nc, S, qT, kT, v, consts["kaug"], consts["qaug"], consts["maskA"], consts["corrB"], a["biasC"], consts["maskC"],
                  a["lamv"], a["sublnT"], _lam_init(l), mixc, mixg)
        phase_O1(nc, S, x_cur, mixg, m["w_o_p"], m["gT_mlp"], x1, h2T)
        phase_O2(nc, S, x1, h2T, m["w_up"], m["w_down"], x_next, g_final=gfin if final else None)
        x_cur = x_next
    return nc


def _prog(key, fn):
    if key not in _PROG_CACHE:
        _PROG_CACHE[key] = fn()
    return _PROG_CACHE[key]


def w_o_perm(w_o_l):
    def f(r, lr):
        if lr < 256:
            return 256 * r + lr
        if lr < 384:
            return 1024 + 128 * r + (lr - 256)
        return 1536 + 128 * r + (lr - 384)
    idx = [f(r, 64 * c + i) for c in range(8) for r in range(4) for i in range(64)]
    return np.ascontiguousarray(w_o_l[np.asarray(idx)])


def _f32(a):
    return np.ascontiguousarray(np.asarray(a, dtype=np.float32))


def _att_inputs(inp, l, j):
    return {
        "w_in_c": w_in_core(inp["w_in"][l], j),
        "qag": colT(inp["q_a_norm"][l], 3),
        "kvag": colT(inp["kv_a_norm"][l], 1),
        "w_uq_c": w_uq_core(inp["w_uq"][l], j),
        "w_ukv_c": w_ukv_core(inp["w_ukv"][l], j),
        "biasC": bias_C(inp["rel_bias"][l, j]),
        "lamv": np.ascontiguousarray(np.stack([inp["lambda_q1"][l], inp["lambda_k1"][l], inp["lambda_q2"][l], inp["lambda_k2"][l]])),
        "sublnT": np.ascontiguousarray(inp["diff_subln"][l].reshape(128, 1)),
    }


def _const_inputs(j):
    rC, rS = rope_tables()
    return {"rC": rC, "rS": rS, "kaug": k_aug(), "qaug": q_aug(j), "maskA": mask_A(), "corrB": corr_B(j), "maskC": mask_C()}


def _mlp_inputs(inp, l):
    return {"w_o_p": w_o_perm(inp["w_o"][l]), "gT_mlp": colT(inp["mlp_norm"][l], KC),
            "w_up": _f32(inp["w_up"][l]), "w_down": _f32(inp["w_down"][l])}


def kernel_fused(inp):
    cores = list(range(NCORES))
    x = inp["x"]
    nc = _prog("fused", build_fused)
    shared = {}
    for l in range(DEPTH):
        pfx = "l%d_" % l
        shared[pfx + "gT_attn"] = colT(inp["attn_norm"][l], KC)
        for k, v in _mlp_inputs(inp, l).items():
            shared[pfx + k] = v
    shared["g_final"] = _f32(inp["final_norm"])
    ins = []
    for c in cores:
        j = c % 4
        d = {"x": np.ascontiguousarray(x[c // 4, j * TOK:(j + 1) * TOK])}
        d.update(shared)
        d.update(_const_inputs(j))
        for l in range(DEPTH):
            for k, v in _att_inputs(inp, l, j).items():
                d["l%d_" % l + k] = v
        ins.append(d)
    res = run_bass_kernel_spmd(nc, ins, core_ids=cores)
    out = np.empty((2, SEQ, D), np.float32)
    for c in cores:
        out[c // 4, (c % 4) * TOK:(c % 4 + 1) * TOK] = res.results[c]["y"]
    return out


def kernel(**inputs):
    inp = {k: np.asarray(v) for k, v in inputs.items()}
    return kernel_fused(inp)
```
